# Optimizing a Trainium2 kernel written in Bass

```python
import math
import jax, jax.numpy as jnp
from jax import lax
import numpy as np

D_MODEL = 1024
BATCH = 16
SEQ = 2048
DEPTH = 1

MIX_WIDTH = 2 * D_MODEL
D_SSD = MIX_WIDTH // 2
D_S5 = MIX_WIDTH - D_SSD
SSD_HEAD_DIM = 64
SSD_HEADS = D_SSD // SSD_HEAD_DIM
SSD_GROUPS = 2
SSD_HEADS_PER_GROUP = SSD_HEADS // SSD_GROUPS
SSD_STATE = 128
SSD_CONV = 5
SSD_CHUNK = 128
D_XBC = D_SSD + 2 * SSD_GROUPS * SSD_STATE
S5_GROUP_CH = 16
S5_GROUPS = D_S5 // S5_GROUP_CH
S5_STATE = 64
D_FF = ((8 * D_MODEL + 3 * 256 - 1) // (3 * 256)) * 256
D_IN_PROJ = D_SSD + D_XBC + 2 * SSD_HEADS + D_S5
RMS_EPS = 1e-6
GATED_NORM_EPS = 1e-5
S5_MAX_REAL = -1e-4

kernel_name = "hymba_ssd_s5_sandwich_block"


def _rms_norm(x, w, eps=RMS_EPS):
    xf = x.astype(jnp.float32)
    y = xf * lax.rsqrt(jnp.mean(xf * xf, axis=-1, keepdims=True) + eps)
    return (y * w.astype(jnp.float32)).astype(x.dtype)


def _depthwise_conv_centred(x, w, b):
    ch = x.shape[-1]
    pad = SSD_CONV // 2
    y = lax.conv_general_dilated(
        x, w[:, None, :].astype(x.dtype), window_strides=(1,),
        padding=[(pad, pad)], dimension_numbers=("NWC", "WIO", "NWC"),
        feature_group_count=ch)
    return y + b.astype(x.dtype)


def _ssd_scan(xh, dt, a_head, b_ssm, c_ssm):
    bsz, seq = xh.shape[0], xh.shape[1]
    nc = seq // SSD_CHUNK
    g, r, l = SSD_GROUPS, SSD_HEADS_PER_GROUP, SSD_CHUNK
    xdt = (xh * dt[..., None]).reshape(bsz, nc, l, g, r, SSD_HEAD_DIM)
    a = (dt.astype(jnp.float32) * a_head.astype(jnp.float32))
    a = a.reshape(bsz, nc, l, g, r).transpose(0, 1, 3, 4, 2)
    bc = b_ssm.reshape(bsz, nc, l, g, SSD_STATE)
    cc = c_ssm.reshape(bsz, nc, l, g, SSD_STATE)
    a_cum = jnp.cumsum(a, axis=-1)
    lower = jnp.tril(jnp.ones((l, l), dtype=bool))
    decay = jnp.exp(jnp.where(lower, a_cum[..., :, None] - a_cum[..., None, :], -jnp.inf))
    scores = jnp.einsum("bclgn,bcsgn->bcgls", cc, bc)
    y_diag = jnp.einsum("bcgrls,bcsgrp->bclgrp", scores[:, :, :, None] * decay, xdt)
    decay_to_end = jnp.exp(a_cum[..., -1:] - a_cum)
    states = jnp.einsum("bclgn,bcgrl,bclgrp->bcgrpn", bc, decay_to_end, xdt)
    a_last = a_cum[..., -1]
    t_incl = jnp.cumsum(a_last, axis=1)
    t_excl = t_incl - a_last
    strict = jnp.tril(jnp.ones((nc, nc), dtype=bool), k=-1)[None, :, :, None, None]
    m = jnp.exp(jnp.where(strict, t_excl[:, :, None] - t_incl[:, None, :], -jnp.inf))
    states_in = jnp.einsum("bzcgr,bcgrpn->bzgrpn", m, states)
    y_off = jnp.einsum("bclgn,bcgrpn,bcgrl->bclgrp", cc, states_in, jnp.exp(a_cum))
    return (y_diag + y_off).reshape(bsz, seq, SSD_HEADS, SSD_HEAD_DIM)


def _ssd_mixer(z, xbc, dt_raw, conv_w, conv_b, dt_bias, a_log, d_skip, norm_w):
    bsz, seq = z.shape[0], z.shape[1]
    xbc = jax.nn.silu(_depthwise_conv_centred(xbc, conv_w, conv_b))
    xs, b_ssm, c_ssm = jnp.split(xbc, [D_SSD, D_SSD + SSD_GROUPS * SSD_STATE], axis=-1)
    xh = xs.reshape(bsz, seq, SSD_HEADS, SSD_HEAD_DIM)
    b_ssm = b_ssm.reshape(bsz, seq, SSD_GROUPS, SSD_STATE)
    c_ssm = c_ssm.reshape(bsz, seq, SSD_GROUPS, SSD_STATE)
    dt = jax.nn.softplus(dt_raw.reshape(bsz, seq, 2, SSD_HEADS).astype(jnp.float32) + dt_bias)
    a = -jnp.exp(a_log.astype(jnp.float32))
    y_f = _ssd_scan(xh, dt[:, :, 0], a[0], b_ssm, c_ssm)
    y_b = jnp.flip(_ssd_scan(jnp.flip(xh, 1), jnp.flip(dt[:, :, 1], 1), a[1],
                             jnp.flip(b_ssm, 1), jnp.flip(c_ssm, 1)), 1)
    y = y_f + y_b + xh * d_skip[:, None]
    y = y.reshape(bsz, seq, D_SSD) * jax.nn.silu(z)
    y = _rms_norm(y.reshape(bsz, seq, SSD_GROUPS, D_SSD // SSD_GROUPS),
                  norm_w.reshape(SSD_GROUPS, D_SSD // SSD_GROUPS), GATED_NORM_EPS)
    return y.reshape(bsz, seq, D_SSD)


def _complex_linear_combine(e1, e2):
    ar1, ai1, br1, bi1 = e1
    ar2, ai2, br2, bi2 = e2
    ar = ar2 * ar1 - ai2 * ai1
    ai = ar2 * ai1 + ai2 * ar1
    br = ar2 * br1 - ai2 * bi1 + br2
    bi = ar2 * bi1 + ai2 * br1 + bi2
    return (ar, ai, br, bi)


def _s5_scan(ug, lam_re, lam_im, log_dt, b_re, b_im, c_re, c_im):
    seq = ug.shape[1]
    lr = jnp.minimum(lam_re.astype(jnp.float32), S5_MAX_REAL)
    li = lam_im.astype(jnp.float32)
    dt = jnp.exp(log_dt.astype(jnp.float32))[:, None]
    mag = jnp.exp(lr * dt)
    ab_re = mag * jnp.cos(li * dt)
    ab_im = mag * jnp.sin(li * dt)
    den = lr * lr + li * li
    nr = ab_re - 1.0
    k_re = (nr * lr + ab_im * li) / den
    k_im = (ab_im * lr - nr * li) / den
    bb_re = k_re[..., None] * b_re - k_im[..., None] * b_im
    bb_im = k_re[..., None] * b_im + k_im[..., None] * b_re
    bu_re = jnp.einsum("blgh,gph->blgp", ug, bb_re)
    bu_im = jnp.einsum("blgh,gph->blgp", ug, bb_im)
    shape_a = (1, seq) + ab_re.shape
    a_re = jnp.broadcast_to(ab_re[None, None], shape_a)
    a_im = jnp.broadcast_to(ab_im[None, None], shape_a)
    _, _, h_re, h_im = lax.associative_scan(_complex_linear_combine,
                                            (a_re, a_im, bu_re, bu_im), axis=1)
    return jnp.einsum("blgp,ghp->blgh", h_re, c_re) - jnp.einsum("blgp,ghp->blgh", h_im, c_im)


def _s5_mixer(u, lam_re, lam_im, log_dt, b_re, b_im, c_re, c_im, d_skip, glu_w, glu_b):
    bsz, seq = u.shape[0], u.shape[1]
    ug = u.reshape(bsz, seq, S5_GROUPS, S5_GROUP_CH)
    y_f = _s5_scan(ug, lam_re[0], lam_im[0], log_dt[0], b_re, b_im, c_re, c_im)
    y_b = jnp.flip(_s5_scan(jnp.flip(ug, 1), lam_re[1], lam_im[1], log_dt[1],
                            b_re, b_im, c_re, c_im), 1)
    y = (y_f + y_b).reshape(bsz, seq, D_S5) + d_skip * u
    g = jax.nn.gelu(y)
    return g * jax.nn.sigmoid(g @ glu_w + glu_b)


def setup_inputs(seed: int = 0) -> dict:
    key = jax.random.key(seed)
    ks = iter(jax.random.split(key, 40))
    f32 = jnp.float32

    def nrm(shape, scale):
        return jax.random.normal(next(ks), shape, f32) * scale

    def gain(n):
        return 1.0 + nrm((DEPTH, n), 0.01)

    x = nrm((BATCH, SEQ, D_MODEL), 1.0)
    norm_mix_pre = gain(D_MODEL)
    w_in = nrm((DEPTH, D_MODEL, D_IN_PROJ), D_MODEL ** -0.5)
    conv_w = nrm((DEPTH, SSD_CONV, D_XBC), SSD_CONV ** -0.5)
    conv_b = nrm((DEPTH, D_XBC), 0.02)
    dt0 = jnp.exp(jax.random.uniform(next(ks), (DEPTH, 2, SSD_HEADS), f32,
                                     math.log(1e-3), math.log(1e-1)))
    ssd_dt_bias = dt0 + jnp.log(-jnp.expm1(-dt0))
    ssd_a_log = jnp.log(jax.random.uniform(next(ks), (DEPTH, 2, SSD_HEADS), f32, 1.0, 16.0))
    ssd_d = 1.0 + nrm((DEPTH, SSD_HEADS), 0.1)
    ssd_norm_w = gain(D_SSD)
    n = jnp.arange(S5_STATE, dtype=f32)
    s5_lambda_re = -0.5 + nrm((DEPTH, 2, S5_GROUPS, S5_STATE), 0.01)
    s5_lambda_im = math.pi * n + nrm((DEPTH, 2, S5_GROUPS, S5_STATE), 0.01)
    s5_log_dt = jax.random.uniform(next(ks), (DEPTH, 2, S5_GROUPS), f32,
                                   math.log(1e-3), math.log(1e-1))
    s5_b_re = nrm((DEPTH, S5_GROUPS, S5_STATE, S5_GROUP_CH), (2 * S5_GROUP_CH) ** -0.5)
    s5_b_im = nrm((DEPTH, S5_GROUPS, S5_STATE, S5_GROUP_CH), (2 * S5_GROUP_CH) ** -0.5)
    s5_c_re = nrm((DEPTH, S5_GROUPS, S5_GROUP_CH, S5_STATE), (2 * S5_STATE) ** -0.5)
    s5_c_im = nrm((DEPTH, S5_GROUPS, S5_GROUP_CH, S5_STATE), (2 * S5_STATE) ** -0.5)
    s5_d = nrm((DEPTH, D_S5), 1.0)
    s5_glu_w = nrm((DEPTH, D_S5, D_S5), D_S5 ** -0.5)
    s5_glu_b = nrm((DEPTH, D_S5), 0.02)
    w_out = nrm((DEPTH, MIX_WIDTH, D_MODEL), MIX_WIDTH ** -0.5)
    norm_mix_post = gain(D_MODEL)
    norm_ffn_pre = gain(D_MODEL)
    w_gate = nrm((DEPTH, D_MODEL, D_FF), D_MODEL ** -0.5)
    w_up = nrm((DEPTH, D_MODEL, D_FF), D_MODEL ** -0.5)
    w_down = nrm((DEPTH, D_FF, D_MODEL), D_FF ** -0.5)
    norm_ffn_post = gain(D_MODEL)
    return {"x": x, "norm_mix_pre": norm_mix_pre, "w_in": w_in, "conv_w": conv_w,
            "conv_b": conv_b, "ssd_dt_bias": ssd_dt_bias, "ssd_a_log": ssd_a_log,
            "ssd_d": ssd_d, "ssd_norm_w": ssd_norm_w, "s5_lambda_re": s5_lambda_re,
            "s5_lambda_im": s5_lambda_im, "s5_log_dt": s5_log_dt, "s5_b_re": s5_b_re,
            "s5_b_im": s5_b_im, "s5_c_re": s5_c_re, "s5_c_im": s5_c_im, "s5_d": s5_d,
            "s5_glu_w": s5_glu_w, "s5_glu_b": s5_glu_b, "w_out": w_out,
            "norm_mix_post": norm_mix_post, "norm_ffn_pre": norm_ffn_pre,
            "w_gate": w_gate, "w_up": w_up, "w_down": w_down,
            "norm_ffn_post": norm_ffn_post}


def reference(x, norm_mix_pre, w_in, conv_w, conv_b, ssd_dt_bias, ssd_a_log, ssd_d,
              ssd_norm_w, s5_lambda_re, s5_lambda_im, s5_log_dt, s5_b_re, s5_b_im,
              s5_c_re, s5_c_im, s5_d, s5_glu_w, s5_glu_b, w_out, norm_mix_post,
              norm_ffn_pre, w_gate, w_up, w_down, norm_ffn_post):
    x_dtype = x.dtype
    split_at = [D_SSD, D_SSD + D_XBC, D_SSD + D_XBC + 2 * SSD_HEADS]
    for layer in range(DEPTH):
        h = _rms_norm(x, norm_mix_pre[layer])
        proj = h @ w_in[layer]
        z, xbc, dt_raw, u = jnp.split(proj, split_at, axis=-1)
        y_ssd = _ssd_mixer(z, xbc, dt_raw, conv_w[layer], conv_b[layer], ssd_dt_bias[layer],
                           ssd_a_log[layer], ssd_d[layer], ssd_norm_w[layer])
        y_s5 = _s5_mixer(u, s5_lambda_re[layer], s5_lambda_im[layer], s5_log_dt[layer],
                         s5_b_re[layer], s5_b_im[layer], s5_c_re[layer], s5_c_im[layer],
                         s5_d[layer], s5_glu_w[layer], s5_glu_b[layer])
        mix = jnp.concatenate([y_ssd.astype(x_dtype), y_s5.astype(x_dtype)], axis=-1) @ w_out[layer]
        x = (x + _rms_norm(mix, norm_mix_post[layer])).astype(x_dtype)
        h = _rms_norm(x, norm_ffn_pre[layer])
        f = (jax.nn.silu(h @ w_gate[layer]) * (h @ w_up[layer])) @ w_down[layer]
        x = (x + _rms_norm(f, norm_ffn_post[layer])).astype(x_dtype)
    return x
```

```python
import math
from contextlib import ExitStack

import numpy as np
import concourse.bass as bass
import concourse.mybir as mybir
from concourse.bass_utils import run_bass_kernel_spmd

F32 = mybir.dt.float32
BF16 = mybir.dt.bfloat16
AF = mybir.ActivationFunctionType
ALU = mybir.AluOpType

NCORES = 8
S = 2048
D = 1024
NSEQ = 2
NCH = S // 128
DIN = 3616
DFF = 2816
NFF = DFF // 128
C_Z, C_XBC, C_DT, C_U = 0, 1024, 2560, 2592
LC = 8
NJ = S // LC

ENGS = ("pe", "act", "dve", "pool", "sp")


class Sched:
    EPOCH = 3000
    NDSEM = 8

    def __init__(self):
        self.ops = []
        self.last_w = {}
        self.readers = {}
        self.cnt = {e: 0 for e in ENGS}
        self.dcnt = {e: 0 for e in ENGS}
        self.last_c = {e: None for e in ENGS}
        self.last_d = {}
        self.pers = {}

    def op(self, eng, fn, r=(), w=(), dma=False, persist=False):
        idx = len(self.ops)
        deps = set()
        for k in r:
            if k in self.last_w:
                deps.add(self.last_w[k])
            if k in self.pers:
                deps.add(self.pers[k])
        for k in w:
            if k in self.last_w:
                deps.add(self.last_w[k])
            for j in self.readers.get(k, ()):
                deps.add(j)
        o = dict(eng=eng, fn=fn, deps=sorted(deps), dma=dma)
        if dma:
            o["di"] = self.dcnt[eng]
            self.dcnt[eng] += 1
            if not persist:
                self.last_d[(eng, o["di"] % self.NDSEM)] = idx
        else:
            o["ci"] = self.cnt[eng]
            self.cnt[eng] += 1
            self.last_c[eng] = idx
        self.ops.append(o)
        if persist:
            for k in w:
                self.pers[k] = idx
            return idx
        for k in w:
            self.last_w[k] = idx
            self.readers[k] = []
        for k in r:
            self.readers.setdefault(k, []).append(idx)
        return idx

    def barrier(self):
        deps = [v for v in self.last_c.values() if v is not None] + list(self.last_d.values())
        for e in ENGS:
            self.ops.append(dict(eng=e, fn=None, deps=sorted(deps), dma=False))
        self.last_w = {}
        self.readers = {}

    def emit(self, nc):
        with ExitStack() as st:
            csem = {e: [st.enter_context(nc.semaphore(f"c_{e}_{i}"))
                        for i in range(self.cnt[e] // self.EPOCH + 1)] for e in ENGS}
            dsem = {e: [st.enter_context(nc.semaphore(f"d_{e}_{i}")) for i in range(self.NDSEM)]
                    for e in ENGS if self.dcnt[e] > 0}
            block = st.enter_context(nc.Block())
            ops = self.ops

            def run(eng_name, eng):
                waited = {}

                def wait(key, sem, val):
                    if waited.get(key, 0) >= val:
                        return
                    waited[key] = val
                    eng.wait_ge(sem, val)

                for o in ops:
                    if o["eng"] != eng_name:
                        continue
                    for d in o["deps"]:
                        p = ops[d]
                        if p["fn"] is None:
                            continue
                        if p["dma"]:
                            slot = p["di"] % self.NDSEM
                            wait(("d", p["eng"], slot), dsem[p["eng"]][slot], 16 * (p["di"] // self.NDSEM + 1))
                        else:
                            if p["eng"] == eng_name and eng_name in ("pe",):
                                continue
                            ep = p["ci"] // self.EPOCH
                            wait(("c", p["eng"], ep), csem[p["eng"]][ep], p["ci"] % self.EPOCH + 1)
                    if o["fn"] is None:
                        continue
                    if o["dma"]:
                        slot = o["di"] % self.NDSEM
                        rnd = o["di"] // self.NDSEM
                        if rnd > 0:
                            wait(("d", eng_name, slot), dsem[eng_name][slot], 16 * rnd)
                        o["fn"](eng).then_inc(dsem[eng_name][slot], 16)
                    else:
                        ep = o["ci"] // self.EPOCH
                        o["fn"](eng).then_inc(csem[eng_name][ep], 1)

            block.tensor(lambda e: run("pe", e))
            block.scalar(lambda e: run("act", e))
            block.vector(lambda e: run("dve", e))
            block.gpsimd(lambda e: run("pool", e))
            block.sync(lambda e: run("sp", e))


class Arena:
    def __init__(self, handle, ncols):
        self.h = handle
        self.ncols = ncols
        self.top = 0
        self.peak = 0

    def alloc(self, shape, dtype):
        esz = 2 if dtype == BF16 else 4
        n = 1
        for s_ in shape[1:]:
            n *= s_
        nb = (n * esz + 3) // 4
        off = self.top
        self.top += nb
        self.peak = max(self.peak, self.top)
        assert self.top <= self.ncols, f"SBUF arena overflow {self.top} > {self.ncols}"
        v = self.h[:, off:off + nb]
        if dtype == BF16:
            v = v.bitcast(BF16)
        v = v[:, 0:n]
        if len(shape) == 3:
            v = v.rearrange("p (a b) -> p a b", b=shape[2])
        elif len(shape) == 4:
            v = v.rearrange("p (a b c) -> p a b c", b=shape[2], c=shape[3])
        elif len(shape) == 5:
            v = v.rearrange("p (a b c d) -> p a b c d", b=shape[2], c=shape[3], d=shape[4])
        if shape[0] < 128:
            v = v[0:shape[0]]
        return v

    def mark(self):
        return self.top

    def release(self, m):
        self.top = m


def bc(ap, shape):
    return ap.broadcast_to(list(shape))


def build(debug=None):
    nc = bass.Bass("TRN2", target_bir_lowering=False)
    I = {}

    def din(name, shape):
        I[name] = nc.dram_tensor(name, list(shape), F32, kind="ExternalInput").ap()
        return I[name]

    x = din("x", [NSEQ, S, D])
    din("norm_mix_pre", [D]); din("w_in", [D, DIN]); din("conv_w", [5, 1536]); din("conv_b", [1536])
    din("ssd_dt_bias", [32]); din("ssd_a_log", [32]); din("ssd_d", [16]); din("ssd_norm_w", [D])
    din("s5_lambda_re", [2, 64, 64]); din("s5_lambda_im", [2, 64, 64]); din("s5_log_dt", [2, 64])
    din("s5_b_re", [64, 64, 16]); din("s5_b_im", [64, 64, 16]); din("s5_c_re", [64, 16, 64]); din("s5_c_im", [64, 16, 64])
    din("s5_d", [D]); din("s5_glu_w", [D, D]); din("s5_glu_b", [D]); din("w_out", [2 * D, D])
    din("norm_mix_post", [D]); din("norm_ffn_pre", [D]); din("w_gate", [D, DFF]); din("w_up", [D, DFF])
    din("w_down", [DFF, D]); din("norm_ffn_post", [D])
    out = nc.dram_tensor("out", [NSEQ, S, D], F32, kind="ExternalOutput").ap()

    def scratch(name, shape, dt=BF16):
        return nc.dram_tensor(name, list(shape), dt, kind="Internal").ap()

    ws_in = scratch("ws_in", [D, DIN]); ws_glu = scratch("ws_glu", [D, D]); ws_out = scratch("ws_out", [2 * D, D])
    ws_gate = scratch("ws_gate", [NFF // 2, 128, 8, 256]); ws_up = scratch("ws_up", [NFF // 2, 128, 8, 256]); ws_down = scratch("ws_down", [DFF, D])
    wb_dram = scratch("wb_dram", [128, 128, 128])
    wc_dram = scratch("wc_dram", [128, 256, 128])
    t_dram = scratch("t_dram", [128, 64, 128])
    ap_dram = scratch("ap_dram", [128, 2, 64, 64], F32)

    dbg = None
    if debug is not None:
        dbg = nc.dram_tensor("dbg", list(debug[1]), F32, kind="ExternalOutput").ap()

    SCH = Sched()
    op = SCH.op

    with ExitStack() as es:
        ARENA_COLS = 49000
        arena_h = es.enter_context(nc.sbuf_tensor("arena", [128, ARENA_COLS], F32))
        A = Arena(arena_h, ARENA_COLS)
        banks = [es.enter_context(nc.psum_tensor(f"bank{i}", [128, 512], F32)) for i in range(8)]
        PS = [b[:, :] for b in banks]
        PSB = [b[:, :].bitcast(BF16) for b in banks]
        pk = [f"ps{i}" for i in range(8)]

        ident_f = A.alloc([128, 128], F32)
        ident_b = A.alloc([128, 128], BF16)
        ones_f = A.alloc([128, 128], F32)
        U_le = A.alloc([128, 128], F32)
        U_ge = A.alloc([128, 128], F32)
        S_gt = A.alloc([128, 128], F32)
        S_lt = A.alloc([128, 128], F32)
        cst = A.alloc([128, 8], F32)
        wpre = A.alloc([128, 8], F32)
        wfpre = A.alloc([128, 8], F32)
        glub = A.alloc([128, 8], F32)
        convw = A.alloc([128, 5, 12], F32)
        convb = A.alloc([128, 12], F32)
        dtb = A.alloc([128, 2], F32)
        w_mpost = A.alloc([128, D], F32)
        w_fpost = A.alloc([128, D], F32)
        w_ssdn = A.alloc([128, D], F32)
        dskip = A.alloc([128, 16], F32)
        s5A = A.alloc([128, 2, 2, 32], F32)
        s5BN = A.alloc([128, 2, 2, 32], F32)

        def aff(out_ap, in_ap, pattern, cmp, base, cm, r, w):
            op("pool", lambda e: e.affine_select(out=out_ap, in_=in_ap, pattern=pattern, compare_op=cmp,
                                                 fill=0.0, base=base, channel_multiplier=cm), r=r, w=w)

        op("pool", lambda e: e.memset(ones_f, 1.0), w=["ones_f"])
        aff(ident_f, ones_f, [[1, 128]], ALU.is_equal, 0, -1, ["ones_f"], ["ident_f"])
        aff(U_le, ones_f, [[1, 128]], ALU.is_ge, 0, -1, ["ones_f"], ["U_le"])
        aff(U_ge, ones_f, [[-1, 128]], ALU.is_ge, 0, 1, ["ones_f"], ["U_ge"])
        aff(S_gt, ones_f, [[-1, 128]], ALU.is_gt, 0, 1, ["ones_f"], ["S_gt"])
        aff(S_lt, ones_f, [[1, 128]], ALU.is_gt, 0, -1, ["ones_f"], ["S_lt"])
        op("dve", lambda e: e.tensor_copy(out=ident_b, in_=ident_f), r=["ident_f"], w=["ident_b"])
        for i_, v_ in enumerate([1e-6, 1e-5, 1.0, math.pi / 2, 0.0]):
            op("pool", lambda e, i_=i_, v_=v_: e.memset(cst[:, i_:i_ + 1], v_), w=["cst"])
        EPS6, EPS5, ONE, HPI = cst[:, 0:1], cst[:, 1:2], cst[:, 2:3], cst[:, 3:4]

        def ld(dst, src, w, eng="sp", slow=False):
            op(eng, lambda e: e.dma_start(out=dst, in_=src, allow_slow_non_contiguous=slow), w=w, dma=True)

        ld(wpre, I["norm_mix_pre"].rearrange("(c p) -> p c", p=128), ["wpre"], eng="act", slow=True)
        ld(wfpre, I["norm_ffn_pre"].rearrange("(c p) -> p c", p=128), ["wfpre"], eng="act", slow=True)
        ld(glub, I["s5_glu_b"].rearrange("(c p) -> p c", p=128), ["glub"], eng="act", slow=True)
        ld(convb, I["conv_b"].rearrange("(c p) -> p c", p=128), ["convb"], eng="act", slow=True)
        for j_ in range(5):
            ld(convw[:, j_, :], I["conv_w"][j_].rearrange("(c p) -> p c", p=128), ["convw"], eng="act", slow=True)
        ld(dtb[0:32, 0:1], I["ssd_dt_bias"].rearrange("(p o) -> p o", o=1), ["dtb0"])
        ld(dtb[0:32, 1:2], I["ssd_a_log"].rearrange("(p o) -> p o", o=1), ["dtb1"])
        ld(w_mpost, I["norm_mix_post"].partition_broadcast(128), ["w_mpost"])
        ld(w_fpost, I["norm_ffn_post"].partition_broadcast(128), ["w_fpost"])
        ld(w_ssdn, I["ssd_norm_w"].partition_broadcast(128), ["w_ssdn"])
        ld(dskip, I["ssd_d"].partition_broadcast(128), ["dskip"])
        op("act", lambda e: e.activation(out=dtb[0:32, 1:2], in_=dtb[0:32, 1:2], func=AF.Exp), r=["dtb1"], w=["dtb1"])
        op("dve", lambda e: e.tensor_scalar(out=dtb[0:32, 1:2], in0=dtb[0:32, 1:2], scalar1=-1.0, scalar2=None,
                                            op0=ALU.mult), r=["dtb1"], w=["dtb1"])

        WK = {}

        def cast_w(dst, src, rows, key):
            step = 256
            WK[key] = []
            for r0 in range(0, rows, step):
                r1 = min(rows, r0 + step)
                k = f"{key}_{r0}"
                WK[key].append(k)
                op("pool", lambda e, r0=r0, r1=r1: e.dma_start(out=dst[r0:r1, :], in_=src[r0:r1, :]),
                   w=[k], dma=True, persist=True)


        BASE = A.top
        REG_SZ = ARENA_COLS - BASE

        class Reg:
            def __init__(self, off, size):
                self.off, self.size, self.top = off, size, 0

            def alloc(self, shape, dtype):
                save = (A.top,)
                A.top = BASE + self.off + self.top
                v = A.alloc(shape, dtype)
                self.top = A.top - BASE - self.off
                assert self.top <= self.size, f"region overflow {self.top} > {self.size}"
                A.top = save[0]
                return v

            def reset(self):
                self.top = 0

        R_H = Reg(0, 8192)
        R_A = Reg(8192, 12288)
        R_Y = Reg(20480, 8192)
        R_T = Reg(28672, REG_SZ - 28672)
        assert R_T.size >= 15500, R_T.size

        def ld_w(dst, src, rkey, wkey):
            op("sp", lambda e: e.dma_start(out=dst, in_=src), r=WK[rkey], w=[wkey], dma=True)

        def dbg_dump_tok(buf_fn, nrows_tiles, width, keys):
            pass

        def phase_P1(b, hT):
            R_T.reset()
            xt = [R_T.alloc([128, 4, D], F32) for _ in range(2)]
            xn = [R_T.alloc([128, 4, D], BF16) for _ in range(2)]
            junk = R_T.alloc([128, D], F32)
            ss = R_T.alloc([128, 8], F32)
            for i in range(4):
                xt_i, xn_i = xt[i % 2], xn[i % 2]
                kx, kn, ks = f"xt{i % 2}", f"xn{i % 2}", f"ss{i % 2}"
                ssv = ss[:, (i % 2) * 4:(i % 2) * 4 + 4]
                ld(xt_i, x[b, i * 512:(i + 1) * 512, :].rearrange("(j p) f -> p j f", p=128), [kx])
                for j in range(4):
                    op("act", lambda e, j=j, xt_i=xt_i, ssv=ssv: e.activation(
                        out=junk, in_=xt_i[:, j, :], func=AF.Square, accum_out=ssv[:, j:j + 1]),
                       r=[kx], w=["junk", ks + f"_{j}"])
                op("act", lambda e, ssv=ssv: e.activation(out=ssv, in_=ssv, func=AF.Ln, bias=EPS6, scale=1.0 / D),
                   r=[ks + f"_{j}" for j in range(4)] + ["cst"], w=[ks])
                op("act", lambda e, ssv=ssv: e.activation(out=ssv, in_=ssv, func=AF.Exp, scale=-0.5), r=[ks], w=[ks])
                for j in range(4):
                    op("dve", lambda e, j=j, xt_i=xt_i, xn_i=xn_i, ssv=ssv: e.tensor_scalar(
                        out=xn_i[:, j, :], in0=xt_i[:, j, :], scalar1=ssv[:, j:j + 1], scalar2=None, op0=ALU.mult),
                       r=[kx, ks], w=[kn + f"_{j}"])
                for fc in range(8):
                    bk = fc % 2
                    def tr(e, fc=fc, bk=bk, xn_i=xn_i):
                        for j in range(4):
                            ins = e.transpose(out=PSB[bk][:, j * 128:(j + 1) * 128],
                                              in_=xn_i[:, j, fc * 128:(fc + 1) * 128], identity=ident_b)
                        return ins
                    op("pe", tr, r=[kn + f"_{j}" for j in range(4)] + ["ident_b"], w=[pk[bk]])
                    dst = hT[:, fc, i * 512:(i + 1) * 512]
                    if fc % 2 == 0:
                        op("act", lambda e, bk=bk, fc=fc, dst=dst: e.activation(
                            out=dst, in_=PSB[bk][:, 0:512], func=AF.Copy, scale=wpre[:, fc:fc + 1]),
                           r=[pk[bk], "wpre"], w=[("hT", i)])
                    else:
                        op("dve", lambda e, bk=bk, fc=fc, dst=dst: e.tensor_scalar(
                            out=dst, in0=PSB[bk][:, 0:512], scalar1=wpre[:, fc:fc + 1], scalar2=None, op0=ALU.mult),
                           r=[pk[bk], "wpre"], w=[("hT", i)])
            SCH.barrier()

        def inproj(hT, col0, width, evac, wbufs):
            wsr = ws_in.rearrange("(kc p) c -> p kc c", p=128)
            g = 0
            cnt = 0
            for m0 in range(0, width, 512):
                mw = min(512, width - m0)
                wb = wbufs[g % 2]
                wk = f"wring{g % 2}"
                g += 1
                op("sp", lambda e, wb=wb, m0=m0, mw=mw: e.dma_start(out=wb[:, :, 0:mw],
                                                                    in_=wsr[:, :, col0 + m0:col0 + m0 + mw]),
                   r=WK["ws_in"], w=[wk], dma=True)
                for mc in range(0, mw, 128):
                    mcw = min(128, mw - mc)
                    for n in range(4):
                        bk = 2 + (cnt % 4)
                        cnt += 1
                        def mm(e, wb=wb, mc=mc, mcw=mcw, n=n, bk=bk):
                            for kc in range(8):
                                ins = e.matmul(PS[bk][0:mcw, :], lhsT=wb[:, kc, mc:mc + mcw],
                                               rhs=hT[:, kc, n * 512:(n + 1) * 512], start=(kc == 0), stop=(kc == 7))
                            return ins
                        op("pe", mm, r=[wk, ("hT", n)], w=[pk[bk]])
                        evac(col0 + m0 + mc, mcw, n, bk)

        def phase_xbc(hT, xbcT):
            R_T.reset()
            dta = R_T.alloc([128, NCH, 64], F32)
            E = R_T.alloc([128, NCH, 96], F32)
            wt = R_T.alloc([128, NCH, 32], F32)
            keep = R_T.top
            dtraw = R_T.alloc([32, S], F32)
            keep2 = R_T.top
            wbufs = [R_T.alloc([128, 8, 512], BF16) for _ in range(2)]

            def evac_xbc(col, mw, n, bk):
                if col >= C_DT:
                    op("act", lambda e: e.activation(out=dtraw[0:32, n * 512:(n + 1) * 512], in_=PS[bk][0:32, :],
                                                     func=AF.Identity, bias=dtb[0:32, 0:1], scale=1.0),
                       r=[pk[bk], "dtb0"], w=["dtraw"])
                    return
                c = (col - C_XBC) // 128
                dst = xbcT[:, c, n * 512:(n + 1) * 512]
                if (c + n) % 2 == 0:
                    op("act", lambda e: e.activation(out=dst, in_=PS[bk], func=AF.Copy), r=[pk[bk]], w=[("xbcT", c)])
                else:
                    op("dve", lambda e: e.tensor_copy(out=dst, in_=PS[bk]), r=[pk[bk]], w=[("xbcT", c)])

            inproj(hT, C_XBC, 1536 + 32, evac_xbc, wbufs)
            SCH.barrier()
            R_T.top = keep2
            t1 = R_T.alloc([32, S], F32)
            dtT = t1
            aT = R_T.alloc([32, S], F32)
            dg = R_T.alloc([128, 12, 5, 128], BF16)
            for c in range(12):
                for j in range(5):
                    op("act", lambda e, c=c, j=j: e.activation(out=dg[:, c, j, :], in_=ident_f, func=AF.Copy, scale=convw[:, j, c:c + 1]),
                       r=["ident_f", "convw"], w=[("dg", c)])
            for c in range(12):
                b0 = 4 * (c % 2)
                def cmm(e, c=c, b0=b0):
                    for n in range(4):
                        for j in (2, 0, 1, 3, 4):
                            d_ = j - 2
                            lo = max(0, -(n * 512 + d_)) if n == 0 else 0
                            hi = 512 - max(0, (n * 512 + 511 + d_) - (S - 1)) if n == 3 else 512
                            ins = e.matmul(PS[b0 + n][:, lo:hi], lhsT=dg[:, c, j, :], rhs=xbcT[:, c, n * 512 + lo + d_:n * 512 + hi + d_],
                                           start=(j == 2), stop=(j == 4))
                    return ins
                op("pe", cmm, r=[("xbcT", c), ("dg", c)], w=[pk[b0 + n] for n in range(4)])
                for n in range(4):
                    op("act", lambda e, c=c, n=n, b0=b0: e.activation(out=xbcT[:, c, n * 512:(n + 1) * 512], in_=PS[b0 + n], func=AF.Silu,
                                                                      bias=convb[:, c:c + 1], scale=1.0),
                       r=[pk[b0 + n], "convb"], w=[("xbcT", c)])
            op("dve", lambda e: e.scalar_tensor_tensor(out=t1, in0=dtraw, scalar=-1.0, in1=dtraw, op0=ALU.mult, op1=ALU.max),
               r=["dtraw"], w=["t1"])
            op("act", lambda e: e.activation(out=t1, in_=t1, func=AF.Exp, scale=-1.0), r=["t1"], w=["t1"])
            op("act", lambda e: e.activation(out=t1, in_=t1, func=AF.Ln, bias=ONE[0:32], scale=1.0), r=["t1", "cst"], w=["t1"])
            op("dve", lambda e: e.scalar_tensor_tensor(out=dtT, in0=dtraw, scalar=0.0, in1=t1, op0=ALU.max, op1=ALU.add),
               r=["dtraw", "t1"], w=["dtT"])
            op("dve", lambda e: e.tensor_scalar(out=aT, in0=dtT, scalar1=dtb[0:32, 1:2], scalar2=None, op0=ALU.mult),
               r=["dtT", "dtb1"], w=["aT"])
            for half in range(2):
                def trd(e, half=half):
                    for cc in range(8):
                        tok = slice((half * 8 + cc) * 128, (half * 8 + cc + 1) * 128)
                        e.transpose(out=PS[half][:, cc * 64:cc * 64 + 32], in_=dtT[0:32, tok], identity=ident_f[0:32, 0:32])
                        ins = e.transpose(out=PS[half][:, cc * 64 + 32:cc * 64 + 64], in_=aT[0:32, tok], identity=ident_f[0:32, 0:32])
                    return ins
                op("pe", trd, r=["dtT", "aT", "ident_f"], w=[pk[half]])
                op("act", lambda e, half=half: e.activation(out=dta[:, half * 8:(half + 1) * 8, :].rearrange("p c k -> p (c k)"),
                                                            in_=PS[half], func=AF.Copy), r=[pk[half]], w=[("dtah", half)])
            for gb in range(4):
                chunks = list(range(gb * 5, min(NCH, gb * 5 + 5)))
                def cum(e, gb=gb, chunks=chunks):
                    for i_, c in enumerate(chunks):
                        o = i_ * 96
                        e.matmul(PS[2 + gb][:, o:o + 16], lhsT=U_le, rhs=dta[:, c, 32:48], start=True, stop=True)
                        e.matmul(PS[2 + gb][:, o + 16:o + 32], lhsT=U_ge, rhs=dta[:, c, 48:64], start=True, stop=True)
                        e.matmul(PS[2 + gb][:, o + 32:o + 48], lhsT=S_gt, rhs=dta[:, c, 32:48], start=True, stop=True)
                        e.matmul(PS[2 + gb][:, o + 48:o + 64], lhsT=S_lt, rhs=dta[:, c, 48:64], start=True, stop=True)
                        ins = e.matmul(PS[2 + gb][:, o + 64:o + 96], lhsT=ones_f, rhs=dta[:, c, 32:64], start=True, stop=True)
                    return ins
                op("pe", cum, r=[("dtah", 0), ("dtah", 1), "U_le", "U_ge", "S_gt", "S_lt", "ones_f"], w=[pk[2 + gb]])
                n_c = len(chunks)
                op("act", lambda e, gb=gb, n_c=n_c, chunks=chunks: e.activation(
                    out=E[:, chunks[0]:chunks[0] + n_c, :].rearrange("p c k -> p (c k)"), in_=PS[2 + gb][:, 0:n_c * 96], func=AF.Exp),
                   r=[pk[2 + gb]], w=[("Eg", gb)])
            op("dve", lambda e: e.tensor_tensor(out=wt, in0=dta[:, :, 0:32], in1=E[:, :, 32:64], op=ALU.mult),
               r=[("dtah", 0), ("dtah", 1)] + [("Eg", gb) for gb in range(4)], w=["wt_all"])
            SCH.barrier()
            R_T.top = keep
            return dta, E, wt

        def phase_ssd(xbcT, ybuf, dta, E, wt):
            Xtok = [R_T.alloc([128, 1024], BF16) for _ in range(2)]
            Btok = [R_T.alloc([128, 256], BF16) for _ in range(2)]
            Gm = [R_T.alloc([128, 2, 128], BF16) for _ in range(2)]
            Xw = [R_T.alloc([128, 16, 64], BF16) for _ in range(2)]
            Xdt = [R_T.alloc([128, 16, 64], BF16) for _ in range(2)]
            L4 = [R_T.alloc([128, 4, 128], F32) for _ in range(2)]
            D4 = [R_T.alloc([128, 4, 128], BF16) for _ in range(2)]
            M4 = [R_T.alloc([128, 4, 128], BF16) for _ in range(2)]
            ytmpD = [R_T.alloc([128, 16, 64], F32) for _ in range(2)]
            xd = R_T.alloc([128, 16, 64], F32)
            R32D = [R_T.alloc([128, 16, 64], F32) for _ in range(2)]
            RbfD = [R_T.alloc([128, 16, 64], BF16) for _ in range(2)]
            touched = set()
            nq = 0
            for ci in range(NCH):
                for d in range(2):
                    c = ci if d == 0 else NCH - 1 - ci
                    maskL = S_gt if d == 0 else S_lt
                    maskR = U_le if d == 0 else U_ge
                    kmL = "S_gt" if d == 0 else "S_lt"
                    kmR = "U_le" if d == 0 else "U_ge"
                    first, last = ci == 0, ci == NCH - 1
                    tok = slice(c * 128, (c + 1) * 128)
                    s2 = d
                    ytmp, R32, Rbf = ytmpD[d], R32D[d], RbfD[d]
                    X_, B_, G_, Xw_, Xdt_ = Xtok[s2], Btok[s2], Gm[s2], Xw[s2], Xdt[s2]
                    kX, kB, kG, kXw, kXdt = f"Xtok{s2}", f"Btok{s2}", f"Gm{s2}", f"Xw{s2}", f"Xdt{s2}"
                    def trx(e, tok=tok):
                        for fc in range(8):
                            ins = e.transpose(out=PSB[0][:, fc * 128:(fc + 1) * 128], in_=xbcT[:, fc, tok], identity=ident_b)
                        return ins
                    op("pe", trx, r=[("xbcT", fc) for fc in range(8)] + ["ident_b"], w=[pk[0]])
                    op("act", lambda e, X_=X_: e.activation(out=X_, in_=PSB[0][:, 0:1024], func=AF.Copy), r=[pk[0]], w=[kX])
                    def trb(e, tok=tok):
                        for g in range(2):
                            e.transpose(out=PSB[1][:, g * 128:(g + 1) * 128], in_=xbcT[:, 8 + g, tok], identity=ident_b)
                        for g in range(2):
                            ins = e.matmul(PS[1][:, 256 + g * 128:256 + (g + 1) * 128], lhsT=xbcT[:, 8 + g, tok],
                                           rhs=xbcT[:, 10 + g, tok], start=True, stop=True)
                        return ins
                    op("pe", trb, r=[("xbcT", 8), ("xbcT", 9), ("xbcT", 10), ("xbcT", 11), "ident_b"], w=[pk[1]])
                    op("act", lambda e, B_=B_: e.activation(out=B_, in_=PSB[1][:, 0:256], func=AF.Copy), r=[pk[1]], w=[kB])
                    op("dve", lambda e, G_=G_, maskR=maskR: e.tensor_tensor(
                        out=G_, in0=PS[1][:, 256:512].rearrange("p (g l) -> p g l", g=2),
                        in1=bc(maskR.unsqueeze(1), [128, 2, 128]), op=ALU.mult), r=[pk[1], kmR], w=[kG])
                    Xv = X_.rearrange("p (h q) -> p h q", h=16)

                    def do_xw():
                        op("pool", lambda e, Xw_=Xw_, Xv=Xv, c=c, d=d: e.tensor_tensor(
                            out=Xw_, in0=Xv, in1=bc(wt[:, c, d * 16:(d + 1) * 16].unsqueeze(2), [128, 16, 64]), op=ALU.mult),
                           r=[kX, ("wt", c)], w=[kXw])

                    def do_xdt():
                        op("dve", lambda e, Xdt_=Xdt_, Xv=Xv, c=c, d=d: e.tensor_tensor(
                            out=Xdt_, in0=Xv, in1=bc(dta[:, c, d * 16:(d + 1) * 16].unsqueeze(2), [128, 16, 64]), op=ALU.mult),
                           r=[kX, ("dta", c)], w=[kXdt])
                    bufs = []
                    for q in range(4):
                        sq = nq % 2
                        nq += 1
                        bufs.append((L4[sq], D4[sq], M4[sq], f"L4{sq}", f"D4{sq}", f"M4{sq}", 2 + sq))

                    def do_l4(q):
                        L_, D_, M_, kL, kD, kM, bs = bufs[q]
                        a0 = 32 + d * 16 + 4 * q
                        op("pool", lambda e, L_=L_, c=c, a0=a0, maskL=maskL: e.tensor_tensor(
                            out=L_, in0=bc(maskL.unsqueeze(1), [128, 4, 128]),
                            in1=bc(dta[:, c, a0:a0 + 4].unsqueeze(2), [128, 4, 128]), op=ALU.mult),
                           r=[kmL, ("dta", c)], w=[kL])

                    def do_seg(q):
                        L_, D_, M_, kL, kD, kM, bs = bufs[q]
                        g = q // 2
                        def seg(e, L_=L_, bs=bs, maskR=maskR):
                            for i in range(4):
                                ins = e.matmul(PS[bs][:, i * 128:(i + 1) * 128], lhsT=L_[:, i, :], rhs=maskR, start=True, stop=True)
                            return ins
                        op("pe", seg, r=[kL, kmR], w=[pk[bs]])
                        op("act", lambda e, D_=D_, bs=bs: e.activation(
                            out=D_, in_=PS[bs].rearrange("p (i l) -> p i l", i=4), func=AF.Exp), r=[pk[bs]], w=[kD])
                        op("dve", lambda e, D_=D_, M_=M_, G_=G_, g=g: e.tensor_tensor(
                            out=M_, in0=D_, in1=bc(G_[:, g:g + 1, :], [128, 4, 128]), op=ALU.mult), r=[kD, kG], w=[kM])

                    def do_ydiag(q):
                        L_, D_, M_, kL, kD, kM, bs = bufs[q]
                        g = q // 2
                        def ydiag(e, M_=M_, Xdt_=Xdt_, q=q, g=g):
                            for i in range(4):
                                h = 4 * q + i
                                ins = e.matmul(PS[4 + g][:, (h % 8) * 64:(h % 8) * 64 + 64], lhsT=M_[:, i, :],
                                               rhs=Xdt_[:, h, :], start=True, stop=True)
                            return ins
                        op("pe", ydiag, r=[kM, kXdt], w=[pk[4 + g]])
                    do_l4(0)
                    do_xdt()
                    do_l4(1)
                    do_seg(0)
                    do_xw()
                    do_seg(1)
                    do_l4(2)
                    do_ydiag(0)
                    do_seg(2)
                    do_l4(3)
                    do_ydiag(1)
                    do_seg(3)
                    do_ydiag(2)
                    do_ydiag(3)
                    yv = ytmp
                    for g in range(2):
                        hs = slice(8 * g, 8 * g + 8)
                        if not first:
                            op("pe", lambda e, g=g, tok=tok, Rbf=Rbf: e.matmul(
                                PS[6 + g], lhsT=xbcT[:, 10 + g, tok], rhs=Rbf[:, 8 * g:8 * g + 8, :].rearrange("p h q -> p (h q)"),
                                start=True, stop=True), r=[("xbcT", 10 + g), ("Rbf", d, g)], w=[pk[6 + g]])
                            op("dve", lambda e, g=g, hs=hs, c=c, d=d, yv=yv: e.tensor_tensor(
                                out=yv[:, hs, :], in0=PS[6 + g].rearrange("p (h q) -> p h q", h=8),
                                in1=bc(E[:, c, d * 16 + 8 * g:d * 16 + 8 * g + 8].unsqueeze(2), [128, 8, 64]), op=ALU.mult),
                               r=[pk[6 + g], ("E", c)], w=[("ytmp", d, g)])
                            op("dve", lambda e, g=g, hs=hs, yv=yv: e.tensor_tensor(
                                out=yv[:, hs, :], in0=PS[4 + g].rearrange("p (h q) -> p h q", h=8), in1=yv[:, hs, :], op=ALU.add),
                               r=[pk[4 + g], ("ytmp", d, g)], w=[("ytmp", d, g)])
                        else:
                            op("act", lambda e, g=g, hs=hs, yv=yv: e.activation(
                                out=yv[:, hs, :], in_=PS[4 + g].rearrange("p (h q) -> p h q", h=8), func=AF.Copy),
                               r=[pk[4 + g]], w=[("ytmp", d, g)])
                    yb = ybuf[:, c, :].rearrange("p (h q) -> p h q", h=16)
                    ky = [("ytmp", d, 0), ("ytmp", d, 1)]
                    if d == 0:
                        op("dve", lambda e, Xv=Xv: e.tensor_tensor(out=xd, in0=Xv, in1=bc(dskip.unsqueeze(2), [128, 16, 64]),
                                                                  op=ALU.mult), r=[kX, "dskip"], w=["xd"])
                        if c not in touched:
                            op("dve", lambda e, yb=yb, yv=yv: e.tensor_tensor(out=yb, in0=yv, in1=xd, op=ALU.add),
                               r=ky + ["xd"], w=[("ybuf", c)])
                        else:
                            op("dve", lambda e, yv=yv: e.tensor_tensor(out=yv, in0=yv, in1=xd, op=ALU.add), r=ky + ["xd"], w=ky)
                            op("dve", lambda e, yb=yb, yv=yv: e.tensor_tensor(out=yb, in0=yv, in1=yb, op=ALU.add),
                               r=ky + [("ybuf", c)], w=[("ybuf", c)])
                    else:
                        if c not in touched:
                            op("act", lambda e, yb=yb, yv=yv: e.activation(out=yb, in_=yv, func=AF.Copy), r=ky, w=[("ybuf", c)])
                        else:
                            op("dve", lambda e, yb=yb, yv=yv: e.tensor_tensor(out=yb, in0=yv, in1=yb, op=ALU.add),
                               r=ky + [("ybuf", c)], w=[("ybuf", c)])
                    touched.add(c)
                    if not last:
                        for g in range(2):
                            hs = slice(8 * g, 8 * g + 8)
                            op("pe", lambda e, g=g, B_=B_, Xw_=Xw_: e.matmul(
                                PS[6 + g], lhsT=B_[:, g * 128:(g + 1) * 128],
                                rhs=Xw_[:, 8 * g:8 * g + 8, :].rearrange("p h q -> p (h q)"), start=True, stop=True),
                               r=[kB, kXw], w=[pk[6 + g]])
                            if first:
                                op("act", lambda e, g=g, hs=hs, R32=R32: e.activation(
                                    out=R32[:, hs, :], in_=PS[6 + g].rearrange("p (h q) -> p h q", h=8), func=AF.Copy),
                                   r=[pk[6 + g]], w=[("R32", d, g)])
                            else:
                                op("pool", lambda e, g=g, hs=hs, c=c, d=d, R32=R32: e.tensor_tensor(
                                    out=R32[:, hs, :], in0=R32[:, hs, :],
                                    in1=bc(E[:, c, 64 + d * 16 + 8 * g:64 + d * 16 + 8 * g + 8].unsqueeze(2), [128, 8, 64]),
                                    op=ALU.mult), r=[("R32", d, g), ("E", c)], w=[("R32", d, g)])
                                op("dve", lambda e, g=g, hs=hs, R32=R32: e.tensor_tensor(
                                    out=R32[:, hs, :], in0=PS[6 + g].rearrange("p (h q) -> p h q", h=8), in1=R32[:, hs, :],
                                    op=ALU.add), r=[pk[6 + g], ("R32", d, g)], w=[("R32", d, g)])
                            op("act", lambda e, g=g, hs=hs, R32=R32, Rbf=Rbf: e.activation(out=Rbf[:, hs, :], in_=R32[:, hs, :], func=AF.Copy),
                               r=[("R32", d, g)], w=[("Rbf", d, g)])
            SCH.barrier()

        def phase_gate(hT, ybuf, yssdT, wz):
            R_T.reset()
            zs = R_T.alloc([128, 1024], F32)
            yg = R_T.alloc([128, 1024], F32)
            jk = R_T.alloc([128, 512], F32)
            yn = [R_T.alloc([128, 1024], BF16) for _ in range(2)]
            gs = R_T.alloc([128, 4], F32)
            wzr = ws_in.rearrange("(kc p) c -> p kc c", p=128)
            for half in range(2):
                ld_w(wz[:, :, half * 512:(half + 1) * 512], wzr[:, :, C_Z + half * 512:C_Z + (half + 1) * 512], "ws_in", "wz")
            for c in range(NCH):
                tok = slice(c * 128, (c + 1) * 128)
                for half in range(2):
                    def zmm(e, half=half, tok=tok):
                        for kc in range(8):
                            ins = e.matmul(PS[2 + half], lhsT=hT[:, kc, tok], rhs=wz[:, kc, half * 512:(half + 1) * 512],
                                           start=(kc == 0), stop=(kc == 7))
                        return ins
                    op("pe", zmm, r=["hT_all", "wz"], w=[pk[2 + half]])
                    op("act", lambda e, half=half: e.activation(out=zs[:, half * 512:(half + 1) * 512], in_=PS[2 + half],
                                                                func=AF.Silu), r=[pk[2 + half]], w=[("zs", half)])
                    op("dve", lambda e, half=half, c=c: e.tensor_tensor(
                        out=yg[:, half * 512:(half + 1) * 512], in0=ybuf[:, c, half * 512:(half + 1) * 512],
                        in1=zs[:, half * 512:(half + 1) * 512], op=ALU.mult), r=[("zs", half), ("ybuf", c)], w=[("yg", half)])
                    op("act", lambda e, half=half: e.activation(out=jk, in_=yg[:, half * 512:(half + 1) * 512], func=AF.Square,
                                                                accum_out=gs[:, half:half + 1]), r=[("yg", half)], w=["jk", ("gs", half)])
                op("act", lambda e: e.activation(out=gs[:, 0:2], in_=gs[:, 0:2], func=AF.Ln, bias=EPS5, scale=1.0 / 512),
                   r=[("gs", 0), ("gs", 1), "cst"], w=["gsr"])
                op("act", lambda e: e.activation(out=gs[:, 0:2], in_=gs[:, 0:2], func=AF.Exp, scale=-0.5), r=["gsr"], w=["gsr"])
                yn_ = yn[c % 2]
                kyn = f"yn{c % 2}"
                for half in range(2):
                    op("dve", lambda e, half=half, yn_=yn_: e.scalar_tensor_tensor(
                        out=yn_[:, half * 512:(half + 1) * 512], in0=yg[:, half * 512:(half + 1) * 512],
                        scalar=gs[:, half:half + 1], in1=w_ssdn[:, half * 512:(half + 1) * 512], op0=ALU.mult, op1=ALU.mult),
                       r=[("yg", half), "gsr", "w_ssdn"], w=[kyn + f"_{half}"])
                def try_(e, yn_=yn_):
                    for fc in range(8):
                        ins = e.transpose(out=PSB[0][:, fc * 128:(fc + 1) * 128], in_=yn_[:, fc * 128:(fc + 1) * 128], identity=ident_b)
                    return ins
                op("pe", try_, r=[kyn + "_0", kyn + "_1", "ident_b"], w=[pk[0]])
                eng = "act" if c % 2 == 0 else "dve"
                if eng == "act":
                    op("act", lambda e, tok=tok: e.activation(out=yssdT[:, :, tok], in_=PSB[0][:, 0:1024].rearrange("p (f t) -> p f t", f=8),
                                                              func=AF.Copy), r=[pk[0]], w=["yssdT"])
                else:
                    op("dve", lambda e, tok=tok: e.tensor_copy(out=yssdT[:, :, tok], in_=PSB[0][:, 0:1024].rearrange("p (f t) -> p f t", f=8)),
                       r=[pk[0]], w=["yssdT"])
            SCH.barrier()
        def s5_prep():
            R = Reg(0, REG_SZ)
            n_ = [0]

            def T_(shape, dt=F32):
                n_[0] += 1
                return R.alloc(shape, dt), f"s5t{n_[0]}"

            def tt(o, a, b_, opx, eng="dve"):
                op(eng, lambda e: e.tensor_tensor(out=o[0], in0=a[0], in1=b_[0], op=opx), r=[a[1], b_[1]], w=[o[1]])

            def ts(o, a, s1, op0, s2=None, op1=None, eng="dve"):
                if op1 is None:
                    op(eng, lambda e: e.tensor_scalar(out=o[0], in0=a[0], scalar1=s1, scalar2=None, op0=op0), r=[a[1]], w=[o[1]])
                else:
                    op("dve", lambda e: e.tensor_scalar(out=o[0], in0=a[0], scalar1=s1, scalar2=s2, op0=op0, op1=op1), r=[a[1]], w=[o[1]])

            def act(o, a, func, scale=1.0, bias=None):
                if bias is None:
                    op("act", lambda e: e.activation(out=o[0], in_=a[0], func=func, scale=scale), r=[a[1]], w=[o[1]])
                else:
                    op("act", lambda e: e.activation(out=o[0], in_=a[0], func=func, scale=scale, bias=bias), r=[a[1], "cst"], w=[o[1]])

            def V(t, ap):
                return (ap, t[1])

            Bre = T_([128, 32, 16]); Bim = T_([128, 32, 16])
            ld(Bre[0], I["s5_b_re"].rearrange("(q t) p h -> (t p) q h", t=2), [Bre[1]])
            ld(Bim[0], I["s5_b_im"].rearrange("(q t) p h -> (t p) q h", t=2), [Bim[1]])
            cre = T_([128, 32, 16]); cim = T_([128, 32, 16])
            ScR = T_([128, 4, 128]); ScI = T_([128, 4, 128])
            for src, Sc in ((I["s5_c_re"], ScR), (I["s5_c_im"], ScI)):
                for q in range(32):
                    ld(Sc[0][16 * (q % 8):16 * (q % 8) + 16, q // 8, :].rearrange("h (t p) -> h t p", t=2),
                       src[2 * q:2 * q + 2].rearrange("t h p -> h t p"), [Sc[1]], eng="pool")
            stg = T_([128, 128]); lre = T_([128, 64]); lim = T_([128, 64]); ldt_all = T_([128, 128]); ldt = T_([128, 64])
            for src, dst in ((I["s5_lambda_re"], lre), (I["s5_lambda_im"], lim)):
                ld(stg[0][0:64, :], src.rearrange("d (q t) p -> (d q) (t p)", t=2), [stg[1]])
                op("pe", lambda e: e.transpose(out=PS[0][:, 0:64], in_=stg[0][0:64, :], identity=ident_f[0:64, 0:64]),
                   r=[stg[1], "ident_f"], w=[pk[0]])
                op("act", lambda e, dst=dst: e.activation(out=dst[0], in_=PS[0][:, 0:64], func=AF.Copy), r=[pk[0]], w=[dst[1]])
            ld(ldt_all[0], I["s5_log_dt"].rearrange("d g -> (d g)").partition_broadcast(128), [ldt_all[1]])
            for g2 in range(2):
                ps_ = slice(64 * g2, 64 * g2 + 64)
                op("dve", lambda e, ps_=ps_, g2=g2: e.tensor_copy(
                    out=ldt[0][ps_, :].rearrange("p (d q) -> p d q", d=2),
                    in_=ldt_all[0][ps_, :].rearrange("p (d q t) -> p d q t", d=2, t=2)[:, :, :, g2]), r=[ldt_all[1]], w=[ldt[1]])
            dtv = T_([128, 64]); act(dtv, ldt, AF.Exp)
            lr = T_([128, 64]); ts(lr, lre, -1e-4, ALU.min)
            xr = T_([128, 64]); tt(xr, lr, dtv, ALU.mult)
            th = T_([128, 64]); tt(th, lim, dtv, ALU.mult)
            mag = T_([128, 64]); act(mag, xr, AF.Exp)
            sn = T_([128, 64]); cs = T_([128, 64])
            act(sn, th, AF.Sin, scale=1.0 / 16)
            act(cs, th, AF.Sin, scale=1.0 / 16, bias=HPI)
            cc = T_([128, 64]); s2_ = T_([128, 64]); sc = T_([128, 64])
            for _ in range(4):
                tt(cc, cs, cs, ALU.mult); tt(s2_, sn, sn, ALU.mult); tt(sc, sn, cs, ALU.mult)
                tt(cs, cc, s2_, ALU.subtract); ts(sn, sc, 2.0, ALU.mult)
            ar = T_([128, 64]); ai = T_([128, 64])
            tt(ar, mag, cs, ALU.mult); tt(ai, mag, sn, ALU.mult)
            den = T_([128, 64]); t1 = T_([128, 64]); t2 = T_([128, 64])
            tt(den, lr, lr, ALU.mult); tt(t1, lim, lim, ALU.mult); tt(den, den, t1, ALU.add)
            op("dve", lambda e: e.reciprocal(out=den[0], in_=den[0]), r=[den[1]], w=[den[1]])
            nr = T_([128, 64]); ts(nr, ar, -1.0, ALU.add)
            kre = T_([128, 64]); kim = T_([128, 64])
            tt(t1, nr, lr, ALU.mult); tt(t2, ai, lim, ALU.mult); tt(t1, t1, t2, ALU.add); tt(kre, t1, den, ALU.mult)
            tt(t1, ai, lr, ALU.mult); tt(t2, nr, lim, ALU.mult); tt(t1, t1, t2, ALU.subtract); tt(kim, t1, den, ALU.mult)
            Pre = T_([128, LC + 1, 64]); Pim = T_([128, LC + 1, 64])
            op("dve", lambda e: e.memset(Pre[0][:, 0, :], 1.0), w=[Pre[1]])
            op("dve", lambda e: e.memset(Pim[0][:, 0, :], 0.0), w=[Pim[1]])
            for k in range(1, LC + 1):
                a_, b_ = V(Pre, Pre[0][:, k - 1, :]), V(Pim, Pim[0][:, k - 1, :])
                tt(t1, a_, ar, ALU.mult); tt(t2, b_, ai, ALU.mult); tt(V(Pre, Pre[0][:, k, :]), t1, t2, ALU.subtract)
                tt(t1, a_, ai, ALU.mult); tt(t2, b_, ar, ALU.mult); tt(V(Pim, Pim[0][:, k, :]), t1, t2, ALU.add)
            p8r, p8i = V(Pre, Pre[0][:, LC, :]), V(Pim, Pim[0][:, LC, :])
            for ri, src in ((0, p8r), (1, p8i)):
                op("dve", lambda e, ri=ri, src=src: e.tensor_copy(out=s5A[:, ri, :, :].rearrange("p d q -> p (d q)"), in_=src[0]),
                   r=[src[1]], w=["s5A"])
            m2 = T_([128, 64]); ivr = T_([128, 64]); ivi = T_([128, 64])
            tt(m2, p8r, p8r, ALU.mult); tt(t1, p8i, p8i, ALU.mult); tt(m2, m2, t1, ALU.add)
            op("dve", lambda e: e.reciprocal(out=m2[0], in_=m2[0]), r=[m2[1]], w=[m2[1]])
            tt(ivr, p8r, m2, ALU.mult); tt(ivi, p8i, m2, ALU.mult); ts(ivi, ivi, -1.0, ALU.mult)
            for Sc, dst in ((ScR, cre), (ScI, cim)):
                for blk in range(4):
                    op("pe", lambda e, blk=blk, Sc=Sc: e.transpose(out=PS[1][:, 0:128], in_=Sc[0][:, blk, :], identity=ident_f),
                       r=[Sc[1], "ident_f"], w=[pk[1]])
                    op("act", lambda e, blk=blk, dst=dst: e.activation(
                        out=dst[0][:, blk * 8:(blk + 1) * 8, :], in_=PS[1][:, 0:128].rearrange("p (q h) -> p q h", q=8), func=AF.Copy),
                       r=[pk[1]], w=[dst[1]])
            dG = T_([64, 16]); dcol = T_([128, 64])
            ld(dG[0], I["s5_d"].rearrange("(g h) -> g h", h=16), [dG[1]])
            dGb = T_([64, LC, 16])
            op("dve", lambda e: e.tensor_copy(out=dGb[0], in_=bc(dG[0].unsqueeze(1), [64, LC, 16])), r=[dG[1]], w=[dGb[1]])
            op("pe", lambda e: e.matmul(PS[2][:, 0:64], lhsT=dGb[0].rearrange("p s h -> p (s h)"), rhs=ident_f[0:64, 0:64],
                                        start=True, stop=True), r=[dGb[1], "ident_f"], w=[pk[2]])
            op("act", lambda e: e.activation(out=dcol[0], in_=PS[2][:, 0:64], func=AF.Copy), r=[pk[2]], w=[dcol[1]])
            mF = T_([128, LC, 16]); mB = T_([128, LC, 16])
            op("pool", lambda e: e.affine_select(out=mF[0], in_=bc(ones_f[:, 0:1].unsqueeze(2), [128, LC, 16]), pattern=[[16, LC], [0, 16]],
                                                 compare_op=ALU.is_ge, fill=0.0, base=15, channel_multiplier=-1), r=["ones_f"], w=[mF[1]])
            op("pool", lambda e: e.affine_select(out=mB[0], in_=bc(ones_f[:, 0:1].unsqueeze(2), [128, LC, 16]), pattern=[[-16, LC], [0, 16]],
                                                 compare_op=ALU.is_ge, fill=0.0, base=0, channel_multiplier=1), r=["ones_f"], w=[mB[1]])
            cast_w(ws_in, I["w_in"], D, "ws_in")
            bbr = T_([128, 2, 32, 16]); bbi = T_([128, 2, 32, 16]); u1 = T_([128, 2, 32, 16]); u2 = T_([128, 2, 32, 16])

            def kb(t):
                return V(t, bc(t[0].rearrange("p (d q) -> p d q", d=2).unsqueeze(3), [128, 2, 32, 16]))

            def bb_(t):
                return V(t, bc(t[0].unsqueeze(1), [128, 2, 32, 16]))
            tt(u1, bb_(Bre), kb(kre), ALU.mult); tt(u2, bb_(Bim), kb(kim), ALU.mult); tt(bbr, u1, u2, ALU.subtract)
            tt(u1, bb_(Bim), kb(kre), ALU.mult); tt(u2, bb_(Bre), kb(kim), ALU.mult); tt(bbi, u1, u2, ALU.add)
            mark_small = R.top
            Tst = T_([128, 64, 128], BF16)
            HQ = 8
            WbT = [T_([128, HQ, LC, 16]), T_([128, HQ, LC, 16])]
            Qm = [T_([128, HQ, LC, 16]), T_([128, HQ, LC, 16])]
            Wc = [T_([128, HQ, LC, 16]), T_([128, HQ, LC, 16])]
            WcZ = [[T_([128, HQ, LC, 16]), T_([128, HQ, LC, 16])], [T_([128, HQ, LC, 16]), T_([128, HQ, LC, 16])]]
            hm = T_([128, 2])
            for g2_ in range(2):
                for hh in range(2):
                    op("dve", lambda e, g2_=g2_, hh=hh: e.memset(hm[0][64 * hh:64 * hh + 64, g2_:g2_ + 1], 1.0 if g2_ == hh else 0.0), w=[hm[1]])
            v1 = T_([128, HQ, LC, 16]); v2 = T_([128, HQ, LC, 16]); v3 = T_([128, HQ, LC, 16]); v4 = T_([128, HQ, LC, 16])
            w1 = T_([128, HQ, LC, 16])
            stb = T_([128, HQ, 2, 128], BF16)
            stc = T_([128, HQ, 4, 128], BF16)
            wbv = wb_dram.rearrange("k (q x) m -> k q x m", x=4)
            wcv = wc_dram.rearrange("k (q x) m -> k q x m", x=8)

            def iv(t, d, hq):
                return V(t, bc(t[0][:, d * 32 + hq * HQ:d * 32 + (hq + 1) * HQ].unsqueeze(2).unsqueeze(3), [128, HQ, LC, 16]))
            for d in range(2):
                for hq in range(32 // HQ):
                    qs = slice(hq * HQ, (hq + 1) * HQ)

                    bR, bI = V(bbr, bbr[0][:, d, qs]), V(bbi, bbi[0][:, d, qs])
                    cR, cI = V(cre, cre[0][:, qs]), V(cim, cim[0][:, qs])
                    cols = slice(d * 32 + hq * HQ, d * 32 + (hq + 1) * HQ)

                    def pwv(P, lo, rev):
                        v = P[0][:, lo:lo + LC, cols]
                        if rev:
                            v = v[:, ::-1, :]
                        return V(P, bc(v.rearrange("p s q -> p q s").unsqueeze(3), [128, HQ, LC, 16]))

                    def b4(t):
                        return V(t, bc(t[0].unsqueeze(2), [128, HQ, LC, 16]))
                    Pe_r, Pe_i = pwv(Pre, 0, d == 0), pwv(Pim, 0, d == 0)
                    Pf_r, Pf_i = pwv(Pre, 1, d == 1), pwv(Pim, 1, d == 1)
                    tt(v1, b4(bR), Pe_r, ALU.mult); tt(v2, b4(bI), Pe_i, ALU.mult); tt(WbT[0], v1, v2, ALU.subtract)
                    tt(v1, b4(bI), Pe_r, ALU.mult); tt(v2, b4(bR), Pe_i, ALU.mult); tt(WbT[1], v1, v2, ALU.add)
                    tt(v3, b4(cR), Pf_r, ALU.mult); tt(v4, b4(cI), Pf_i, ALU.mult); tt(Wc[0], v3, v4, ALU.subtract)
                    tt(v3, b4(cI), Pf_r, ALU.mult); tt(v4, b4(cR), Pf_i, ALU.mult); tt(v3, v3, v4, ALU.add)
                    act(Wc[1], v3, AF.Copy, scale=-1.0)
                    tt(Qm[0], WbT[0], iv(ivr, d, hq), ALU.mult); tt(w1, WbT[1], iv(ivi, d, hq), ALU.mult); tt(Qm[0], Qm[0], w1, ALU.subtract)
                    tt(Qm[1], WbT[0], iv(ivi, d, hq), ALU.mult); tt(w1, WbT[1], iv(ivr, d, hq), ALU.mult); tt(Qm[1], Qm[1], w1, ALU.add)
                    for g2_ in range(2):
                        for ri in range(2):
                            op("act", lambda e, g2_=g2_, ri=ri: e.activation(out=WcZ[g2_][ri][0], in_=Wc[ri][0], func=AF.Copy,
                                                                             scale=hm[0][:, g2_:g2_ + 1]), r=[Wc[ri][1], hm[1]], w=[WcZ[g2_][ri][1]])
                    msk = mF if d == 0 else mB
                    for g4 in range(HQ // 2):
                        bk = 4 + g4 % 2
                        def tmm(e, g4=g4, bk=bk):
                            for i in range(4):
                                gl = 4 * g4 + i
                                ql, g2 = gl // 2, gl % 2
                                e.matmul(PS[bk][:, i * 128:(i + 1) * 128], lhsT=Qm[0][0][:, ql].rearrange("p s h -> p (s h)"),
                                         rhs=WcZ[g2][0][0][:, ql].rearrange("p s h -> p (s h)"), start=True, stop=False)
                                ins = e.matmul(PS[bk][:, i * 128:(i + 1) * 128], lhsT=Qm[1][0][:, ql].rearrange("p s h -> p (s h)"),
                                               rhs=WcZ[g2][1][0][:, ql].rearrange("p s h -> p (s h)"), start=False, stop=True)
                            return ins
                        op("pe", tmm, r=[Qm[0][1], Qm[1][1], WcZ[0][0][1], WcZ[0][1][1], WcZ[1][0][1], WcZ[1][1][1]], w=[pk[bk]])
                        g0 = hq * 2 * HQ + 4 * g4
                        tv = Tst[0][:, g0:g0 + 4, :]
                        mv = bc(msk[0].rearrange("p t h -> p (t h)").unsqueeze(1), [128, 4, 128])
                        pv = PS[bk].rearrange("p (i m) -> p i m", i=4)
                        if d == 0:
                            op("dve", lambda e, tv=tv, mv=mv, pv=pv: e.tensor_tensor(out=tv, in0=pv, in1=mv, op=ALU.mult),
                               r=[pk[bk], msk[1]], w=[Tst[1]])
                        else:
                            wv = w1[0].rearrange("p q s h -> p (q s h)")[:, 0:512].rearrange("p (i m) -> p i m", i=4)
                            op("dve", lambda e, wv=wv, mv=mv, pv=pv: e.tensor_tensor(out=wv, in0=pv, in1=mv, op=ALU.mult),
                               r=[pk[bk], msk[1]], w=[w1[1]])
                            iv_ = bc(ident_f.unsqueeze(1), [128, 4, 128])
                            dv_ = bc(dcol[0][:, g0:g0 + 4].unsqueeze(2), [128, 4, 128])
                            wv2 = w1[0].rearrange("p q s h -> p (q s h)")[:, 512:1024].rearrange("p (i m) -> p i m", i=4)
                            op("dve", lambda e, wv2=wv2, iv_=iv_, dv_=dv_: e.tensor_tensor(out=wv2, in0=iv_, in1=dv_, op=ALU.mult),
                               r=["ident_f", dcol[1], w1[1]], w=[w1[1]])
                            op("dve", lambda e, wv=wv, wv2=wv2: e.tensor_tensor(out=wv, in0=wv, in1=wv2, op=ALU.add),
                               r=[w1[1]], w=[w1[1]])
                            op("dve", lambda e, tv=tv, wv=wv: e.tensor_tensor(out=tv, in0=tv, in1=wv, op=ALU.add),
                               r=[w1[1], Tst[1]], w=[Tst[1]])
                    for ql in range(HQ):
                        for ri in range(2):
                            bk = 6 + (ql * 2 + ri) % 2
                            op("pe", lambda e, ql=ql, ri=ri, bk=bk: e.transpose(
                                out=PS[bk][:, 0:128], in_=WbT[ri][0][:, ql].rearrange("p s h -> p (s h)"), identity=ident_f),
                               r=[WbT[ri][1], "ident_f"], w=[pk[bk]])
                            op("act", lambda e, ql=ql, ri=ri, bk=bk: e.activation(out=stb[0][:, ql, ri, :], in_=PS[bk][:, 0:128], func=AF.Copy),
                               r=[pk[bk]], w=[stb[1]])
                    for g2_ in range(2):
                        for ri in range(2):
                            op("pool", lambda e, ri=ri, g2_=g2_: e.tensor_copy(out=stc[0][:, :, g2_ * 2 + ri, :],
                                                                               in_=WcZ[g2_][ri][0].rearrange("p q s h -> p q (s h)")),
                               r=[WcZ[g2_][ri][1]], w=[stc[1]])
                    op("sp", lambda e, d=d, qs=qs: e.dma_start(out=wbv[:, qs, 2 * d:2 * d + 2, :], in_=stb[0]), r=[stb[1]], w=["wb_dram"], dma=True)
                    op("sp", lambda e, d=d, qs=qs: e.dma_start(out=wcv[:, qs, 4 * d:4 * d + 4, :], in_=stc[0]), r=[stc[1]], w=["wc_dram"], dma=True)
            op("sp", lambda e: e.dma_start(out=t_dram, in_=Tst[0]), r=[Tst[1]], w=["t_dram"], dma=True)
            SCH.barrier()
            R.top = mark_small
            NSEG = 64
            ApT = T_([128, 2, 64, NSEG]); au1 = T_([128, 64, NSEG // 2]); au2 = T_([128, 64, NSEG // 2])
            for ri, src in ((0, p8r), (1, p8i)):
                op("dve", lambda e, ri=ri, src=src: e.tensor_copy(out=ApT[0][:, ri, :, 0], in_=src[0]), r=[src[1]], w=[ApT[1]])
            nn = 1
            while nn < NSEG:
                lo_r, lo_i = V(ApT, ApT[0][:, 0, :, 0:nn]), V(ApT, ApT[0][:, 1, :, 0:nn])
                br = V(ApT, bc(ApT[0][:, 0, :, nn - 1:nn], [128, 64, nn])); bi = V(ApT, bc(ApT[0][:, 1, :, nn - 1:nn], [128, 64, nn]))
                a1 = V(au1, au1[0][:, :, 0:nn]); a2 = V(au2, au2[0][:, :, 0:nn])
                tt(a1, lo_r, br, ALU.mult); tt(a2, lo_i, bi, ALU.mult); tt(V(ApT, ApT[0][:, 0, :, nn:2 * nn]), a1, a2, ALU.subtract)
                tt(a1, lo_r, bi, ALU.mult); tt(a2, lo_i, br, ALU.mult); tt(V(ApT, ApT[0][:, 1, :, nn:2 * nn]), a1, a2, ALU.add)
                nn *= 2
            op("sp", lambda e: e.dma_start(out=ap_dram, in_=ApT[0]), r=[ApT[1]], w=["ap_dram"], dma=True)
            op("dve", lambda e: e.tensor_scalar(out=s5BN[:, 0, :, :], in0=s5A[:, 1, :, :], scalar1=-1.0, scalar2=None, op0=ALU.mult),
               r=["s5A"], w=["s5BN"])
            op("dve", lambda e: e.tensor_copy(out=s5BN[:, 1, :, :], in_=s5A[:, 1, :, :]), r=["s5A"], w=["s5BN"])
            SCH.barrier()

        def phase_s5(hT):
            R_S5H = Reg(16384, 8192)
            R_S5U = Reg(24576, 8192)
            R_S5S = Reg(32768, 4096)
            R_S5R = Reg(36864, REG_SZ - 36864)
            U8 = R_S5U.alloc([128, 64, NJ], BF16)
            Sel = R_S5S.alloc([128, 8, 8, 128], BF16)
            def gen_sel():
                op("pool", lambda e: e.memset(Sel, 1.0), w=["Sel"])
                for pat, cmp, base, cm in (([[16, 8], [-16, 8], [1, 128]], ALU.is_equal, 0, -1),
                                           ([[-16, 8], [0, 8], [0, 128]], ALU.is_ge, 0, 1),
                                           ([[16, 8], [0, 8], [0, 128]], ALU.is_ge, 15, -1)):
                    op("pool", lambda e, pat=pat, cmp=cmp, base=base, cm=cm: e.affine_select(
                        out=Sel, in_=Sel, pattern=pat, compare_op=cmp, fill=0.0, base=base, channel_multiplier=cm), r=["Sel"], w=["Sel"])
            gen_sel()
            wbufs = [R_S5H.alloc([128, 8, 512], BF16) for _ in range(2)]
            utmp = [R_S5H.alloc([128, S], BF16) for _ in range(2)]
            state = {"n": 0}

            def evac_u(col, mw, n, bk):
                fc = (col - C_U) // 128
                ut = utmp[fc % 2]
                ku = f"utmp{fc % 2}"
                dst = ut.rearrange("p (s j) -> p s j", s=LC)[:, :, n * 64:(n + 1) * 64]
                src = PS[bk].rearrange("p (j s) -> p s j", s=LC)
                if n % 2 == 0:
                    op("act", lambda e: e.activation(out=dst, in_=src, func=AF.Copy), r=[pk[bk]], w=[(ku, n)])
                else:
                    op("dve", lambda e: e.tensor_copy(out=dst, in_=src), r=[pk[bk]], w=[(ku, n)])
                if n == 3:
                    for gl in range(8):
                        g = fc * 8 + gl
                        bq = gl % 2
                        def shf(e, gl=gl, ut=ut, bq=bq):
                            uv = ut.rearrange("p (s j) -> p s j", s=LC)
                            for s_ in range(LC):
                                ins = e.matmul(PS[bq][:, 0:NJ], lhsT=Sel[:, gl, s_, :], rhs=uv[:, s_, :],
                                               start=(s_ == 0), stop=(s_ == LC - 1))
                            return ins
                        op("pe", shf, r=[(ku, 0), (ku, 1), (ku, 2), (ku, 3), "Sel"], w=[pk[bq]])
                        if gl % 2 == 0:
                            op("act", lambda e, g=g, bq=bq: e.activation(out=U8[:, g, :], in_=PS[bq][:, 0:NJ], func=AF.Copy),
                               r=[pk[bq]], w=[("U8", g)])
                        else:
                            op("dve", lambda e, g=g, bq=bq: e.tensor_copy(out=U8[:, g, :], in_=PS[bq][:, 0:NJ]),
                               r=[pk[bq]], w=[("U8", g)])

            inproj(hT, C_U, 1024, evac_u, wbufs)
            if debug is not None and debug[0] == "u8":
                SCH.barrier()
                Rd_ = Reg(0, 16384)
                t32 = Rd_.alloc([128, 64 * NJ], F32)
                op("dve", lambda e: e.tensor_copy(out=t32, in_=U8.rearrange("p g j -> p (g j)")), w=["t32"])
                op("sp", lambda e: e.dma_start(out=dbg[:, :], in_=t32), r=["t32"], w=["dbg"], dma=True)
                return None
            SCH.barrier()
            R_S5H.reset(); R_H.reset()
            HistD = [R_H.alloc([128, 2, 32, NJ], BF16), R_S5H.alloc([128, 2, 32, NJ], BF16)]
            ring_w = [R_S5R.alloc([128, 4, 128], BF16) for _ in range(4)]
            for q in range(32):
                rw = ring_w[q % 4]
                kw = f"ringw{q % 4}"
                op("sp", lambda e, rw=rw, q=q: e.dma_start(out=rw, in_=wb_dram[:, 4 * q:4 * q + 4, :]), w=[kw], dma=True)
                for d in range(2):
                    bk = 2 + (2 * q + d) % 4
                    def inj(e, rw=rw, q=q, d=d, bk=bk):
                        for ri in range(2):
                            x_ = d * 2 + ri
                            e.matmul(PS[bk][0:64, ri * NJ:(ri + 1) * NJ], lhsT=rw[:, x_, 0:64], rhs=U8[:, 2 * q, :], start=True, stop=True)
                            ins = e.matmul(PS[bk][64:128, ri * NJ:(ri + 1) * NJ], lhsT=rw[:, x_, 64:128], rhs=U8[:, 2 * q + 1, :],
                                           start=True, stop=True, tile_position=(0, 64))
                        return ins
                    op("pe", inj, r=[kw, ("U8", 2 * q), ("U8", 2 * q + 1)], w=[pk[bk]])
                    src = PS[bk].rearrange("p (r j) -> p r j", r=2)
                    if d == 0:
                        op("act", lambda e, q=q, src=src: e.activation(out=HistD[0][:, :, q, :], in_=src, func=AF.Copy),
                           r=[pk[bk]], w=["Hist_f"])
                    else:
                        op("dve", lambda e, q=q, src=src: e.tensor_copy(out=HistD[1][:, :, q, :], in_=src), r=[pk[bk]], w=["Hist_b"])
            SCH.barrier()
            NS, SL = 4, NJ // 4
            R_SC = Reg(32768, REG_SZ - 32768)
            ZD = [R_SC.alloc([128, 2, 32, NS], F32) for _ in range(2)]
            T1 = [R_SC.alloc([128, 2, 32, NS], F32) for _ in range(2)]
            T2 = [R_SC.alloc([128, 2, 32, NS], F32) for _ in range(2)]
            FcD = [R_SC.alloc([128, 2, 32, NS], F32) for _ in range(2)]
            A64 = R_SC.alloc([128, 2, 2, 32], F32)
            c1 = R_SC.alloc([128, 2, 32], F32); c2 = R_SC.alloc([128, 2, 32], F32)
            ApR = R_SC.alloc([128, 32, SL], F32); ApI = R_SC.alloc([128, 32, SL], F32)
            tA = R_SC.alloc([128, 32, SL], F32); tB = R_SC.alloc([128, 32, SL], F32)
            for d, eng in ((0, "dve"), (1, "pool")):
                Z, t1, t2 = ZD[d], T1[d], T2[d]
                kz = f"Z{d}"
                Hseg = HistD[d].rearrange("p r q (s j) -> p r q s j", s=NS)
                Ac = bc(s5A[:, 0:1, d, :].unsqueeze(3), [128, 2, 32, NS])
                Bc = bc(s5BN[:, :, d, :].unsqueeze(3), [128, 2, 32, NS])
                kH = "Hist_f" if d == 0 else "Hist_b"
                order = list(range(SL)) if d == 0 else list(range(SL - 1, -1, -1))
                for si, k in enumerate(order):
                    hj = Hseg[:, :, :, :, k]
                    if si == 0:
                        op(eng, lambda e, Z=Z, hj=hj: e.tensor_copy(out=Z, in_=hj), r=[kH], w=[kz])
                        continue
                    Zs = Z[:, ::-1, :, :]
                    op(eng, lambda e, Z=Z, t1=t1, Ac=Ac: e.tensor_tensor(out=t1, in0=Z, in1=Ac, op=ALU.mult), r=[kz, "s5A"], w=[kz + "t1"])
                    op(eng, lambda e, Zs=Zs, t2=t2, Bc=Bc: e.tensor_tensor(out=t2, in0=Zs, in1=Bc, op=ALU.mult), r=[kz, "s5BN"], w=[kz + "t2"])
                    op(eng, lambda e, t1=t1, t2=t2: e.tensor_tensor(out=t1, in0=t1, in1=t2, op=ALU.add), r=[kz + "t1", kz + "t2"], w=[kz + "t1"])
                    op(eng, lambda e, Z=Z, t1=t1, hj=hj: e.tensor_tensor(out=Z, in0=t1, in1=hj, op=ALU.add), r=[kz + "t1", kH], w=[kz])
                    op(eng, lambda e, Z=Z, hj=hj: e.tensor_copy(out=hj, in_=Z), r=[kz], w=[kH])
            for d in range(2):
                Z, Fc = ZD[d], FcD[d]
                kz, kF = f"Z{d}", f"Fc{d}"
                kH = "Hist_f" if d == 0 else "Hist_b"
                Hseg = HistD[d].rearrange("p r q (s j) -> p r q s j", s=NS)
                for ri in range(2):
                    op("sp", lambda e, ri=ri, d=d: e.dma_start(out=(ApR if ri == 0 else ApI), in_=ap_dram[:, ri, d * 32:(d + 1) * 32, :]),
                       w=["ApR" if ri == 0 else "ApI"], dma=True)
                op("dve", lambda e: e.tensor_copy(out=A64[:, 0, :, :], in_=bc(ApR[:, :, SL - 1].unsqueeze(1), [128, 2, 32])), r=["ApR"], w=["A64"])
                op("dve", lambda e: e.tensor_scalar(out=A64[:, 1, 0, :], in0=ApI[:, :, SL - 1], scalar1=-1.0, scalar2=None, op0=ALU.mult),
                   r=["ApI", "A64"], w=["A64"])
                op("dve", lambda e: e.tensor_copy(out=A64[:, 1, 1, :], in_=ApI[:, :, SL - 1]), r=["ApI", "A64"], w=["A64"])
                segs = list(range(NS)) if d == 0 else list(range(NS - 1, -1, -1))
                for i_, sg_ in enumerate(segs):
                    if i_ == 0:
                        op("dve", lambda e, sg_=sg_, Z=Z, Fc=Fc: e.tensor_copy(out=Fc[:, :, :, sg_], in_=Z[:, :, :, sg_]), r=[kz], w=[kF])
                    elif i_ < NS - 1:
                        pv = segs[i_ - 1]
                        Fp = Fc[:, :, :, pv]
                        Fps = Fc[:, ::-1, :, pv]
                        op("dve", lambda e, Fp=Fp: e.tensor_tensor(out=c1, in0=Fp, in1=A64[:, 0, :, :], op=ALU.mult), r=[kF, "A64"], w=["c1"])
                        op("dve", lambda e, Fps=Fps: e.tensor_tensor(out=c2, in0=Fps, in1=A64[:, 1, :, :], op=ALU.mult), r=[kF, "A64"], w=["c2"])
                        op("dve", lambda e: e.tensor_tensor(out=c1, in0=c1, in1=c2, op=ALU.add), r=["c1", "c2"], w=["c1"])
                        op("dve", lambda e, sg_=sg_, Z=Z, Fc=Fc: e.tensor_tensor(out=Fc[:, :, :, sg_], in0=c1, in1=Z[:, :, :, sg_], op=ALU.add),
                           r=["c1", kz, kF], w=[kF])
                for i_, sg_ in enumerate(segs):
                    if i_ == 0:
                        continue
                    pv = segs[i_ - 1]
                    cR = bc(Fc[:, 0, :, pv].unsqueeze(2), [128, 32, SL])
                    cI = bc(Fc[:, 1, :, pv].unsqueeze(2), [128, 32, SL])
                    PR = ApR if d == 0 else ApR[:, :, ::-1]
                    PI = ApI if d == 0 else ApI[:, :, ::-1]
                    Hre, Him = Hseg[:, 0, :, sg_, :], Hseg[:, 1, :, sg_, :]
                    op("dve", lambda e, PR=PR, cR=cR: e.tensor_tensor(out=tA, in0=PR, in1=cR, op=ALU.mult), r=["ApR", kF], w=["tA"])
                    op("dve", lambda e, PI=PI, cI=cI: e.tensor_tensor(out=tB, in0=PI, in1=cI, op=ALU.mult), r=["ApI", kF], w=["tB"])
                    op("dve", lambda e: e.tensor_tensor(out=tA, in0=tA, in1=tB, op=ALU.subtract), r=["tA", "tB"], w=["tA"])
                    op("dve", lambda e, Hre=Hre: e.tensor_tensor(out=Hre, in0=Hre, in1=tA, op=ALU.add), r=["tA", kH], w=[kH])
                    op("dve", lambda e, PR=PR, cI=cI: e.tensor_tensor(out=tA, in0=PR, in1=cI, op=ALU.mult), r=["ApR", kF, "tA"], w=["tA"])
                    op("dve", lambda e, PI=PI, cR=cR: e.tensor_tensor(out=tB, in0=PI, in1=cR, op=ALU.mult), r=["ApI", kF, "tB"], w=["tB"])
                    op("dve", lambda e: e.tensor_tensor(out=tA, in0=tA, in1=tB, op=ALU.add), r=["tA", "tB"], w=["tA"])
                    op("dve", lambda e, Him=Him: e.tensor_tensor(out=Him, in0=Him, in1=tA, op=ALU.add), r=["tA", kH], w=[kH])
            SCH.barrier()
            R_S5R.reset()
            ring_c = [R_S5R.alloc([128, 8, 128], BF16) for _ in range(4)]
            ring_t = [R_S5R.alloc([128, 2, 128], BF16) for _ in range(4)]
            gen_sel()
            for q in range(32):
                rc, rt = ring_c[q % 4], ring_t[q % 4]
                kc_, kt_ = f"ringc{q % 4}", f"ringt{q % 4}"
                op("sp", lambda e, rc=rc, q=q: e.dma_start(out=rc, in_=wc_dram[:, 8 * q:8 * q + 8, :]), w=[kc_], dma=True)
                op("sp", lambda e, rt=rt, q=q: e.dma_start(out=rt, in_=t_dram[:, 2 * q:2 * q + 2, :]), w=[kt_], dma=True)
                for g2 in range(2):
                    g = 2 * q + g2
                    bk = 2 + g % 4
                    ps_ = slice(64 * g2, 64 * g2 + 64)
                    def outm(e, rc=rc, rt=rt, q=q, g2=g2, g=g, bk=bk, ps_=ps_):
                        e.matmul(PS[bk][:, 0:NJ], lhsT=rt[:, g2, :], rhs=U8[:, g, :], start=True, stop=False)
                        for d in range(2):
                            for ri in range(2):
                                last = (d == 1 and ri == 1)
                                if d == 0:
                                    ins = e.matmul(PS[bk][:, 1:NJ], lhsT=rc[:, g2 * 2 + ri, :], rhs=HistD[0][:, ri, q, 0:NJ - 1],
                                                   start=False, stop=last)
                                else:
                                    ins = e.matmul(PS[bk][:, 0:NJ - 1], lhsT=rc[:, 4 + g2 * 2 + ri, :], rhs=HistD[1][:, ri, q, 1:NJ],
                                                   start=False, stop=last)
                        return ins
                    op("pe", outm, r=[kc_, kt_, ("U8", g), "Hist"], w=[pk[bk]])
                    if g % 2 == 0:
                        op("act", lambda e, g=g, bk=bk: e.activation(out=U8[:, g, :], in_=PS[bk][:, 0:NJ], func=AF.Copy),
                           r=[pk[bk]], w=[("U8", g)])
                    else:
                        op("dve", lambda e, g=g, bk=bk: e.tensor_copy(out=U8[:, g, :], in_=PS[bk][:, 0:NJ]), r=[pk[bk]], w=[("U8", g)])
            SCH.barrier()
            R_S5H.reset(); R_S5R.reset(); R_H.reset()
            gT = R_S5H.alloc([128, 8, S], BF16)
            glw = R_S5R.alloc([128, 8, D], BF16)
            tmp = [R_S5R.alloc([128, NJ], F32) for _ in range(2)]
            sg = [R_S5R.alloc([128, 512], F32) for _ in range(2)]
            glr = ws_glu.rearrange("(kc p) c -> p kc c", p=128)
            for half in range(2):
                ld_w(glw[:, :, half * 512:(half + 1) * 512], glr[:, :, half * 512:(half + 1) * 512], "ws_glu", "glw")
            it = 0
            for fc in range(8):
                gv = gT[:, fc, :].rearrange("p (j t) -> p t j", t=LC)
                for t_ in range(LC):
                    bk = 2 + it % 4
                    tm = tmp[it % 2]
                    ktm = f"tmp{it % 2}"
                    it += 1
                    def unsh(e, fc=fc, t_=t_, bk=bk):
                        for gl in range(8):
                            ins = e.matmul(PS[bk][:, 0:NJ], lhsT=Sel[:, t_, gl, :], rhs=U8[:, fc * 8 + gl, :], start=(gl == 0), stop=(gl == 7))
                        return ins
                    op("pe", unsh, r=["Sel"] + [("U8", fc * 8 + gl) for gl in range(8)], w=[pk[bk]])
                    yv = PS[bk][:, 0:NJ]
                    op("act", lambda e, tm=tm, yv=yv: e.activation(out=tm, in_=yv, func=AF.Square), r=[pk[bk]], w=[ktm])
                    op("dve", lambda e, tm=tm: e.tensor_scalar(out=tm, in0=tm, scalar1=0.044715, scalar2=1.0, op0=ALU.mult, op1=ALU.add),
                       r=[ktm], w=[ktm])
                    op("dve", lambda e, tm=tm, yv=yv: e.tensor_tensor(out=tm, in0=yv, in1=tm, op=ALU.mult), r=[ktm, pk[bk]], w=[ktm])
                    op("act", lambda e, tm=tm: e.activation(out=tm, in_=tm, func=AF.Sigmoid, scale=1.5957691216057308), r=[ktm], w=[ktm])
                    op("dve", lambda e, tm=tm, yv=yv, gv=gv, t_=t_: e.tensor_tensor(out=gv[:, t_, :], in0=yv, in1=tm, op=ALU.mult),
                       r=[ktm, pk[bk]], w=[("gT", fc)])
            SCH.barrier()
            ys5T = R_H.alloc([128, 8, S], BF16)
            cnt = 0
            for mc in range(8):
                for n in range(4):
                    bk = 2 + cnt % 4
                    sg_ = sg[cnt % 2]
                    ksg = f"sg{cnt % 2}"
                    cnt += 1
                    tl = slice(n * 512, (n + 1) * 512)
                    def glm(e, mc=mc, tl=tl, bk=bk):
                        for kc in range(8):
                            ins = e.matmul(PS[bk], lhsT=glw[:, kc, mc * 128:(mc + 1) * 128], rhs=gT[:, kc, tl], start=(kc == 0), stop=(kc == 7))
                        return ins
                    op("pe", glm, r=["glw"] + [("gT", fc) for fc in range(8)], w=[pk[bk]])
                    op("act", lambda e, mc=mc, bk=bk, sg_=sg_: e.activation(out=sg_, in_=PS[bk], func=AF.Sigmoid, bias=glub[:, mc:mc + 1], scale=1.0),
                       r=[pk[bk], "glub"], w=[ksg])
                    op("dve", lambda e, mc=mc, tl=tl, sg_=sg_: e.tensor_tensor(out=ys5T[:, mc, tl], in0=sg_, in1=gT[:, mc, tl], op=ALU.mult),
                       r=[ksg, ("gT", mc)], w=["ys5T"])
            SCH.barrier()
            return ys5T

        def phase_D(b, yssdT, ys5T):
            Rg = Reg(16384, REG_SZ - 16384)
            xt = Rg.alloc([128, 4, D], F32)
            ms4 = [Rg.alloc([128, D], F32) for _ in range(4)]
            xn2 = Rg.alloc([128, 4, D], BF16)
            h2T = Rg.alloc([128, 8, 512], BF16)
            aT = Rg.alloc([128, NFF, 512], BF16)
            sgt = [Rg.alloc([128, 512], F32) for _ in range(2)]
            NWR = 5
            wo = [Rg.alloc([128, D], BF16) for _ in range(NWR)]
            wg = [Rg.alloc([128, 8, 256], BF16) for _ in range(2)]
            wu = [Rg.alloc([128, 8, 256], BF16) for _ in range(2)]
            wd = [Rg.alloc([128, D], BF16) for _ in range(NWR)]
            ssm = Rg.alloc([128, 16], F32)
            cnt = {"wo": 0, "wg": 0, "wd": 0, "jk": 0}

            def jkey():
                cnt["jk"] += 1
                return f"jk{cnt['jk']}"

            def rstd4(c0, tag):
                sv = ssm[:, c0:c0 + 4]
                op("act", lambda e, sv=sv: e.activation(out=sv, in_=sv, func=AF.Ln, bias=EPS6, scale=1.0 / D),
                   r=[(tag, j) for j in range(4)] + ["cst"], w=[tag + "r"])
                op("act", lambda e, sv=sv: e.activation(out=sv, in_=sv, func=AF.Exp, scale=-0.5), r=[tag + "r"], w=[tag + "r"])

            def evac_sumsq(c0, tag):
                for j in range(4):
                    for half in range(2):
                        op("act", lambda e, j=j, half=half: e.activation(out=ms4[j][:, half * 512:(half + 1) * 512], in_=PS[2 * j + half],
                                                                         func=AF.Copy), r=[pk[2 * j + half]], w=[("ms", j)])
                for j in range(4):
                    op("act", lambda e, j=j: e.activation(out=xn2[:, j, :], in_=ms4[j], func=AF.Square, accum_out=ssm[:, c0 + j:c0 + j + 1]),
                       r=[("ms", j)], w=[("xn2", j), (tag, j)])
                rstd4(c0, tag)

            for i in range(4):
                t0 = i * 512
                ld(xt, x[b, t0:t0 + 512, :].rearrange("(j p) f -> p j f", p=128), [("x1", j) for j in range(4)], eng="pool")
                for kc in range(16):
                    w_ = wo[cnt["wo"] % NWR]
                    kw = f"wo{cnt['wo'] % NWR}"
                    cnt["wo"] += 1
                    ld_w(w_, ws_out[kc * 128:(kc + 1) * 128, :], "ws_out", kw)
                    def omm(e, kc=kc, w_=w_, t0=t0):
                        for j in range(4):
                            tok = slice(t0 + j * 128, t0 + (j + 1) * 128)
                            src = yssdT[:, kc, tok] if kc < 8 else ys5T[:, kc - 8, tok]
                            for half in range(2):
                                ins = e.matmul(PS[j * 2 + half], lhsT=src, rhs=w_[:, half * 512:(half + 1) * 512],
                                               start=(kc == 0), stop=(kc == 15))
                        return ins
                    op("pe", omm, r=[kw], w=pk)
                evac_sumsq(0, "sA")
                for j in range(4):
                    op("dve", lambda e, j=j: e.scalar_tensor_tensor(out=ms4[j], in0=ms4[j], scalar=ssm[:, j:j + 1], in1=w_mpost,
                                                                    op0=ALU.mult, op1=ALU.mult), r=[("ms", j), "sAr"], w=[("ms", j)])
                    op("dve", lambda e, j=j: e.tensor_tensor(out=xt[:, j, :], in0=ms4[j], in1=xt[:, j, :], op=ALU.add),
                       r=[("ms", j), ("x1", j)], w=[("x1", j)])
                for j in range(4):
                    op("act", lambda e, j=j: e.activation(out=xn2[:, j, :], in_=xt[:, j, :], func=AF.Square, accum_out=ssm[:, 4 + j:5 + j]),
                       r=[("x1", j)], w=[("xn2", j), ("sB", j)])
                rstd4(4, "sB")
                for j in range(4):
                    op("dve", lambda e, j=j: e.tensor_scalar(out=xn2[:, j, :], in0=xt[:, j, :], scalar1=ssm[:, 4 + j:5 + j], scalar2=None,
                                                             op0=ALU.mult), r=[("x1", j), "sBr"], w=[("xn2", j)])
                for j in range(4):
                    tb_ = 2 * j
                    def trh(e, j=j, tb_=tb_):
                        for fc in range(8):
                            ins = e.transpose(out=PSB[tb_][:, fc * 128:(fc + 1) * 128], in_=xn2[:, j, fc * 128:(fc + 1) * 128], identity=ident_b)
                        return ins
                    op("pe", trh, r=[("xn2", j), "ident_b"], w=[pk[tb_]])
                    for fc in range(8):
                        src = PSB[tb_][:, fc * 128:(fc + 1) * 128]
                        dst = h2T[:, fc, j * 128:(j + 1) * 128]
                        if fc % 2 == 0:
                            op("act", lambda e, src=src, dst=dst, fc=fc: e.activation(out=dst, in_=src, func=AF.Copy, scale=wfpre[:, fc:fc + 1]),
                               r=[pk[tb_], "wfpre"], w=[("h2T", j)])
                        else:
                            op("dve", lambda e, src=src, dst=dst, fc=fc: e.tensor_scalar(out=dst, in0=src, scalar1=wfpre[:, fc:fc + 1], scalar2=None,
                                                                                         op0=ALU.mult), r=[pk[tb_], "wfpre"], w=[("h2T", j)])
                for gi in range(NFF // 2):
                    wg_, wu_ = wg[cnt["wg"] % 2], wu[cnt["wg"] % 2]
                    kg = f"wgu{cnt['wg'] % 2}"
                    cnt["wg"] += 1
                    op("pool", lambda e, wg_=wg_, gi=gi: e.dma_start(out=wg_, in_=ws_gate[gi]), r=WK["ws_gate"], w=[kg + "g"], dma=True)
                    op("pool", lambda e, wu_=wu_, gi=gi: e.dma_start(out=wu_, in_=ws_up[gi]), r=WK["ws_up"], w=[kg + "u"], dma=True)
                    for c2 in range(2):
                        c = gi * 2 + c2
                        bg = 4 + 2 * (c % 2)
                        def gmm(e, wg_=wg_, wu_=wu_, c2=c2, bg=bg):
                            for kc in range(8):
                                e.matmul(PS[bg], lhsT=wg_[:, kc, c2 * 128:(c2 + 1) * 128], rhs=h2T[:, kc, :], start=(kc == 0), stop=(kc == 7))
                            for kc in range(8):
                                ins = e.matmul(PS[bg + 1], lhsT=wu_[:, kc, c2 * 128:(c2 + 1) * 128], rhs=h2T[:, kc, :], start=(kc == 0), stop=(kc == 7))
                            return ins
                        op("pe", gmm, r=[kg + "g", kg + "u"] + [("h2T", j) for j in range(4)], w=[pk[bg], pk[bg + 1]])
                        s_ = sgt[c % 2]
                        ksg = f"sgt{c % 2}"
                        op("act", lambda e, s_=s_, bg=bg: e.activation(out=s_, in_=PS[bg], func=AF.Silu), r=[pk[bg]], w=[ksg])
                        op("dve", lambda e, s_=s_, bg=bg, c=c: e.tensor_tensor(out=aT[:, c, :], in0=PS[bg + 1], in1=s_, op=ALU.mult),
                           r=[pk[bg + 1], ksg], w=[("aT", c)])
                for c in range(NFF):
                    w_ = wd[cnt["wd"] % NWR]
                    kw = f"wd{cnt['wd'] % NWR}"
                    cnt["wd"] += 1
                    ld_w(w_, ws_down[c * 128:(c + 1) * 128, :], "ws_down", kw)
                    def dmm(e, c=c, w_=w_):
                        for j in range(4):
                            for half in range(2):
                                ins = e.matmul(PS[j * 2 + half], lhsT=aT[:, c, j * 128:(j + 1) * 128], rhs=w_[:, half * 512:(half + 1) * 512],
                                               start=(c == 0), stop=(c == NFF - 1))
                        return ins
                    op("pe", dmm, r=[kw, ("aT", c)], w=pk)
                evac_sumsq(8, "sC")
                for j in range(4):
                    op("dve", lambda e, j=j: e.scalar_tensor_tensor(out=ms4[j], in0=ms4[j], scalar=ssm[:, 8 + j:9 + j], in1=w_fpost,
                                                                    op0=ALU.mult, op1=ALU.mult), r=[("ms", j), "sCr"], w=[("ms", j)])
                    op("dve", lambda e, j=j: e.tensor_tensor(out=ms4[j], in0=ms4[j], in1=xt[:, j, :], op=ALU.add),
                       r=[("ms", j), ("x1", j)], w=[("ms", j)])
                    r0 = t0 + j * 128
                    op("pool", lambda e, j=j, r0=r0: e.dma_start(out=out[b, r0:r0 + 128, :], in_=ms4[j]), r=[("ms", j)], w=["out"], dma=True)
            SCH.barrier()

        def dump_fm(buf, nfc, keybase):
            SCH.barrier()
            Rd = Reg(28672, REG_SZ - 28672)
            t32 = Rd.alloc([128, S], F32)
            for c in range(nfc):
                op("dve", lambda e, c=c: e.tensor_copy(out=t32, in_=buf[:, c, :]), w=["t32"])
                op("sp", lambda e, c=c: e.dma_start(out=dbg[c * 128:(c + 1) * 128, :], in_=t32), r=["t32"], w=["dbg"], dma=True)

        s5_prep()
        if debug is not None and debug[0] == "s5prep":
            Rd = Reg(0, REG_SZ)
            tb = Rd.alloc([128, 16384], BF16)
            tf = Rd.alloc([128, 16384], F32)
            wcf = wc_dram.rearrange("k a m -> k (a m)")
            pieces = ((t_dram.rearrange("k a m -> k (a m)"), 8192, 0), (wb_dram.rearrange("k a m -> k (a m)"), 16384, 8192),
                      (wcf[:, 0:16384], 16384, 24576), (wcf[:, 16384:32768], 16384, 40960))
            for src, n_, off in pieces:
                op("sp", lambda e, src=src, n_=n_: e.dma_start(out=tb[:, 0:n_], in_=src), w=["tb"], dma=True)
                op("dve", lambda e, n_=n_: e.tensor_copy(out=tf[:, 0:n_], in_=tb[:, 0:n_]), r=["tb"], w=["tf"])
                op("sp", lambda e, n_=n_, off=off: e.dma_start(out=dbg[:, off:off + n_], in_=tf[:, 0:n_]), r=["tf"], w=["dbg"], dma=True)
            SCH.barrier()
            SCH.emit(nc)
            return nc
        cast_w(ws_glu, I["s5_glu_w"], D, "ws_glu")
        cast_w(ws_out, I["w_out"], 2 * D, "ws_out")
        def cast_gu(dst, src, key):
            WK[key] = []
            for kc in range(8):
                k = f"{key}_{kc}"
                WK[key].append(k)
                op("pool", lambda e, kc=kc: e.dma_start(out=dst[:, :, kc, :].rearrange("g p f -> p g f"),
                                                        in_=src[kc * 128:(kc + 1) * 128, :].rearrange("p (g f) -> p g f", f=256)),
                   w=[k], dma=True, persist=True)

        cast_gu(ws_gate, I["w_gate"], "ws_gate")
        cast_gu(ws_up, I["w_up"], "ws_up")
        cast_w(ws_down, I["w_down"], DFF, "ws_down")
        for b in range(NSEQ):
            R_H.reset(); R_A.reset(); R_Y.reset(); R_T.reset()
            hT = R_H.alloc([128, 8, S], BF16)
            xbcT = R_A.alloc([128, 12, S], BF16)
            ybuf = R_Y.alloc([128, NCH, 1024], BF16)
            phase_P1(b, hT)
            dta, E, wt = phase_xbc(hT, xbcT)
            phase_ssd(xbcT, ybuf, dta, E, wt)
            R_A.reset()
            yssdT = R_A.alloc([128, 8, S], BF16)
            wz = R_A.alloc([128, 8, 1024], BF16)
            phase_gate(hT, ybuf, yssdT, wz)
            if debug is not None and debug[0] == "yssd":
                dump_fm(yssdT, 8, "yssdT")
                break
            ys5T = phase_s5(hT)
            if ys5T is None:
                break
            if debug is not None and debug[0] == "ys5":
                dump_fm(ys5T, 8, "ys5T")
                break
            phase_D(b, yssdT, ys5T)

        SCH.barrier()
        SCH.emit(nc)
        print("arena peak cols", A.peak, "ops", len(SCH.ops), {e: SCH.cnt[e] for e in ENGS}, SCH.dcnt)
    return nc


_NC_CACHE = {}


def _prep_inputs(inputs):
    maps = []
    xs = np.ascontiguousarray(inputs["x"], dtype=np.float32)
    shared = {}
    for k, v in inputs.items():
        if k == "x":
            continue
        a = np.ascontiguousarray(np.asarray(v, dtype=np.float32)[0])
        if k in ("ssd_dt_bias", "ssd_a_log"):
            a = a.reshape(32)
        shared[k] = a
    for c in range(NCORES):
        m = dict(shared)
        m["x"] = xs[c * NSEQ:(c + 1) * NSEQ]
        maps.append(m)
    return maps


def kernel(**inputs):
    if "nc" not in _NC_CACHE:
        _NC_CACHE["nc"] = build()
    nc = _NC_CACHE["nc"]
    maps = _prep_inputs(inputs)
    res = run_bass_kernel_spmd(nc, maps, core_ids=list(range(NCORES)))
    return np.concatenate([r["out"] for r in res.results], axis=0).astype(np.float32)
```

```python
import math
from contextlib import ExitStack

import numpy as np
import concourse.bass as bass
import concourse.mybir as mybir
from concourse.bass_utils import run_bass_kernel_spmd

F32 = mybir.dt.float32
BF16 = mybir.dt.bfloat16
AF = mybir.ActivationFunctionType
ALU = mybir.AluOpType

NCORES = 8
S = 2048
D = 1024
NSEQ = 2
NCH = S // 128
DIN = 3616
DFF = 2816
NFF = DFF // 128
C_Z, C_XBC, C_DT, C_U = 0, 1024, 2560, 2592
LC = 8
NJ = S // LC

ENGS = ("pe", "act", "dve", "pool", "sp")


class Sched:
    EPOCH = 3000
    NDSEM = 16

    def __init__(self):
        self.ops = []
        self.last_w = {}
        self.readers = {}
        self.cnt = {e: 0 for e in ENGS}
        self.dcnt = {e: 0 for e in ENGS}
        self.last_c = {e: None for e in ENGS}
        self.last_d = {}
        self.pers = {}

    def op(self, eng, fn, r=(), w=(), dma=False, persist=False):
        idx = len(self.ops)
        deps = set()
        for k in r:
            if k in self.last_w:
                deps.add(self.last_w[k])
            if k in self.pers:
                deps.add(self.pers[k])
        for k in w:
            if k in self.last_w:
                deps.add(self.last_w[k])
            for j in self.readers.get(k, ()):
                deps.add(j)
        o = dict(eng=eng, fn=fn, deps=sorted(deps), dma=dma)
        if dma:
            o["di"] = self.dcnt[eng]
            self.dcnt[eng] += 1
            if not persist:
                self.last_d[(eng, o["di"] % self.NDSEM)] = idx
        else:
            o["ci"] = self.cnt[eng]
            self.cnt[eng] += 1
            self.last_c[eng] = idx
        self.ops.append(o)
        if persist:
            for k in w:
                self.pers[k] = idx
            return idx
        for k in w:
            self.last_w[k] = idx
            self.readers[k] = []
        for k in r:
            self.readers.setdefault(k, []).append(idx)
        return idx

    def barrier(self):
        deps = [v for v in self.last_c.values() if v is not None] + list(self.last_d.values())
        for e in ENGS:
            self.ops.append(dict(eng=e, fn=None, deps=sorted(deps), dma=False))
        self.last_w = {}
        self.readers = {}

    def emit(self, nc):
        with ExitStack() as st:
            csem = {e: [st.enter_context(nc.semaphore(f"c_{e}_{i}"))
                        for i in range(self.cnt[e] // self.EPOCH + 1)] for e in ENGS}
            dsem = {e: [st.enter_context(nc.semaphore(f"d_{e}_{i}")) for i in range(self.NDSEM)]
                    for e in ENGS if self.dcnt[e] > 0}
            block = st.enter_context(nc.Block())
            ops = self.ops

            def run(eng_name, eng):
                waited = {}

                def wait(key, sem, val):
                    if waited.get(key, 0) >= val:
                        return
                    waited[key] = val
                    eng.wait_ge(sem, val)

                for o in ops:
                    if o["eng"] != eng_name:
                        continue
                    for d in o["deps"]:
                        p = ops[d]
                        if p["fn"] is None:
                            continue
                        if p["dma"]:
                            slot = p["di"] % self.NDSEM
                            wait(("d", p["eng"], slot), dsem[p["eng"]][slot], 16 * (p["di"] // self.NDSEM + 1))
                        else:
                            if p["eng"] == eng_name and eng_name in ("pe",):
                                continue
                            ep = p["ci"] // self.EPOCH
                            wait(("c", p["eng"], ep), csem[p["eng"]][ep], p["ci"] % self.EPOCH + 1)
                    if o["fn"] is None:
                        continue
                    if o["dma"]:
                        slot = o["di"] % self.NDSEM
                        rnd = o["di"] // self.NDSEM
                        if rnd > 0:
                            wait(("d", eng_name, slot), dsem[eng_name][slot], 16 * rnd)
                        o["fn"](eng).then_inc(dsem[eng_name][slot], 16)
                    else:
                        ep = o["ci"] // self.EPOCH
                        o["fn"](eng).then_inc(csem[eng_name][ep], 1)

            block.tensor(lambda e: run("pe", e))
            block.scalar(lambda e: run("act", e))
            block.vector(lambda e: run("dve", e))
            block.gpsimd(lambda e: run("pool", e))
            block.sync(lambda e: run("sp", e))


class Arena:
    def __init__(self, handle, ncols):
        self.h = handle
        self.ncols = ncols
        self.top = 0
        self.peak = 0

    def alloc(self, shape, dtype):
        esz = 2 if dtype == BF16 else 4
        n = 1
        for s_ in shape[1:]:
            n *= s_
        nb = (n * esz + 3) // 4
        off = self.top
        self.top += nb
        self.peak = max(self.peak, self.top)
        assert self.top <= self.ncols, f"SBUF arena overflow {self.top} > {self.ncols}"
        v = self.h[:, off:off + nb]
        if dtype == BF16:
            v = v.bitcast(BF16)
        v = v[:, 0:n]
        if len(shape) == 3:
            v = v.rearrange("p (a b) -> p a b", b=shape[2])
        elif len(shape) == 4:
            v = v.rearrange("p (a b c) -> p a b c", b=shape[2], c=shape[3])
        elif len(shape) == 5:
            v = v.rearrange("p (a b c d) -> p a b c d", b=shape[2], c=shape[3], d=shape[4])
        if shape[0] < 128:
            v = v[0:shape[0]]
        return v

    def mark(self):
        return self.top

    def release(self, m):
        self.top = m


def bc(ap, shape):
    return ap.broadcast_to(list(shape))


def build(debug=None):
    nc = bass.Bass("TRN2", target_bir_lowering=False)
    I = {}

    def din(name, shape):
        I[name] = nc.dram_tensor(name, list(shape), F32, kind="ExternalInput").ap()
        return I[name]

    x = din("x", [NSEQ, S, D])
    din("norm_mix_pre", [D]); din("w_in", [D, DIN]); din("conv_w", [5, 1536]); din("conv_b", [1536])
    din("ssd_dt_bias", [32]); din("ssd_a_log", [32]); din("ssd_d", [16]); din("ssd_norm_w", [D])
    din("s5_lambda_re", [2, 64, 64]); din("s5_lambda_im", [2, 64, 64]); din("s5_log_dt", [2, 64])
    din("s5_b_re", [64, 64, 16]); din("s5_b_im", [64, 64, 16]); din("s5_c_re", [64, 16, 64]); din("s5_c_im", [64, 16, 64])
    din("s5_d", [D]); din("s5_glu_w", [D, D]); din("s5_glu_b", [D]); din("w_out", [2 * D, D])
    din("norm_mix_post", [D]); din("norm_ffn_pre", [D]); din("w_gate", [D, DFF]); din("w_up", [D, DFF])
    din("w_down", [DFF, D]); din("norm_ffn_post", [D])
    out = nc.dram_tensor("out", [NSEQ, S, D], F32, kind="ExternalOutput").ap()

    def scratch(name, shape, dt=BF16):
        return nc.dram_tensor(name, list(shape), dt, kind="Internal").ap()

    ws_in = scratch("ws_in", [D, DIN]); ws_glu = scratch("ws_glu", [D, D]); ws_out = scratch("ws_out", [2 * D, D])
    ws_gate = scratch("ws_gate", [NFF // 2, 128, 8, 256]); ws_up = scratch("ws_up", [NFF // 2, 128, 8, 256]); ws_down = scratch("ws_down", [DFF, D])
    wb_dram = scratch("wb_dram", [128, 128, 128])
    wc_dram = scratch("wc_dram", [128, 256, 128])
    t_dram = scratch("t_dram", [128, 64, 128])
    ap_dram = scratch("ap_dram", [128, 2, 64, 64], F32)

    dbg = None
    if debug is not None:
        dbg = nc.dram_tensor("dbg", list(debug[1]), F32, kind="ExternalOutput").ap()

    SCH = Sched()
    op = SCH.op

    with ExitStack() as es:
        ARENA_COLS = 49000
        arena_h = es.enter_context(nc.sbuf_tensor("arena", [128, ARENA_COLS], F32))
        A = Arena(arena_h, ARENA_COLS)
        banks = [es.enter_context(nc.psum_tensor(f"bank{i}", [128, 512], F32)) for i in range(8)]
        PS = [b[:, :] for b in banks]
        PSB = [b[:, :].bitcast(BF16) for b in banks]
        pk = [f"ps{i}" for i in range(8)]

        ident_f = A.alloc([128, 128], F32)
        ident_b = A.alloc([128, 128], BF16)
        ones_f = A.alloc([128, 128], F32)
        U_le = A.alloc([128, 128], F32)
        U_ge = A.alloc([128, 128], F32)
        S_gt = A.alloc([128, 128], F32)
        S_lt = A.alloc([128, 128], F32)
        cst = A.alloc([128, 8], F32)
        wpre = A.alloc([128, 8], F32)
        wfpre = A.alloc([128, 8], F32)
        glub = A.alloc([128, 8], F32)
        convw = A.alloc([128, 5, 12], F32)
        convb = A.alloc([128, 12], F32)
        dtb = A.alloc([128, 2], F32)
        w_mpost = A.alloc([128, D], F32)
        w_fpost = A.alloc([128, D], F32)
        w_ssdn = A.alloc([128, D], F32)
        dskip = A.alloc([128, 16], F32)
        s5A = A.alloc([128, 2, 2, 32], F32)
        s5BN = A.alloc([128, 2, 2, 32], F32)

        def aff(out_ap, in_ap, pattern, cmp, base, cm, r, w):
            op("pool", lambda e: e.affine_select(out=out_ap, in_=in_ap, pattern=pattern, compare_op=cmp,
                                                 fill=0.0, base=base, channel_multiplier=cm), r=r, w=w)

        op("pool", lambda e: e.memset(ones_f, 1.0), w=["ones_f"])
        aff(ident_f, ones_f, [[1, 128]], ALU.is_equal, 0, -1, ["ones_f"], ["ident_f"])
        aff(U_le, ones_f, [[1, 128]], ALU.is_ge, 0, -1, ["ones_f"], ["U_le"])
        aff(U_ge, ones_f, [[-1, 128]], ALU.is_ge, 0, 1, ["ones_f"], ["U_ge"])
        aff(S_gt, ones_f, [[-1, 128]], ALU.is_gt, 0, 1, ["ones_f"], ["S_gt"])
        aff(S_lt, ones_f, [[1, 128]], ALU.is_gt, 0, -1, ["ones_f"], ["S_lt"])
        op("dve", lambda e: e.tensor_copy(out=ident_b, in_=ident_f), r=["ident_f"], w=["ident_b"])
        for i_, v_ in enumerate([1e-6, 1e-5, 1.0, math.pi / 2, 0.0]):
            op("pool", lambda e, i_=i_, v_=v_: e.memset(cst[:, i_:i_ + 1], v_), w=["cst"])
        EPS6, EPS5, ONE, HPI = cst[:, 0:1], cst[:, 1:2], cst[:, 2:3], cst[:, 3:4]

        def ld(dst, src, w, eng="sp", slow=False):
            op(eng, lambda e: e.dma_start(out=dst, in_=src, allow_slow_non_contiguous=slow), w=w, dma=True)

        ld(wpre, I["norm_mix_pre"].rearrange("(c p) -> p c", p=128), ["wpre"], eng="act", slow=True)
        ld(wfpre, I["norm_ffn_pre"].rearrange("(c p) -> p c", p=128), ["wfpre"], eng="act", slow=True)
        ld(glub, I["s5_glu_b"].rearrange("(c p) -> p c", p=128), ["glub"], eng="act", slow=True)
        ld(convb, I["conv_b"].rearrange("(c p) -> p c", p=128), ["convb"], eng="act", slow=True)
        for j_ in range(5):
            ld(convw[:, j_, :], I["conv_w"][j_].rearrange("(c p) -> p c", p=128), ["convw"], eng="act", slow=True)
        ld(dtb[0:32, 0:1], I["ssd_dt_bias"].rearrange("(p o) -> p o", o=1), ["dtb0"])
        ld(dtb[0:32, 1:2], I["ssd_a_log"].rearrange("(p o) -> p o", o=1), ["dtb1"])
        ld(w_mpost, I["norm_mix_post"].partition_broadcast(128), ["w_mpost"])
        ld(w_fpost, I["norm_ffn_post"].partition_broadcast(128), ["w_fpost"])
        ld(w_ssdn, I["ssd_norm_w"].partition_broadcast(128), ["w_ssdn"])
        ld(dskip, I["ssd_d"].partition_broadcast(128), ["dskip"])
        op("act", lambda e: e.activation(out=dtb[0:32, 1:2], in_=dtb[0:32, 1:2], func=AF.Exp), r=["dtb1"], w=["dtb1"])
        op("dve", lambda e: e.tensor_scalar(out=dtb[0:32, 1:2], in0=dtb[0:32, 1:2], scalar1=-1.0, scalar2=None,
                                            op0=ALU.mult), r=["dtb1"], w=["dtb1"])

        WK = {}

        def cast_w(dst, src, rows, key):
            step = 256
            WK[key] = []
            for r0 in range(0, rows, step):
                r1 = min(rows, r0 + step)
                k = f"{key}_{r0}"
                WK[key].append(k)
                op("pool", lambda e, r0=r0, r1=r1: e.dma_start(out=dst[r0:r1, :], in_=src[r0:r1, :]),
                   w=[k], dma=True, persist=True)


        BASE = A.top
        REG_SZ = ARENA_COLS - BASE

        class Reg:
            def __init__(self, off, size):
                self.off, self.size, self.top = off, size, 0

            def alloc(self, shape, dtype):
                save = (A.top,)
                A.top = BASE + self.off + self.top
                v = A.alloc(shape, dtype)
                self.top = A.top - BASE - self.off
                assert self.top <= self.size, f"region overflow {self.top} > {self.size}"
                A.top = save[0]
                return v

            def reset(self):
                self.top = 0

        R_H = Reg(0, 8192)
        R_A = Reg(8192, 12288)
        R_Y = Reg(20480, 8192)
        R_T = Reg(28672, REG_SZ - 28672)
        assert R_T.size >= 15500, R_T.size

        def ld_w(dst, src, rkey, wkey):
            op("sp", lambda e: e.dma_start(out=dst, in_=src), r=WK[rkey], w=[wkey], dma=True)

        def dbg_dump_tok(buf_fn, nrows_tiles, width, keys):
            pass

        def phase_P1(b, hT):
            R_T.reset()
            xt = [R_T.alloc([128, 4, D], F32) for _ in range(2)]
            xn = [R_T.alloc([128, 4, D], BF16) for _ in range(2)]
            junk = R_T.alloc([128, D], F32)
            ss = R_T.alloc([128, 8], F32)
            for i in range(4):
                xt_i, xn_i = xt[i % 2], xn[i % 2]
                kx, kn, ks = f"xt{i % 2}", f"xn{i % 2}", f"ss{i % 2}"
                ssv = ss[:, (i % 2) * 4:(i % 2) * 4 + 4]
                ld(xt_i, x[b, i * 512:(i + 1) * 512, :].rearrange("(j p) f -> p j f", p=128), [kx])
                for j in range(4):
                    op("act", lambda e, j=j, xt_i=xt_i, ssv=ssv: e.activation(
                        out=junk, in_=xt_i[:, j, :], func=AF.Square, accum_out=ssv[:, j:j + 1]),
                       r=[kx], w=["junk", ks + f"_{j}"])
                op("act", lambda e, ssv=ssv: e.activation(out=ssv, in_=ssv, func=AF.Ln, bias=EPS6, scale=1.0 / D),
                   r=[ks + f"_{j}" for j in range(4)] + ["cst"], w=[ks])
                op("act", lambda e, ssv=ssv: e.activation(out=ssv, in_=ssv, func=AF.Exp, scale=-0.5), r=[ks], w=[ks])
                for j in range(4):
                    op("dve", lambda e, j=j, xt_i=xt_i, xn_i=xn_i, ssv=ssv: e.tensor_scalar(
                        out=xn_i[:, j, :], in0=xt_i[:, j, :], scalar1=ssv[:, j:j + 1], scalar2=None, op0=ALU.mult),
                       r=[kx, ks], w=[kn + f"_{j}"])
                for fc in range(8):
                    bk = fc % 2
                    def tr(e, fc=fc, bk=bk, xn_i=xn_i):
                        for j in range(4):
                            ins = e.transpose(out=PSB[bk][:, j * 128:(j + 1) * 128],
                                              in_=xn_i[:, j, fc * 128:(fc + 1) * 128], identity=ident_b)
                        return ins
                    op("pe", tr, r=[kn + f"_{j}" for j in range(4)] + ["ident_b"], w=[pk[bk]])
                    dst = hT[:, fc, i * 512:(i + 1) * 512]
                    if fc % 2 == 0:
                        op("act", lambda e, bk=bk, fc=fc, dst=dst: e.activation(
                            out=dst, in_=PSB[bk][:, 0:512], func=AF.Copy, scale=wpre[:, fc:fc + 1]),
                           r=[pk[bk], "wpre"], w=[("hT", i)])
                    else:
                        op("dve", lambda e, bk=bk, fc=fc, dst=dst: e.tensor_scalar(
                            out=dst, in0=PSB[bk][:, 0:512], scalar1=wpre[:, fc:fc + 1], scalar2=None, op0=ALU.mult),
                           r=[pk[bk], "wpre"], w=[("hT", i)])
            SCH.barrier()

        def inproj(hT, col0, width, evac, wbufs):
            wsr = ws_in.rearrange("(kc p) c -> p kc c", p=128)
            g = 0
            cnt = 0
            for m0 in range(0, width, 512):
                mw = min(512, width - m0)
                wb = wbufs[g % 2]
                wk = f"wring{g % 2}"
                g += 1
                op("sp", lambda e, wb=wb, m0=m0, mw=mw: e.dma_start(out=wb[:, :, 0:mw],
                                                                    in_=wsr[:, :, col0 + m0:col0 + m0 + mw]),
                   r=WK["ws_in"], w=[wk], dma=True)
                for mc in range(0, mw, 128):
                    mcw = min(128, mw - mc)
                    for n in range(4):
                        bk = 2 + (cnt % 4)
                        cnt += 1
                        def mm(e, wb=wb, mc=mc, mcw=mcw, n=n, bk=bk):
                            for kc in range(8):
                                ins = e.matmul(PS[bk][0:mcw, :], lhsT=wb[:, kc, mc:mc + mcw],
                                               rhs=hT[:, kc, n * 512:(n + 1) * 512], start=(kc == 0), stop=(kc == 7))
                            return ins
                        op("pe", mm, r=[wk, ("hT", n)], w=[pk[bk]])
                        evac(col0 + m0 + mc, mcw, n, bk)

        def phase_xbc(hT, xbcT):
            R_T.reset()
            dta = R_T.alloc([128, NCH, 64], F32)
            E = R_T.alloc([128, NCH, 96], F32)
            wt = R_T.alloc([128, NCH, 32], F32)
            keep = R_T.top
            dtraw = R_T.alloc([32, S], F32)
            keep2 = R_T.top
            wbufs = [R_T.alloc([128, 8, 512], BF16) for _ in range(2)]

            def evac_xbc(col, mw, n, bk):
                if col >= C_DT:
                    op("act", lambda e: e.activation(out=dtraw[0:32, n * 512:(n + 1) * 512], in_=PS[bk][0:32, :],
                                                     func=AF.Identity, bias=dtb[0:32, 0:1], scale=1.0),
                       r=[pk[bk], "dtb0"], w=["dtraw"])
                    return
                c = (col - C_XBC) // 128
                dst = xbcT[:, c, n * 512:(n + 1) * 512]
                if (c + n) % 2 == 0:
                    op("act", lambda e: e.activation(out=dst, in_=PS[bk], func=AF.Copy), r=[pk[bk]], w=[("xbcT", c)])
                else:
                    op("dve", lambda e: e.tensor_copy(out=dst, in_=PS[bk]), r=[pk[bk]], w=[("xbcT", c)])

            inproj(hT, C_XBC, 1536 + 32, evac_xbc, wbufs)
            SCH.barrier()
            R_T.top = keep2
            t1 = R_T.alloc([32, S], F32)
            dtT = t1
            aT = R_T.alloc([32, S], F32)
            dg = R_T.alloc([128, 12, 5, 128], BF16)
            for c in range(12):
                for j in range(5):
                    op("act", lambda e, c=c, j=j: e.activation(out=dg[:, c, j, :], in_=ident_f, func=AF.Copy, scale=convw[:, j, c:c + 1]),
                       r=["ident_f", "convw"], w=[("dg", c)])
            for c in range(12):
                b0 = 4 * (c % 2)
                def cmm(e, c=c, b0=b0):
                    for n in range(4):
                        for j in (2, 0, 1, 3, 4):
                            d_ = j - 2
                            lo = max(0, -(n * 512 + d_)) if n == 0 else 0
                            hi = 512 - max(0, (n * 512 + 511 + d_) - (S - 1)) if n == 3 else 512
                            ins = e.matmul(PS[b0 + n][:, lo:hi], lhsT=dg[:, c, j, :], rhs=xbcT[:, c, n * 512 + lo + d_:n * 512 + hi + d_],
                                           start=(j == 2), stop=(j == 4))
                    return ins
                op("pe", cmm, r=[("xbcT", c), ("dg", c)], w=[pk[b0 + n] for n in range(4)])
                for n in range(4):
                    op("act", lambda e, c=c, n=n, b0=b0: e.activation(out=xbcT[:, c, n * 512:(n + 1) * 512], in_=PS[b0 + n], func=AF.Silu,
                                                                      bias=convb[:, c:c + 1], scale=1.0),
                       r=[pk[b0 + n], "convb"], w=[("xbcT", c)])
            op("dve", lambda e: e.scalar_tensor_tensor(out=t1, in0=dtraw, scalar=-1.0, in1=dtraw, op0=ALU.mult, op1=ALU.max),
               r=["dtraw"], w=["t1"])
            op("act", lambda e: e.activation(out=t1, in_=t1, func=AF.Exp, scale=-1.0), r=["t1"], w=["t1"])
            op("act", lambda e: e.activation(out=t1, in_=t1, func=AF.Ln, bias=ONE[0:32], scale=1.0), r=["t1", "cst"], w=["t1"])
            op("dve", lambda e: e.scalar_tensor_tensor(out=dtT, in0=dtraw, scalar=0.0, in1=t1, op0=ALU.max, op1=ALU.add),
               r=["dtraw", "t1"], w=["dtT"])
            op("dve", lambda e: e.tensor_scalar(out=aT, in0=dtT, scalar1=dtb[0:32, 1:2], scalar2=None, op0=ALU.mult),
               r=["dtT", "dtb1"], w=["aT"])
            for half in range(2):
                def trd(e, half=half):
                    for cc in range(8):
                        tok = slice((half * 8 + cc) * 128, (half * 8 + cc + 1) * 128)
                        e.transpose(out=PS[half][:, cc * 64:cc * 64 + 32], in_=dtT[0:32, tok], identity=ident_f[0:32, 0:32])
                        ins = e.transpose(out=PS[half][:, cc * 64 + 32:cc * 64 + 64], in_=aT[0:32, tok], identity=ident_f[0:32, 0:32])
                    return ins
                op("pe", trd, r=["dtT", "aT", "ident_f"], w=[pk[half]])
                op("act", lambda e, half=half: e.activation(out=dta[:, half * 8:(half + 1) * 8, :].rearrange("p c k -> p (c k)"),
                                                            in_=PS[half], func=AF.Copy), r=[pk[half]], w=[("dtah", half)])
            for gb in range(4):
                chunks = list(range(gb * 5, min(NCH, gb * 5 + 5)))
                def cum(e, gb=gb, chunks=chunks):
                    for i_, c in enumerate(chunks):
                        o = i_ * 96
                        e.matmul(PS[2 + gb][:, o:o + 16], lhsT=U_le, rhs=dta[:, c, 32:48], start=True, stop=True)
                        e.matmul(PS[2 + gb][:, o + 16:o + 32], lhsT=U_ge, rhs=dta[:, c, 48:64], start=True, stop=True)
                        e.matmul(PS[2 + gb][:, o + 32:o + 48], lhsT=S_gt, rhs=dta[:, c, 32:48], start=True, stop=True)
                        e.matmul(PS[2 + gb][:, o + 48:o + 64], lhsT=S_lt, rhs=dta[:, c, 48:64], start=True, stop=True)
                        ins = e.matmul(PS[2 + gb][:, o + 64:o + 96], lhsT=ones_f, rhs=dta[:, c, 32:64], start=True, stop=True)
                    return ins
                op("pe", cum, r=[("dtah", 0), ("dtah", 1), "U_le", "U_ge", "S_gt", "S_lt", "ones_f"], w=[pk[2 + gb]])
                n_c = len(chunks)
                op("act", lambda e, gb=gb, n_c=n_c, chunks=chunks: e.activation(
                    out=E[:, chunks[0]:chunks[0] + n_c, :].rearrange("p c k -> p (c k)"), in_=PS[2 + gb][:, 0:n_c * 96], func=AF.Exp),
                   r=[pk[2 + gb]], w=[("Eg", gb)])
            op("dve", lambda e: e.tensor_tensor(out=wt, in0=dta[:, :, 0:32], in1=E[:, :, 32:64], op=ALU.mult),
               r=[("dtah", 0), ("dtah", 1)] + [("Eg", gb) for gb in range(4)], w=["wt_all"])
            SCH.barrier()
            R_T.top = keep
            return dta, E, wt

        def phase_ssd(xbcT, ybuf, dta, E, wt):
            Xtok = [R_T.alloc([128, 1024], BF16) for _ in range(2)]
            Btok = [R_T.alloc([128, 256], BF16) for _ in range(2)]
            Gm = [R_T.alloc([128, 2, 128], BF16) for _ in range(2)]
            Xw = [R_T.alloc([128, 16, 64], BF16) for _ in range(2)]
            Xdt = [R_T.alloc([128, 16, 64], BF16) for _ in range(2)]
            L4 = [R_T.alloc([128, 4, 128], F32) for _ in range(2)]
            D4 = [R_T.alloc([128, 4, 128], BF16) for _ in range(2)]
            M4 = [R_T.alloc([128, 4, 128], BF16) for _ in range(2)]
            ytmpD = [R_T.alloc([128, 16, 64], F32) for _ in range(2)]
            xd = R_T.alloc([128, 16, 64], F32)
            R32D = [R_T.alloc([128, 16, 64], F32) for _ in range(2)]
            RbfD = [R_T.alloc([128, 16, 64], BF16) for _ in range(2)]
            touched = set()
            nq = 0
            for ci in range(NCH):
                for d in range(2):
                    c = ci if d == 0 else NCH - 1 - ci
                    maskL = S_gt if d == 0 else S_lt
                    maskR = U_le if d == 0 else U_ge
                    kmL = "S_gt" if d == 0 else "S_lt"
                    kmR = "U_le" if d == 0 else "U_ge"
                    first, last = ci == 0, ci == NCH - 1
                    tok = slice(c * 128, (c + 1) * 128)
                    s2 = d
                    ytmp, R32, Rbf = ytmpD[d], R32D[d], RbfD[d]
                    X_, B_, G_, Xw_, Xdt_ = Xtok[s2], Btok[s2], Gm[s2], Xw[s2], Xdt[s2]
                    kX, kB, kG, kXw, kXdt = f"Xtok{s2}", f"Btok{s2}", f"Gm{s2}", f"Xw{s2}", f"Xdt{s2}"
                    def trx(e, tok=tok):
                        for fc in range(8):
                            ins = e.transpose(out=PSB[0][:, fc * 128:(fc + 1) * 128], in_=xbcT[:, fc, tok], identity=ident_b)
                        return ins
                    op("pe", trx, r=[("xbcT", fc) for fc in range(8)] + ["ident_b"], w=[pk[0]])
                    op("act", lambda e, X_=X_: e.activation(out=X_, in_=PSB[0][:, 0:1024], func=AF.Copy), r=[pk[0]], w=[kX])
                    def trb(e, tok=tok):
                        for g in range(2):
                            e.transpose(out=PSB[1][:, g * 128:(g + 1) * 128], in_=xbcT[:, 8 + g, tok], identity=ident_b)
                        for g in range(2):
                            ins = e.matmul(PS[1][:, 256 + g * 128:256 + (g + 1) * 128], lhsT=xbcT[:, 8 + g, tok],
                                           rhs=xbcT[:, 10 + g, tok], start=True, stop=True)
                        return ins
                    op("pe", trb, r=[("xbcT", 8), ("xbcT", 9), ("xbcT", 10), ("xbcT", 11), "ident_b"], w=[pk[1]])
                    op("act", lambda e, B_=B_: e.activation(out=B_, in_=PSB[1][:, 0:256], func=AF.Copy), r=[pk[1]], w=[kB])
                    op("dve", lambda e, G_=G_, maskR=maskR: e.tensor_tensor(
                        out=G_, in0=PS[1][:, 256:512].rearrange("p (g l) -> p g l", g=2),
                        in1=bc(maskR.unsqueeze(1), [128, 2, 128]), op=ALU.mult), r=[pk[1], kmR], w=[kG])
                    Xv = X_.rearrange("p (h q) -> p h q", h=16)

                    def do_xw():
                        op("pool", lambda e, Xw_=Xw_, Xv=Xv, c=c, d=d: e.tensor_tensor(
                            out=Xw_, in0=Xv, in1=bc(wt[:, c, d * 16:(d + 1) * 16].unsqueeze(2), [128, 16, 64]), op=ALU.mult),
                           r=[kX, ("wt", c)], w=[kXw])

                    def do_xdt():
                        op("dve", lambda e, Xdt_=Xdt_, Xv=Xv, c=c, d=d: e.tensor_tensor(
                            out=Xdt_, in0=Xv, in1=bc(dta[:, c, d * 16:(d + 1) * 16].unsqueeze(2), [128, 16, 64]), op=ALU.mult),
                           r=[kX, ("dta", c)], w=[kXdt])
                    bufs = []
                    for q in range(4):
                        sq = nq % 2
                        nq += 1
                        bufs.append((L4[sq], D4[sq], M4[sq], f"L4{sq}", f"D4{sq}", f"M4{sq}", 2 + sq))

                    def do_l4(q):
                        L_, D_, M_, kL, kD, kM, bs = bufs[q]
                        a0 = 32 + d * 16 + 4 * q
                        op("pool", lambda e, L_=L_, c=c, a0=a0, maskL=maskL: e.tensor_tensor(
                            out=L_, in0=bc(maskL.unsqueeze(1), [128, 4, 128]),
                            in1=bc(dta[:, c, a0:a0 + 4].unsqueeze(2), [128, 4, 128]), op=ALU.mult),
                           r=[kmL, ("dta", c)], w=[kL])

                    def do_seg(q):
                        L_, D_, M_, kL, kD, kM, bs = bufs[q]
                        g = q // 2
                        def seg(e, L_=L_, bs=bs, maskR=maskR):
                            for i in range(4):
                                ins = e.matmul(PS[bs][:, i * 128:(i + 1) * 128], lhsT=L_[:, i, :], rhs=maskR, start=True, stop=True)
                            return ins
                        op("pe", seg, r=[kL, kmR], w=[pk[bs]])
                        op("act", lambda e, D_=D_, bs=bs: e.activation(
                            out=D_, in_=PS[bs].rearrange("p (i l) -> p i l", i=4), func=AF.Exp), r=[pk[bs]], w=[kD])
                        op("dve", lambda e, D_=D_, M_=M_, G_=G_, g=g: e.tensor_tensor(
                            out=M_, in0=D_, in1=bc(G_[:, g:g + 1, :], [128, 4, 128]), op=ALU.mult), r=[kD, kG], w=[kM])

                    def do_ydiag(q):
                        L_, D_, M_, kL, kD, kM, bs = bufs[q]
                        g = q // 2
                        def ydiag(e, M_=M_, Xdt_=Xdt_, q=q, g=g):
                            for i in range(4):
                                h = 4 * q + i
                                ins = e.matmul(PS[4 + g][:, (h % 8) * 64:(h % 8) * 64 + 64], lhsT=M_[:, i, :],
                                               rhs=Xdt_[:, h, :], start=True, stop=True)
                            return ins
                        op("pe", ydiag, r=[kM, kXdt], w=[pk[4 + g]])
                    do_l4(0)
                    do_xdt()
                    do_l4(1)
                    do_seg(0)
                    do_xw()
                    do_seg(1)
                    do_l4(2)
                    do_ydiag(0)
                    do_seg(2)
                    do_l4(3)
                    do_ydiag(1)
                    do_seg(3)
                    do_ydiag(2)
                    do_ydiag(3)
                    yv = ytmp
                    for g in range(2):
                        hs = slice(8 * g, 8 * g + 8)
                        if not first:
                            op("pe", lambda e, g=g, tok=tok, Rbf=Rbf: e.matmul(
                                PS[6 + g], lhsT=xbcT[:, 10 + g, tok], rhs=Rbf[:, 8 * g:8 * g + 8, :].rearrange("p h q -> p (h q)"),
                                start=True, stop=True), r=[("xbcT", 10 + g), ("Rbf", d, g)], w=[pk[6 + g]])
                            op("dve", lambda e, g=g, hs=hs, c=c, d=d, yv=yv: e.tensor_tensor(
                                out=yv[:, hs, :], in0=PS[6 + g].rearrange("p (h q) -> p h q", h=8),
                                in1=bc(E[:, c, d * 16 + 8 * g:d * 16 + 8 * g + 8].unsqueeze(2), [128, 8, 64]), op=ALU.mult),
                               r=[pk[6 + g], ("E", c)], w=[("ytmp", d, g)])
                            op("dve", lambda e, g=g, hs=hs, yv=yv: e.tensor_tensor(
                                out=yv[:, hs, :], in0=PS[4 + g].rearrange("p (h q) -> p h q", h=8), in1=yv[:, hs, :], op=ALU.add),
                               r=[pk[4 + g], ("ytmp", d, g)], w=[("ytmp", d, g)])
                        else:
                            op("act", lambda e, g=g, hs=hs, yv=yv: e.activation(
                                out=yv[:, hs, :], in_=PS[4 + g].rearrange("p (h q) -> p h q", h=8), func=AF.Copy),
                               r=[pk[4 + g]], w=[("ytmp", d, g)])
                    yb = ybuf[:, c, :].rearrange("p (h q) -> p h q", h=16)
                    ky = [("ytmp", d, 0), ("ytmp", d, 1)]
                    if d == 0:
                        op("dve", lambda e, Xv=Xv: e.tensor_tensor(out=xd, in0=Xv, in1=bc(dskip.unsqueeze(2), [128, 16, 64]),
                                                                  op=ALU.mult), r=[kX, "dskip"], w=["xd"])
                        if c not in touched:
                            op("dve", lambda e, yb=yb, yv=yv: e.tensor_tensor(out=yb, in0=yv, in1=xd, op=ALU.add),
                               r=ky + ["xd"], w=[("ybuf", c)])
                        else:
                            op("dve", lambda e, yv=yv: e.tensor_tensor(out=yv, in0=yv, in1=xd, op=ALU.add), r=ky + ["xd"], w=ky)
                            op("dve", lambda e, yb=yb, yv=yv: e.tensor_tensor(out=yb, in0=yv, in1=yb, op=ALU.add),
                               r=ky + [("ybuf", c)], w=[("ybuf", c)])
                    else:
                        if c not in touched:
                            op("act", lambda e, yb=yb, yv=yv: e.activation(out=yb, in_=yv, func=AF.Copy), r=ky, w=[("ybuf", c)])
                        else:
                            op("dve", lambda e, yb=yb, yv=yv: e.tensor_tensor(out=yb, in0=yv, in1=yb, op=ALU.add),
                               r=ky + [("ybuf", c)], w=[("ybuf", c)])
                    touched.add(c)
                    if not last:
                        for g in range(2):
                            hs = slice(8 * g, 8 * g + 8)
                            op("pe", lambda e, g=g, B_=B_, Xw_=Xw_: e.matmul(
                                PS[6 + g], lhsT=B_[:, g * 128:(g + 1) * 128],
                                rhs=Xw_[:, 8 * g:8 * g + 8, :].rearrange("p h q -> p (h q)"), start=True, stop=True),
                               r=[kB, kXw], w=[pk[6 + g]])
                            if first:
                                op("act", lambda e, g=g, hs=hs, R32=R32: e.activation(
                                    out=R32[:, hs, :], in_=PS[6 + g].rearrange("p (h q) -> p h q", h=8), func=AF.Copy),
                                   r=[pk[6 + g]], w=[("R32", d, g)])
                            else:
                                op("pool", lambda e, g=g, hs=hs, c=c, d=d, R32=R32: e.tensor_tensor(
                                    out=R32[:, hs, :], in0=R32[:, hs, :],
                                    in1=bc(E[:, c, 64 + d * 16 + 8 * g:64 + d * 16 + 8 * g + 8].unsqueeze(2), [128, 8, 64]),
                                    op=ALU.mult), r=[("R32", d, g), ("E", c)], w=[("R32", d, g)])
                                op("dve", lambda e, g=g, hs=hs, R32=R32: e.tensor_tensor(
                                    out=R32[:, hs, :], in0=PS[6 + g].rearrange("p (h q) -> p h q", h=8), in1=R32[:, hs, :],
                                    op=ALU.add), r=[pk[6 + g], ("R32", d, g)], w=[("R32", d, g)])
                            op("act", lambda e, g=g, hs=hs, R32=R32, Rbf=Rbf: e.activation(out=Rbf[:, hs, :], in_=R32[:, hs, :], func=AF.Copy),
                               r=[("R32", d, g)], w=[("Rbf", d, g)])
            SCH.barrier()

        def phase_gate(hT, ybuf, yssdT, wz):
            R_T.reset()
            zs = R_T.alloc([128, 1024], F32)
            yg = R_T.alloc([128, 1024], F32)
            jk = R_T.alloc([128, 512], F32)
            yn = [R_T.alloc([128, 1024], BF16) for _ in range(2)]
            gs = R_T.alloc([128, 4], F32)
            wzr = ws_in.rearrange("(kc p) c -> p kc c", p=128)
            for half in range(2):
                ld_w(wz[:, :, half * 512:(half + 1) * 512], wzr[:, :, C_Z + half * 512:C_Z + (half + 1) * 512], "ws_in", "wz")
            for c in range(NCH):
                tok = slice(c * 128, (c + 1) * 128)
                for half in range(2):
                    def zmm(e, half=half, tok=tok):
                        for kc in range(8):
                            ins = e.matmul(PS[2 + half], lhsT=hT[:, kc, tok], rhs=wz[:, kc, half * 512:(half + 1) * 512],
                                           start=(kc == 0), stop=(kc == 7))
                        return ins
                    op("pe", zmm, r=["hT_all", "wz"], w=[pk[2 + half]])
                    op("act", lambda e, half=half: e.activation(out=zs[:, half * 512:(half + 1) * 512], in_=PS[2 + half],
                                                                func=AF.Silu), r=[pk[2 + half]], w=[("zs", half)])
                    op("dve", lambda e, half=half, c=c: e.tensor_tensor(
                        out=yg[:, half * 512:(half + 1) * 512], in0=ybuf[:, c, half * 512:(half + 1) * 512],
                        in1=zs[:, half * 512:(half + 1) * 512], op=ALU.mult), r=[("zs", half), ("ybuf", c)], w=[("yg", half)])
                    op("act", lambda e, half=half: e.activation(out=jk, in_=yg[:, half * 512:(half + 1) * 512], func=AF.Square,
                                                                accum_out=gs[:, half:half + 1]), r=[("yg", half)], w=["jk", ("gs", half)])
                op("act", lambda e: e.activation(out=gs[:, 0:2], in_=gs[:, 0:2], func=AF.Ln, bias=EPS5, scale=1.0 / 512),
                   r=[("gs", 0), ("gs", 1), "cst"], w=["gsr"])
                op("act", lambda e: e.activation(out=gs[:, 0:2], in_=gs[:, 0:2], func=AF.Exp, scale=-0.5), r=["gsr"], w=["gsr"])
                yn_ = yn[c % 2]
                kyn = f"yn{c % 2}"
                for half in range(2):
                    op("dve", lambda e, half=half, yn_=yn_: e.scalar_tensor_tensor(
                        out=yn_[:, half * 512:(half + 1) * 512], in0=yg[:, half * 512:(half + 1) * 512],
                        scalar=gs[:, half:half + 1], in1=w_ssdn[:, half * 512:(half + 1) * 512], op0=ALU.mult, op1=ALU.mult),
                       r=[("yg", half), "gsr", "w_ssdn"], w=[kyn + f"_{half}"])
                def try_(e, yn_=yn_):
                    for fc in range(8):
                        ins = e.transpose(out=PSB[0][:, fc * 128:(fc + 1) * 128], in_=yn_[:, fc * 128:(fc + 1) * 128], identity=ident_b)
                    return ins
                op("pe", try_, r=[kyn + "_0", kyn + "_1", "ident_b"], w=[pk[0]])
                eng = "act" if c % 2 == 0 else "dve"
                if eng == "act":
                    op("act", lambda e, tok=tok: e.activation(out=yssdT[:, :, tok], in_=PSB[0][:, 0:1024].rearrange("p (f t) -> p f t", f=8),
                                                              func=AF.Copy), r=[pk[0]], w=["yssdT"])
                else:
                    op("dve", lambda e, tok=tok: e.tensor_copy(out=yssdT[:, :, tok], in_=PSB[0][:, 0:1024].rearrange("p (f t) -> p f t", f=8)),
                       r=[pk[0]], w=["yssdT"])
            SCH.barrier()
        def s5_prep():
            R = Reg(0, REG_SZ)
            n_ = [0]

            def T_(shape, dt=F32):
                n_[0] += 1
                return R.alloc(shape, dt), f"s5t{n_[0]}"

            def tt(o, a, b_, opx, eng="dve"):
                op(eng, lambda e: e.tensor_tensor(out=o[0], in0=a[0], in1=b_[0], op=opx), r=[a[1], b_[1]], w=[o[1]])

            def ts(o, a, s1, op0, s2=None, op1=None, eng="dve"):
                if op1 is None:
                    op(eng, lambda e: e.tensor_scalar(out=o[0], in0=a[0], scalar1=s1, scalar2=None, op0=op0), r=[a[1]], w=[o[1]])
                else:
                    op("dve", lambda e: e.tensor_scalar(out=o[0], in0=a[0], scalar1=s1, scalar2=s2, op0=op0, op1=op1), r=[a[1]], w=[o[1]])

            def act(o, a, func, scale=1.0, bias=None):
                if bias is None:
                    op("act", lambda e: e.activation(out=o[0], in_=a[0], func=func, scale=scale), r=[a[1]], w=[o[1]])
                else:
                    op("act", lambda e: e.activation(out=o[0], in_=a[0], func=func, scale=scale, bias=bias), r=[a[1], "cst"], w=[o[1]])

            def V(t, ap):
                return (ap, t[1])

            Bre = T_([128, 32, 16]); Bim = T_([128, 32, 16])
            ld(Bre[0], I["s5_b_re"].rearrange("(q t) p h -> (t p) q h", t=2), [Bre[1]])
            ld(Bim[0], I["s5_b_im"].rearrange("(q t) p h -> (t p) q h", t=2), [Bim[1]])
            cre = T_([128, 32, 16]); cim = T_([128, 32, 16])
            ScR = T_([128, 4, 128]); ScI = T_([128, 4, 128])
            for src, Sc in ((I["s5_c_re"], ScR), (I["s5_c_im"], ScI)):
                for q in range(32):
                    ld(Sc[0][16 * (q % 8):16 * (q % 8) + 16, q // 8, :].rearrange("h (t p) -> h t p", t=2),
                       src[2 * q:2 * q + 2].rearrange("t h p -> h t p"), [Sc[1]])
            stg = T_([128, 128]); lre = T_([128, 64]); lim = T_([128, 64]); ldt_all = T_([128, 128]); ldt = T_([128, 64])
            for src, dst in ((I["s5_lambda_re"], lre), (I["s5_lambda_im"], lim)):
                ld(stg[0][0:64, :], src.rearrange("d (q t) p -> (d q) (t p)", t=2), [stg[1]])
                op("pe", lambda e: e.transpose(out=PS[0][:, 0:64], in_=stg[0][0:64, :], identity=ident_f[0:64, 0:64]),
                   r=[stg[1], "ident_f"], w=[pk[0]])
                op("act", lambda e, dst=dst: e.activation(out=dst[0], in_=PS[0][:, 0:64], func=AF.Copy), r=[pk[0]], w=[dst[1]])
            ld(ldt_all[0], I["s5_log_dt"].rearrange("d g -> (d g)").partition_broadcast(128), [ldt_all[1]])
            for g2 in range(2):
                ps_ = slice(64 * g2, 64 * g2 + 64)
                op("dve", lambda e, ps_=ps_, g2=g2: e.tensor_copy(
                    out=ldt[0][ps_, :].rearrange("p (d q) -> p d q", d=2),
                    in_=ldt_all[0][ps_, :].rearrange("p (d q t) -> p d q t", d=2, t=2)[:, :, :, g2]), r=[ldt_all[1]], w=[ldt[1]])
            dtv = T_([128, 64]); act(dtv, ldt, AF.Exp)
            lr = T_([128, 64]); ts(lr, lre, -1e-4, ALU.min)
            xr = T_([128, 64]); tt(xr, lr, dtv, ALU.mult)
            th = T_([128, 64]); tt(th, lim, dtv, ALU.mult)
            mag = T_([128, 64]); act(mag, xr, AF.Exp)
            sn = T_([128, 64]); cs = T_([128, 64])
            act(sn, th, AF.Sin, scale=1.0 / 16)
            act(cs, th, AF.Sin, scale=1.0 / 16, bias=HPI)
            cc = T_([128, 64]); s2_ = T_([128, 64]); sc = T_([128, 64])
            for _ in range(4):
                tt(cc, cs, cs, ALU.mult); tt(s2_, sn, sn, ALU.mult); tt(sc, sn, cs, ALU.mult)
                tt(cs, cc, s2_, ALU.subtract); ts(sn, sc, 2.0, ALU.mult)
            ar = T_([128, 64]); ai = T_([128, 64])
            tt(ar, mag, cs, ALU.mult); tt(ai, mag, sn, ALU.mult)
            den = T_([128, 64]); t1 = T_([128, 64]); t2 = T_([128, 64])
            tt(den, lr, lr, ALU.mult); tt(t1, lim, lim, ALU.mult); tt(den, den, t1, ALU.add)
            op("dve", lambda e: e.reciprocal(out=den[0], in_=den[0]), r=[den[1]], w=[den[1]])
            nr = T_([128, 64]); ts(nr, ar, -1.0, ALU.add)
            kre = T_([128, 64]); kim = T_([128, 64])
            tt(t1, nr, lr, ALU.mult); tt(t2, ai, lim, ALU.mult); tt(t1, t1, t2, ALU.add); tt(kre, t1, den, ALU.mult)
            tt(t1, ai, lr, ALU.mult); tt(t2, nr, lim, ALU.mult); tt(t1, t1, t2, ALU.subtract); tt(kim, t1, den, ALU.mult)
            Pre = T_([128, LC + 1, 64]); Pim = T_([128, LC + 1, 64])
            op("dve", lambda e: e.memset(Pre[0][:, 0, :], 1.0), w=[Pre[1]])
            op("dve", lambda e: e.memset(Pim[0][:, 0, :], 0.0), w=[Pim[1]])
            for k in range(1, LC + 1):
                a_, b_ = V(Pre, Pre[0][:, k - 1, :]), V(Pim, Pim[0][:, k - 1, :])
                tt(t1, a_, ar, ALU.mult); tt(t2, b_, ai, ALU.mult); tt(V(Pre, Pre[0][:, k, :]), t1, t2, ALU.subtract)
                tt(t1, a_, ai, ALU.mult); tt(t2, b_, ar, ALU.mult); tt(V(Pim, Pim[0][:, k, :]), t1, t2, ALU.add)
            p8r, p8i = V(Pre, Pre[0][:, LC, :]), V(Pim, Pim[0][:, LC, :])
            for ri, src in ((0, p8r), (1, p8i)):
                op("dve", lambda e, ri=ri, src=src: e.tensor_copy(out=s5A[:, ri, :, :].rearrange("p d q -> p (d q)"), in_=src[0]),
                   r=[src[1]], w=["s5A"])
            m2 = T_([128, 64]); ivr = T_([128, 64]); ivi = T_([128, 64])
            tt(m2, p8r, p8r, ALU.mult); tt(t1, p8i, p8i, ALU.mult); tt(m2, m2, t1, ALU.add)
            op("dve", lambda e: e.reciprocal(out=m2[0], in_=m2[0]), r=[m2[1]], w=[m2[1]])
            tt(ivr, p8r, m2, ALU.mult); tt(ivi, p8i, m2, ALU.mult); ts(ivi, ivi, -1.0, ALU.mult)
            for Sc, dst in ((ScR, cre), (ScI, cim)):
                for blk in range(4):
                    op("pe", lambda e, blk=blk, Sc=Sc: e.transpose(out=PS[1][:, 0:128], in_=Sc[0][:, blk, :], identity=ident_f),
                       r=[Sc[1], "ident_f"], w=[pk[1]])
                    op("act", lambda e, blk=blk, dst=dst: e.activation(
                        out=dst[0][:, blk * 8:(blk + 1) * 8, :], in_=PS[1][:, 0:128].rearrange("p (q h) -> p q h", q=8), func=AF.Copy),
                       r=[pk[1]], w=[dst[1]])
            dG = T_([64, 16]); dcol = T_([128, 64])
            ld(dG[0], I["s5_d"].rearrange("(g h) -> g h", h=16), [dG[1]])
            dGb = T_([64, LC, 16])
            op("dve", lambda e: e.tensor_copy(out=dGb[0], in_=bc(dG[0].unsqueeze(1), [64, LC, 16])), r=[dG[1]], w=[dGb[1]])
            op("pe", lambda e: e.matmul(PS[2][:, 0:64], lhsT=dGb[0].rearrange("p s h -> p (s h)"), rhs=ident_f[0:64, 0:64],
                                        start=True, stop=True), r=[dGb[1], "ident_f"], w=[pk[2]])
            op("act", lambda e: e.activation(out=dcol[0], in_=PS[2][:, 0:64], func=AF.Copy), r=[pk[2]], w=[dcol[1]])
            mF = T_([128, LC, 16]); mB = T_([128, LC, 16])
            op("pool", lambda e: e.affine_select(out=mF[0], in_=bc(ones_f[:, 0:1].unsqueeze(2), [128, LC, 16]), pattern=[[16, LC], [0, 16]],
                                                 compare_op=ALU.is_ge, fill=0.0, base=15, channel_multiplier=-1), r=["ones_f"], w=[mF[1]])
            op("pool", lambda e: e.affine_select(out=mB[0], in_=bc(ones_f[:, 0:1].unsqueeze(2), [128, LC, 16]), pattern=[[-16, LC], [0, 16]],
                                                 compare_op=ALU.is_ge, fill=0.0, base=0, channel_multiplier=1), r=["ones_f"], w=[mB[1]])
            cast_w(ws_in, I["w_in"], D, "ws_in")
            bbr = T_([128, 2, 32, 16]); bbi = T_([128, 2, 32, 16]); u1 = T_([128, 2, 32, 16]); u2 = T_([128, 2, 32, 16])

            def kb(t):
                return V(t, bc(t[0].rearrange("p (d q) -> p d q", d=2).unsqueeze(3), [128, 2, 32, 16]))

            def bb_(t):
                return V(t, bc(t[0].unsqueeze(1), [128, 2, 32, 16]))
            tt(u1, bb_(Bre), kb(kre), ALU.mult); tt(u2, bb_(Bim), kb(kim), ALU.mult); tt(bbr, u1, u2, ALU.subtract)
            tt(u1, bb_(Bim), kb(kre), ALU.mult); tt(u2, bb_(Bre), kb(kim), ALU.mult); tt(bbi, u1, u2, ALU.add)
            mark_small = R.top
            Tst = T_([128, 64, 128], BF16)
            HQ = 8
            WbT = [T_([128, HQ, LC, 16]), T_([128, HQ, LC, 16])]
            Qm = [T_([128, HQ, LC, 16]), T_([128, HQ, LC, 16])]
            Wc = [T_([128, HQ, LC, 16]), T_([128, HQ, LC, 16])]
            WcZ = [[T_([128, HQ, LC, 16]), T_([128, HQ, LC, 16])], [T_([128, HQ, LC, 16]), T_([128, HQ, LC, 16])]]
            hm = T_([128, 2])
            for g2_ in range(2):
                for hh in range(2):
                    op("dve", lambda e, g2_=g2_, hh=hh: e.memset(hm[0][64 * hh:64 * hh + 64, g2_:g2_ + 1], 1.0 if g2_ == hh else 0.0), w=[hm[1]])
            v1 = T_([128, HQ, LC, 16]); v2 = T_([128, HQ, LC, 16]); v3 = T_([128, HQ, LC, 16]); v4 = T_([128, HQ, LC, 16])
            w1 = T_([128, HQ, LC, 16])
            stb = T_([128, HQ, 2, 128], BF16)
            stc = T_([128, HQ, 4, 128], BF16)
            wbv = wb_dram.rearrange("k (q x) m -> k q x m", x=4)
            wcv = wc_dram.rearrange("k (q x) m -> k q x m", x=8)

            def iv(t, d, hq):
                return V(t, bc(t[0][:, d * 32 + hq * HQ:d * 32 + (hq + 1) * HQ].unsqueeze(2).unsqueeze(3), [128, HQ, LC, 16]))
            for d in range(2):
                for hq in range(32 // HQ):
                    qs = slice(hq * HQ, (hq + 1) * HQ)

                    bR, bI = V(bbr, bbr[0][:, d, qs]), V(bbi, bbi[0][:, d, qs])
                    cR, cI = V(cre, cre[0][:, qs]), V(cim, cim[0][:, qs])
                    cols = slice(d * 32 + hq * HQ, d * 32 + (hq + 1) * HQ)

                    def pwv(P, lo, rev):
                        v = P[0][:, lo:lo + LC, cols]
                        if rev:
                            v = v[:, ::-1, :]
                        return V(P, bc(v.rearrange("p s q -> p q s").unsqueeze(3), [128, HQ, LC, 16]))

                    def b4(t):
                        return V(t, bc(t[0].unsqueeze(2), [128, HQ, LC, 16]))
                    Pe_r, Pe_i = pwv(Pre, 0, d == 0), pwv(Pim, 0, d == 0)
                    Pf_r, Pf_i = pwv(Pre, 1, d == 1), pwv(Pim, 1, d == 1)
                    tt(v1, b4(bR), Pe_r, ALU.mult); tt(v2, b4(bI), Pe_i, ALU.mult); tt(WbT[0], v1, v2, ALU.subtract)
                    tt(v1, b4(bI), Pe_r, ALU.mult); tt(v2, b4(bR), Pe_i, ALU.mult); tt(WbT[1], v1, v2, ALU.add)
                    tt(v3, b4(cR), Pf_r, ALU.mult); tt(v4, b4(cI), Pf_i, ALU.mult); tt(Wc[0], v3, v4, ALU.subtract)
                    tt(v3, b4(cI), Pf_r, ALU.mult); tt(v4, b4(cR), Pf_i, ALU.mult); tt(v3, v3, v4, ALU.add)
                    act(Wc[1], v3, AF.Copy, scale=-1.0)
                    tt(Qm[0], WbT[0], iv(ivr, d, hq), ALU.mult); tt(w1, WbT[1], iv(ivi, d, hq), ALU.mult); tt(Qm[0], Qm[0], w1, ALU.subtract)
                    tt(Qm[1], WbT[0], iv(ivi, d, hq), ALU.mult); tt(w1, WbT[1], iv(ivr, d, hq), ALU.mult); tt(Qm[1], Qm[1], w1, ALU.add)
                    for g2_ in range(2):
                        for ri in range(2):
                            op("act", lambda e, g2_=g2_, ri=ri: e.activation(out=WcZ[g2_][ri][0], in_=Wc[ri][0], func=AF.Copy,
                                                                             scale=hm[0][:, g2_:g2_ + 1]), r=[Wc[ri][1], hm[1]], w=[WcZ[g2_][ri][1]])
                    msk = mF if d == 0 else mB
                    for g4 in range(HQ // 2):
                        bk = 4 + g4 % 2
                        def tmm(e, g4=g4, bk=bk):
                            for i in range(4):
                                gl = 4 * g4 + i
                                ql, g2 = gl // 2, gl % 2
                                e.matmul(PS[bk][:, i * 128:(i + 1) * 128], lhsT=Qm[0][0][:, ql].rearrange("p s h -> p (s h)"),
                                         rhs=WcZ[g2][0][0][:, ql].rearrange("p s h -> p (s h)"), start=True, stop=False)
                                ins = e.matmul(PS[bk][:, i * 128:(i + 1) * 128], lhsT=Qm[1][0][:, ql].rearrange("p s h -> p (s h)"),
                                               rhs=WcZ[g2][1][0][:, ql].rearrange("p s h -> p (s h)"), start=False, stop=True)
                            return ins
                        op("pe", tmm, r=[Qm[0][1], Qm[1][1], WcZ[0][0][1], WcZ[0][1][1], WcZ[1][0][1], WcZ[1][1][1]], w=[pk[bk]])
                        g0 = hq * 2 * HQ + 4 * g4
                        tv = Tst[0][:, g0:g0 + 4, :]
                        mv = bc(msk[0].rearrange("p t h -> p (t h)").unsqueeze(1), [128, 4, 128])
                        pv = PS[bk].rearrange("p (i m) -> p i m", i=4)
                        if d == 0:
                            op("dve", lambda e, tv=tv, mv=mv, pv=pv: e.tensor_tensor(out=tv, in0=pv, in1=mv, op=ALU.mult),
                               r=[pk[bk], msk[1]], w=[Tst[1]])
                        else:
                            wv = w1[0].rearrange("p q s h -> p (q s h)")[:, 0:512].rearrange("p (i m) -> p i m", i=4)
                            op("dve", lambda e, wv=wv, mv=mv, pv=pv: e.tensor_tensor(out=wv, in0=pv, in1=mv, op=ALU.mult),
                               r=[pk[bk], msk[1]], w=[w1[1]])
                            iv_ = bc(ident_f.unsqueeze(1), [128, 4, 128])
                            dv_ = bc(dcol[0][:, g0:g0 + 4].unsqueeze(2), [128, 4, 128])
                            wv2 = w1[0].rearrange("p q s h -> p (q s h)")[:, 512:1024].rearrange("p (i m) -> p i m", i=4)
                            op("dve", lambda e, wv2=wv2, iv_=iv_, dv_=dv_: e.tensor_tensor(out=wv2, in0=iv_, in1=dv_, op=ALU.mult),
                               r=["ident_f", dcol[1], w1[1]], w=[w1[1]])
                            op("dve", lambda e, wv=wv, wv2=wv2: e.tensor_tensor(out=wv, in0=wv, in1=wv2, op=ALU.add),
                               r=[w1[1]], w=[w1[1]])
                            op("dve", lambda e, tv=tv, wv=wv: e.tensor_tensor(out=tv, in0=tv, in1=wv, op=ALU.add),
                               r=[w1[1], Tst[1]], w=[Tst[1]])
                    for ql in range(HQ):
                        for ri in range(2):
                            bk = 6 + (ql * 2 + ri) % 2
                            op("pe", lambda e, ql=ql, ri=ri, bk=bk: e.transpose(
                                out=PS[bk][:, 0:128], in_=WbT[ri][0][:, ql].rearrange("p s h -> p (s h)"), identity=ident_f),
                               r=[WbT[ri][1], "ident_f"], w=[pk[bk]])
                            op("act", lambda e, ql=ql, ri=ri, bk=bk: e.activation(out=stb[0][:, ql, ri, :], in_=PS[bk][:, 0:128], func=AF.Copy),
                               r=[pk[bk]], w=[stb[1]])
                    for g2_ in range(2):
                        for ri in range(2):
                            op("pool", lambda e, ri=ri, g2_=g2_: e.tensor_copy(out=stc[0][:, :, g2_ * 2 + ri, :],
                                                                               in_=WcZ[g2_][ri][0].rearrange("p q s h -> p q (s h)")),
                               r=[WcZ[g2_][ri][1]], w=[stc[1]])
                    op("sp", lambda e, d=d, qs=qs: e.dma_start(out=wbv[:, qs, 2 * d:2 * d + 2, :], in_=stb[0]), r=[stb[1]], w=["wb_dram"], dma=True)
                    op("sp", lambda e, d=d, qs=qs: e.dma_start(out=wcv[:, qs, 4 * d:4 * d + 4, :], in_=stc[0]), r=[stc[1]], w=["wc_dram"], dma=True)
            op("sp", lambda e: e.dma_start(out=t_dram, in_=Tst[0]), r=[Tst[1]], w=["t_dram"], dma=True)
            SCH.barrier()
            R.top = mark_small
            NSEG = 64
            ApT = T_([128, 2, 64, NSEG]); au1 = T_([128, 64, NSEG // 2]); au2 = T_([128, 64, NSEG // 2])
            for ri, src in ((0, p8r), (1, p8i)):
                op("dve", lambda e, ri=ri, src=src: e.tensor_copy(out=ApT[0][:, ri, :, 0], in_=src[0]), r=[src[1]], w=[ApT[1]])
            nn = 1
            while nn < NSEG:
                lo_r, lo_i = V(ApT, ApT[0][:, 0, :, 0:nn]), V(ApT, ApT[0][:, 1, :, 0:nn])
                br = V(ApT, bc(ApT[0][:, 0, :, nn - 1:nn], [128, 64, nn])); bi = V(ApT, bc(ApT[0][:, 1, :, nn - 1:nn], [128, 64, nn]))
                a1 = V(au1, au1[0][:, :, 0:nn]); a2 = V(au2, au2[0][:, :, 0:nn])
                tt(a1, lo_r, br, ALU.mult); tt(a2, lo_i, bi, ALU.mult); tt(V(ApT, ApT[0][:, 0, :, nn:2 * nn]), a1, a2, ALU.subtract)
                tt(a1, lo_r, bi, ALU.mult); tt(a2, lo_i, br, ALU.mult); tt(V(ApT, ApT[0][:, 1, :, nn:2 * nn]), a1, a2, ALU.add)
                nn *= 2
            op("sp", lambda e: e.dma_start(out=ap_dram, in_=ApT[0]), r=[ApT[1]], w=["ap_dram"], dma=True)
            op("dve", lambda e: e.tensor_scalar(out=s5BN[:, 0, :, :], in0=s5A[:, 1, :, :], scalar1=-1.0, scalar2=None, op0=ALU.mult),
               r=["s5A"], w=["s5BN"])
            op("dve", lambda e: e.tensor_copy(out=s5BN[:, 1, :, :], in_=s5A[:, 1, :, :]), r=["s5A"], w=["s5BN"])
            SCH.barrier()

        def phase_s5(hT):
            R_S5H = Reg(16384, 8192)
            R_S5U = Reg(24576, 8192)
            R_S5S = Reg(32768, 4096)
            R_S5R = Reg(36864, REG_SZ - 36864)
            U8 = R_S5U.alloc([128, 64, NJ], BF16)
            Sel = R_S5S.alloc([128, 8, 8, 128], BF16)
            def gen_sel():
                op("pool", lambda e: e.memset(Sel, 1.0), w=["Sel"])
                for pat, cmp, base, cm in (([[16, 8], [-16, 8], [1, 128]], ALU.is_equal, 0, -1),
                                           ([[-16, 8], [0, 8], [0, 128]], ALU.is_ge, 0, 1),
                                           ([[16, 8], [0, 8], [0, 128]], ALU.is_ge, 15, -1)):
                    op("pool", lambda e, pat=pat, cmp=cmp, base=base, cm=cm: e.affine_select(
                        out=Sel, in_=Sel, pattern=pat, compare_op=cmp, fill=0.0, base=base, channel_multiplier=cm), r=["Sel"], w=["Sel"])
            gen_sel()
            wbufs = [R_S5H.alloc([128, 8, 512], BF16) for _ in range(2)]
            utmp = [R_S5H.alloc([128, S], BF16) for _ in range(2)]
            state = {"n": 0}

            def evac_u(col, mw, n, bk):
                fc = (col - C_U) // 128
                ut = utmp[fc % 2]
                ku = f"utmp{fc % 2}"
                dst = ut.rearrange("p (s j) -> p s j", s=LC)[:, :, n * 64:(n + 1) * 64]
                src = PS[bk].rearrange("p (j s) -> p s j", s=LC)
                if n % 2 == 0:
                    op("act", lambda e: e.activation(out=dst, in_=src, func=AF.Copy), r=[pk[bk]], w=[(ku, n)])
                else:
                    op("dve", lambda e: e.tensor_copy(out=dst, in_=src), r=[pk[bk]], w=[(ku, n)])
                if n == 3:
                    for gl in range(8):
                        g = fc * 8 + gl
                        bq = gl % 2
                        def shf(e, gl=gl, ut=ut, bq=bq):
                            uv = ut.rearrange("p (s j) -> p s j", s=LC)
                            for s_ in range(LC):
                                ins = e.matmul(PS[bq][:, 0:NJ], lhsT=Sel[:, gl, s_, :], rhs=uv[:, s_, :],
                                               start=(s_ == 0), stop=(s_ == LC - 1))
                            return ins
                        op("pe", shf, r=[(ku, 0), (ku, 1), (ku, 2), (ku, 3), "Sel"], w=[pk[bq]])
                        if gl % 2 == 0:
                            op("act", lambda e, g=g, bq=bq: e.activation(out=U8[:, g, :], in_=PS[bq][:, 0:NJ], func=AF.Copy),
                               r=[pk[bq]], w=[("U8", g)])
                        else:
                            op("dve", lambda e, g=g, bq=bq: e.tensor_copy(out=U8[:, g, :], in_=PS[bq][:, 0:NJ]),
                               r=[pk[bq]], w=[("U8", g)])

            inproj(hT, C_U, 1024, evac_u, wbufs)
            if debug is not None and debug[0] == "u8":
                SCH.barrier()
                Rd_ = Reg(0, 16384)
                t32 = Rd_.alloc([128, 64 * NJ], F32)
                op("dve", lambda e: e.tensor_copy(out=t32, in_=U8.rearrange("p g j -> p (g j)")), w=["t32"])
                op("sp", lambda e: e.dma_start(out=dbg[:, :], in_=t32), r=["t32"], w=["dbg"], dma=True)
                return None
            SCH.barrier()
            R_S5H.reset(); R_H.reset()
            HistD = [R_H.alloc([128, 2, 32, NJ], BF16), R_S5H.alloc([128, 2, 32, NJ], BF16)]
            ring_w = [R_S5R.alloc([128, 4, 128], BF16) for _ in range(4)]
            for q in range(32):
                rw = ring_w[q % 4]
                kw = f"ringw{q % 4}"
                op("sp", lambda e, rw=rw, q=q: e.dma_start(out=rw, in_=wb_dram[:, 4 * q:4 * q + 4, :]), w=[kw], dma=True)
                for d in range(2):
                    bk = 2 + (2 * q + d) % 4
                    def inj(e, rw=rw, q=q, d=d, bk=bk):
                        for ri in range(2):
                            x_ = d * 2 + ri
                            e.matmul(PS[bk][0:64, ri * NJ:(ri + 1) * NJ], lhsT=rw[:, x_, 0:64], rhs=U8[:, 2 * q, :], start=True, stop=True)
                            ins = e.matmul(PS[bk][64:128, ri * NJ:(ri + 1) * NJ], lhsT=rw[:, x_, 64:128], rhs=U8[:, 2 * q + 1, :],
                                           start=True, stop=True, tile_position=(0, 64))
                        return ins
                    op("pe", inj, r=[kw, ("U8", 2 * q), ("U8", 2 * q + 1)], w=[pk[bk]])
                    src = PS[bk].rearrange("p (r j) -> p r j", r=2)
                    if d == 0:
                        op("act", lambda e, q=q, src=src: e.activation(out=HistD[0][:, :, q, :], in_=src, func=AF.Copy),
                           r=[pk[bk]], w=["Hist_f"])
                    else:
                        op("dve", lambda e, q=q, src=src: e.tensor_copy(out=HistD[1][:, :, q, :], in_=src), r=[pk[bk]], w=["Hist_b"])
            SCH.barrier()
            NS, SL = 4, NJ // 4
            R_SC = Reg(32768, REG_SZ - 32768)
            ZD = [R_SC.alloc([128, 2, 32, NS], F32) for _ in range(2)]
            T1 = [R_SC.alloc([128, 2, 32, NS], F32) for _ in range(2)]
            T2 = [R_SC.alloc([128, 2, 32, NS], F32) for _ in range(2)]
            FcD = [R_SC.alloc([128, 2, 32, NS], F32) for _ in range(2)]
            A64 = R_SC.alloc([128, 2, 2, 32], F32)
            c1 = R_SC.alloc([128, 2, 32], F32); c2 = R_SC.alloc([128, 2, 32], F32)
            ApR = R_SC.alloc([128, 32, SL], F32); ApI = R_SC.alloc([128, 32, SL], F32)
            tA = R_SC.alloc([128, 32, SL], F32); tB = R_SC.alloc([128, 32, SL], F32)
            for d, eng in ((0, "dve"), (1, "pool")):
                Z, t1, t2 = ZD[d], T1[d], T2[d]
                kz = f"Z{d}"
                Hseg = HistD[d].rearrange("p r q (s j) -> p r q s j", s=NS)
                Ac = bc(s5A[:, 0:1, d, :].unsqueeze(3), [128, 2, 32, NS])
                Bc = bc(s5BN[:, :, d, :].unsqueeze(3), [128, 2, 32, NS])
                kH = "Hist_f" if d == 0 else "Hist_b"
                order = list(range(SL)) if d == 0 else list(range(SL - 1, -1, -1))
                for si, k in enumerate(order):
                    hj = Hseg[:, :, :, :, k]
                    if si == 0:
                        op(eng, lambda e, Z=Z, hj=hj: e.tensor_copy(out=Z, in_=hj), r=[kH], w=[kz])
                        continue
                    Zs = Z[:, ::-1, :, :]
                    op(eng, lambda e, Z=Z, t1=t1, Ac=Ac: e.tensor_tensor(out=t1, in0=Z, in1=Ac, op=ALU.mult), r=[kz, "s5A"], w=[kz + "t1"])
                    op(eng, lambda e, Zs=Zs, t2=t2, Bc=Bc: e.tensor_tensor(out=t2, in0=Zs, in1=Bc, op=ALU.mult), r=[kz, "s5BN"], w=[kz + "t2"])
                    op(eng, lambda e, t1=t1, t2=t2: e.tensor_tensor(out=t1, in0=t1, in1=t2, op=ALU.add), r=[kz + "t1", kz + "t2"], w=[kz + "t1"])
                    op(eng, lambda e, Z=Z, t1=t1, hj=hj: e.tensor_tensor(out=Z, in0=t1, in1=hj, op=ALU.add), r=[kz + "t1", kH], w=[kz])
                    op(eng, lambda e, Z=Z, hj=hj: e.tensor_copy(out=hj, in_=Z), r=[kz], w=[kH])
            for d in range(2):
                Z, Fc = ZD[d], FcD[d]
                kz, kF = f"Z{d}", f"Fc{d}"
                kH = "Hist_f" if d == 0 else "Hist_b"
                Hseg = HistD[d].rearrange("p r q (s j) -> p r q s j", s=NS)
                for ri in range(2):
                    op("sp", lambda e, ri=ri, d=d: e.dma_start(out=(ApR if ri == 0 else ApI), in_=ap_dram[:, ri, d * 32:(d + 1) * 32, :]),
                       w=["ApR" if ri == 0 else "ApI"], dma=True)
                op("dve", lambda e: e.tensor_copy(out=A64[:, 0, :, :], in_=bc(ApR[:, :, SL - 1].unsqueeze(1), [128, 2, 32])), r=["ApR"], w=["A64"])
                op("dve", lambda e: e.tensor_scalar(out=A64[:, 1, 0, :], in0=ApI[:, :, SL - 1], scalar1=-1.0, scalar2=None, op0=ALU.mult),
                   r=["ApI", "A64"], w=["A64"])
                op("dve", lambda e: e.tensor_copy(out=A64[:, 1, 1, :], in_=ApI[:, :, SL - 1]), r=["ApI", "A64"], w=["A64"])
                segs = list(range(NS)) if d == 0 else list(range(NS - 1, -1, -1))
                for i_, sg_ in enumerate(segs):
                    if i_ == 0:
                        op("dve", lambda e, sg_=sg_, Z=Z, Fc=Fc: e.tensor_copy(out=Fc[:, :, :, sg_], in_=Z[:, :, :, sg_]), r=[kz], w=[kF])
                    elif i_ < NS - 1:
                        pv = segs[i_ - 1]
                        Fp = Fc[:, :, :, pv]
                        Fps = Fc[:, ::-1, :, pv]
                        op("dve", lambda e, Fp=Fp: e.tensor_tensor(out=c1, in0=Fp, in1=A64[:, 0, :, :], op=ALU.mult), r=[kF, "A64"], w=["c1"])
                        op("dve", lambda e, Fps=Fps: e.tensor_tensor(out=c2, in0=Fps, in1=A64[:, 1, :, :], op=ALU.mult), r=[kF, "A64"], w=["c2"])
                        op("dve", lambda e: e.tensor_tensor(out=c1, in0=c1, in1=c2, op=ALU.add), r=["c1", "c2"], w=["c1"])
                        op("dve", lambda e, sg_=sg_, Z=Z, Fc=Fc: e.tensor_tensor(out=Fc[:, :, :, sg_], in0=c1, in1=Z[:, :, :, sg_], op=ALU.add),
                           r=["c1", kz, kF], w=[kF])
                for i_, sg_ in enumerate(segs):
                    if i_ == 0:
                        continue
                    pv = segs[i_ - 1]
                    cR = bc(Fc[:, 0, :, pv].unsqueeze(2), [128, 32, SL])
                    cI = bc(Fc[:, 1, :, pv].unsqueeze(2), [128, 32, SL])
                    PR = ApR if d == 0 else ApR[:, :, ::-1]
                    PI = ApI if d == 0 else ApI[:, :, ::-1]
                    Hre, Him = Hseg[:, 0, :, sg_, :], Hseg[:, 1, :, sg_, :]
                    op("dve", lambda e, PR=PR, cR=cR: e.tensor_tensor(out=tA, in0=PR, in1=cR, op=ALU.mult), r=["ApR", kF], w=["tA"])
                    op("dve", lambda e, PI=PI, cI=cI: e.tensor_tensor(out=tB, in0=PI, in1=cI, op=ALU.mult), r=["ApI", kF], w=["tB"])
                    op("dve", lambda e: e.tensor_tensor(out=tA, in0=tA, in1=tB, op=ALU.subtract), r=["tA", "tB"], w=["tA"])
                    op("dve", lambda e, Hre=Hre: e.tensor_tensor(out=Hre, in0=Hre, in1=tA, op=ALU.add), r=["tA", kH], w=[kH])
                    op("dve", lambda e, PR=PR, cI=cI: e.tensor_tensor(out=tA, in0=PR, in1=cI, op=ALU.mult), r=["ApR", kF, "tA"], w=["tA"])
                    op("dve", lambda e, PI=PI, cR=cR: e.tensor_tensor(out=tB, in0=PI, in1=cR, op=ALU.mult), r=["ApI", kF, "tB"], w=["tB"])
                    op("dve", lambda e: e.tensor_tensor(out=tA, in0=tA, in1=tB, op=ALU.add), r=["tA", "tB"], w=["tA"])
                    op("dve", lambda e, Him=Him: e.tensor_tensor(out=Him, in0=Him, in1=tA, op=ALU.add), r=["tA", kH], w=[kH])
            SCH.barrier()
            R_S5R.reset()
            ring_c = [R_S5R.alloc([128, 8, 128], BF16) for _ in range(4)]
            ring_t = [R_S5R.alloc([128, 2, 128], BF16) for _ in range(4)]
            gen_sel()
            for q in range(32):
                rc, rt = ring_c[q % 4], ring_t[q % 4]
                kc_, kt_ = f"ringc{q % 4}", f"ringt{q % 4}"
                op("sp", lambda e, rc=rc, q=q: e.dma_start(out=rc, in_=wc_dram[:, 8 * q:8 * q + 8, :]), w=[kc_], dma=True)
                op("sp", lambda e, rt=rt, q=q: e.dma_start(out=rt, in_=t_dram[:, 2 * q:2 * q + 2, :]), w=[kt_], dma=True)
                for g2 in range(2):
                    g = 2 * q + g2
                    bk = 2 + g % 4
                    ps_ = slice(64 * g2, 64 * g2 + 64)
                    def outm(e, rc=rc, rt=rt, q=q, g2=g2, g=g, bk=bk, ps_=ps_):
                        e.matmul(PS[bk][:, 0:NJ], lhsT=rt[:, g2, :], rhs=U8[:, g, :], start=True, stop=False)
                        for d in range(2):
                            for ri in range(2):
                                last = (d == 1 and ri == 1)
                                if d == 0:
                                    ins = e.matmul(PS[bk][:, 1:NJ], lhsT=rc[:, g2 * 2 + ri, :], rhs=HistD[0][:, ri, q, 0:NJ - 1],
                                                   start=False, stop=last)
                                else:
                                    ins = e.matmul(PS[bk][:, 0:NJ - 1], lhsT=rc[:, 4 + g2 * 2 + ri, :], rhs=HistD[1][:, ri, q, 1:NJ],
                                                   start=False, stop=last)
                        return ins
                    op("pe", outm, r=[kc_, kt_, ("U8", g), "Hist"], w=[pk[bk]])
                    if g % 2 == 0:
                        op("act", lambda e, g=g, bk=bk: e.activation(out=U8[:, g, :], in_=PS[bk][:, 0:NJ], func=AF.Copy),
                           r=[pk[bk]], w=[("U8", g)])
                    else:
                        op("dve", lambda e, g=g, bk=bk: e.tensor_copy(out=U8[:, g, :], in_=PS[bk][:, 0:NJ]), r=[pk[bk]], w=[("U8", g)])
            SCH.barrier()
            R_S5H.reset(); R_S5R.reset(); R_H.reset()
            gT = R_S5H.alloc([128, 8, S], BF16)
            glw = R_S5R.alloc([128, 8, D], BF16)
            tmp = [R_S5R.alloc([128, NJ], F32) for _ in range(2)]
            sg = [R_S5R.alloc([128, 512], F32) for _ in range(2)]
            glr = ws_glu.rearrange("(kc p) c -> p kc c", p=128)
            for half in range(2):
                ld_w(glw[:, :, half * 512:(half + 1) * 512], glr[:, :, half * 512:(half + 1) * 512], "ws_glu", "glw")
            it = 0
            for fc in range(8):
                gv = gT[:, fc, :].rearrange("p (j t) -> p t j", t=LC)
                for t_ in range(LC):
                    bk = 2 + it % 4
                    tm = tmp[it % 2]
                    ktm = f"tmp{it % 2}"
                    it += 1
                    def unsh(e, fc=fc, t_=t_, bk=bk):
                        for gl in range(8):
                            ins = e.matmul(PS[bk][:, 0:NJ], lhsT=Sel[:, t_, gl, :], rhs=U8[:, fc * 8 + gl, :], start=(gl == 0), stop=(gl == 7))
                        return ins
                    op("pe", unsh, r=["Sel"] + [("U8", fc * 8 + gl) for gl in range(8)], w=[pk[bk]])
                    yv = PS[bk][:, 0:NJ]
                    op("act", lambda e, tm=tm, yv=yv: e.activation(out=tm, in_=yv, func=AF.Square), r=[pk[bk]], w=[ktm])
                    op("dve", lambda e, tm=tm: e.tensor_scalar(out=tm, in0=tm, scalar1=0.044715, scalar2=1.0, op0=ALU.mult, op1=ALU.add),
                       r=[ktm], w=[ktm])
                    op("dve", lambda e, tm=tm, yv=yv: e.tensor_tensor(out=tm, in0=yv, in1=tm, op=ALU.mult), r=[ktm, pk[bk]], w=[ktm])
                    op("act", lambda e, tm=tm: e.activation(out=tm, in_=tm, func=AF.Sigmoid, scale=1.5957691216057308), r=[ktm], w=[ktm])
                    op("dve", lambda e, tm=tm, yv=yv, gv=gv, t_=t_: e.tensor_tensor(out=gv[:, t_, :], in0=yv, in1=tm, op=ALU.mult),
                       r=[ktm, pk[bk]], w=[("gT", fc)])
            SCH.barrier()
            ys5T = R_H.alloc([128, 8, S], BF16)
            cnt = 0
            for mc in range(8):
                for n in range(4):
                    bk = 2 + cnt % 4
                    sg_ = sg[cnt % 2]
                    ksg = f"sg{cnt % 2}"
                    cnt += 1
                    tl = slice(n * 512, (n + 1) * 512)
                    def glm(e, mc=mc, tl=tl, bk=bk):
                        for kc in range(8):
                            ins = e.matmul(PS[bk], lhsT=glw[:, kc, mc * 128:(mc + 1) * 128], rhs=gT[:, kc, tl], start=(kc == 0), stop=(kc == 7))
                        return ins
                    op("pe", glm, r=["glw"] + [("gT", fc) for fc in range(8)], w=[pk[bk]])
                    op("act", lambda e, mc=mc, bk=bk, sg_=sg_: e.activation(out=sg_, in_=PS[bk], func=AF.Sigmoid, bias=glub[:, mc:mc + 1], scale=1.0),
                       r=[pk[bk], "glub"], w=[ksg])
                    op("dve", lambda e, mc=mc, tl=tl, sg_=sg_: e.tensor_tensor(out=ys5T[:, mc, tl], in0=sg_, in1=gT[:, mc, tl], op=ALU.mult),
                       r=[ksg, ("gT", mc)], w=["ys5T"])
            SCH.barrier()
            return ys5T

        def phase_D(b, yssdT, ys5T):
            Rg = Reg(16384, REG_SZ - 16384)
            xt = Rg.alloc([128, 4, D], F32)
            ms4 = [Rg.alloc([128, D], F32) for _ in range(4)]
            xn2 = Rg.alloc([128, 4, D], BF16)
            h2T = Rg.alloc([128, 8, 512], BF16)
            aT = Rg.alloc([128, NFF, 512], BF16)
            sgt = [Rg.alloc([128, 512], F32) for _ in range(2)]
            NWR = 5
            wo = [Rg.alloc([128, D], BF16) for _ in range(NWR)]
            wg = [Rg.alloc([128, 8, 256], BF16) for _ in range(2)]
            wu = [Rg.alloc([128, 8, 256], BF16) for _ in range(2)]
            wd = [Rg.alloc([128, D], BF16) for _ in range(NWR)]
            ssm = Rg.alloc([128, 16], F32)
            cnt = {"wo": 0, "wg": 0, "wd": 0, "jk": 0}

            def jkey():
                cnt["jk"] += 1
                return f"jk{cnt['jk']}"

            def rstd4(c0, tag):
                sv = ssm[:, c0:c0 + 4]
                op("act", lambda e, sv=sv: e.activation(out=sv, in_=sv, func=AF.Ln, bias=EPS6, scale=1.0 / D),
                   r=[(tag, j) for j in range(4)] + ["cst"], w=[tag + "r"])
                op("act", lambda e, sv=sv: e.activation(out=sv, in_=sv, func=AF.Exp, scale=-0.5), r=[tag + "r"], w=[tag + "r"])

            def evac_sumsq(c0, tag):
                for j in range(4):
                    for half in range(2):
                        op("act", lambda e, j=j, half=half: e.activation(out=ms4[j][:, half * 512:(half + 1) * 512], in_=PS[2 * j + half],
                                                                         func=AF.Copy), r=[pk[2 * j + half]], w=[("ms", j)])
                for j in range(4):
                    op("act", lambda e, j=j: e.activation(out=xn2[:, j, :], in_=ms4[j], func=AF.Square, accum_out=ssm[:, c0 + j:c0 + j + 1]),
                       r=[("ms", j)], w=[("xn2", j), (tag, j)])
                rstd4(c0, tag)

            for i in range(4):
                t0 = i * 512
                ld(xt, x[b, t0:t0 + 512, :].rearrange("(j p) f -> p j f", p=128), [("x1", j) for j in range(4)], eng="pool")
                for kc in range(16):
                    w_ = wo[cnt["wo"] % NWR]
                    kw = f"wo{cnt['wo'] % NWR}"
                    cnt["wo"] += 1
                    ld_w(w_, ws_out[kc * 128:(kc + 1) * 128, :], "ws_out", kw)
                    def omm(e, kc=kc, w_=w_, t0=t0):
                        for j in range(4):
                            tok = slice(t0 + j * 128, t0 + (j + 1) * 128)
                            src = yssdT[:, kc, tok] if kc < 8 else ys5T[:, kc - 8, tok]
                            for half in range(2):
                                ins = e.matmul(PS[j * 2 + half], lhsT=src, rhs=w_[:, half * 512:(half + 1) * 512],
                                               start=(kc == 0), stop=(kc == 15))
                        return ins
                    op("pe", omm, r=[kw], w=pk)
                evac_sumsq(0, "sA")
                for j in range(4):
                    op("dve", lambda e, j=j: e.scalar_tensor_tensor(out=ms4[j], in0=ms4[j], scalar=ssm[:, j:j + 1], in1=w_mpost,
                                                                    op0=ALU.mult, op1=ALU.mult), r=[("ms", j), "sAr"], w=[("ms", j)])
                    op("dve", lambda e, j=j: e.tensor_tensor(out=xt[:, j, :], in0=ms4[j], in1=xt[:, j, :], op=ALU.add),
                       r=[("ms", j), ("x1", j)], w=[("x1", j)])
                for j in range(4):
                    op("act", lambda e, j=j: e.activation(out=xn2[:, j, :], in_=xt[:, j, :], func=AF.Square, accum_out=ssm[:, 4 + j:5 + j]),
                       r=[("x1", j)], w=[("xn2", j), ("sB", j)])
                rstd4(4, "sB")
                for j in range(4):
                    op("dve", lambda e, j=j: e.tensor_scalar(out=xn2[:, j, :], in0=xt[:, j, :], scalar1=ssm[:, 4 + j:5 + j], scalar2=None,
                                                             op0=ALU.mult), r=[("x1", j), "sBr"], w=[("xn2", j)])
                for j in range(4):
                    tb_ = 2 * j
                    def trh(e, j=j, tb_=tb_):
                        for fc in range(8):
                            ins = e.transpose(out=PSB[tb_][:, fc * 128:(fc + 1) * 128], in_=xn2[:, j, fc * 128:(fc + 1) * 128], identity=ident_b)
                        return ins
                    op("pe", trh, r=[("xn2", j), "ident_b"], w=[pk[tb_]])
                    for fc in range(8):
                        src = PSB[tb_][:, fc * 128:(fc + 1) * 128]
                        dst = h2T[:, fc, j * 128:(j + 1) * 128]
                        if fc % 2 == 0:
                            op("act", lambda e, src=src, dst=dst, fc=fc: e.activation(out=dst, in_=src, func=AF.Copy, scale=wfpre[:, fc:fc + 1]),
                               r=[pk[tb_], "wfpre"], w=[("h2T", j)])
                        else:
                            op("dve", lambda e, src=src, dst=dst, fc=fc: e.tensor_scalar(out=dst, in0=src, scalar1=wfpre[:, fc:fc + 1], scalar2=None,
                                                                                         op0=ALU.mult), r=[pk[tb_], "wfpre"], w=[("h2T", j)])
                for gi in range(NFF // 2):
                    wg_, wu_ = wg[cnt["wg"] % 2], wu[cnt["wg"] % 2]
                    kg = f"wgu{cnt['wg'] % 2}"
                    cnt["wg"] += 1
                    op("pool", lambda e, wg_=wg_, gi=gi: e.dma_start(out=wg_, in_=ws_gate[gi]), r=WK["ws_gate"], w=[kg + "g"], dma=True)
                    op("pool", lambda e, wu_=wu_, gi=gi: e.dma_start(out=wu_, in_=ws_up[gi]), r=WK["ws_up"], w=[kg + "u"], dma=True)
                    for c2 in range(2):
                        c = gi * 2 + c2
                        bg = 4 + 2 * (c % 2)
                        def gmm(e, wg_=wg_, wu_=wu_, c2=c2, bg=bg):
                            for kc in range(8):
                                e.matmul(PS[bg], lhsT=wg_[:, kc, c2 * 128:(c2 + 1) * 128], rhs=h2T[:, kc, :], start=(kc == 0), stop=(kc == 7))
                            for kc in range(8):
                                ins = e.matmul(PS[bg + 1], lhsT=wu_[:, kc, c2 * 128:(c2 + 1) * 128], rhs=h2T[:, kc, :], start=(kc == 0), stop=(kc == 7))
                            return ins
                        op("pe", gmm, r=[kg + "g", kg + "u"] + [("h2T", j) for j in range(4)], w=[pk[bg], pk[bg + 1]])
                        s_ = sgt[c % 2]
                        ksg = f"sgt{c % 2}"
                        op("act", lambda e, s_=s_, bg=bg: e.activation(out=s_, in_=PS[bg], func=AF.Silu), r=[pk[bg]], w=[ksg])
                        op("dve", lambda e, s_=s_, bg=bg, c=c: e.tensor_tensor(out=aT[:, c, :], in0=PS[bg + 1], in1=s_, op=ALU.mult),
                           r=[pk[bg + 1], ksg], w=[("aT", c)])
                for c in range(NFF):
                    w_ = wd[cnt["wd"] % NWR]
                    kw = f"wd{cnt['wd'] % NWR}"
                    cnt["wd"] += 1
                    ld_w(w_, ws_down[c * 128:(c + 1) * 128, :], "ws_down", kw)
                    def dmm(e, c=c, w_=w_):
                        for j in range(4):
                            for half in range(2):
                                ins = e.matmul(PS[j * 2 + half], lhsT=aT[:, c, j * 128:(j + 1) * 128], rhs=w_[:, half * 512:(half + 1) * 512],
                                               start=(c == 0), stop=(c == NFF - 1))
                        return ins
                    op("pe", dmm, r=[kw, ("aT", c)], w=pk)
                evac_sumsq(8, "sC")
                for j in range(4):
                    op("dve", lambda e, j=j: e.scalar_tensor_tensor(out=ms4[j], in0=ms4[j], scalar=ssm[:, 8 + j:9 + j], in1=w_fpost,
                                                                    op0=ALU.mult, op1=ALU.mult), r=[("ms", j), "sCr"], w=[("ms", j)])
                    op("dve", lambda e, j=j: e.tensor_tensor(out=ms4[j], in0=ms4[j], in1=xt[:, j, :], op=ALU.add),
                       r=[("ms", j), ("x1", j)], w=[("ms", j)])
                    r0 = t0 + j * 128
                    op("pool", lambda e, j=j, r0=r0: e.dma_start(out=out[b, r0:r0 + 128, :], in_=ms4[j]), r=[("ms", j)], w=["out"], dma=True)
            SCH.barrier()

        def dump_fm(buf, nfc, keybase):
            SCH.barrier()
            Rd = Reg(28672, REG_SZ - 28672)
            t32 = Rd.alloc([128, S], F32)
            for c in range(nfc):
                op("dve", lambda e, c=c: e.tensor_copy(out=t32, in_=buf[:, c, :]), w=["t32"])
                op("sp", lambda e, c=c: e.dma_start(out=dbg[c * 128:(c + 1) * 128, :], in_=t32), r=["t32"], w=["dbg"], dma=True)

        s5_prep()
        if debug is not None and debug[0] == "s5prep":
            Rd = Reg(0, REG_SZ)
            tb = Rd.alloc([128, 16384], BF16)
            tf = Rd.alloc([128, 16384], F32)
            wcf = wc_dram.rearrange("k a m -> k (a m)")
            pieces = ((t_dram.rearrange("k a m -> k (a m)"), 8192, 0), (wb_dram.rearrange("k a m -> k (a m)"), 16384, 8192),
                      (wcf[:, 0:16384], 16384, 24576), (wcf[:, 16384:32768], 16384, 40960))
            for src, n_, off in pieces:
                op("sp", lambda e, src=src, n_=n_: e.dma_start(out=tb[:, 0:n_], in_=src), w=["tb"], dma=True)
                op("dve", lambda e, n_=n_: e.tensor_copy(out=tf[:, 0:n_], in_=tb[:, 0:n_]), r=["tb"], w=["tf"])
                op("sp", lambda e, n_=n_, off=off: e.dma_start(out=dbg[:, off:off + n_], in_=tf[:, 0:n_]), r=["tf"], w=["dbg"], dma=True)
            SCH.barrier()
            SCH.emit(nc)
            return nc
        cast_w(ws_glu, I["s5_glu_w"], D, "ws_glu")
        cast_w(ws_out, I["w_out"], 2 * D, "ws_out")
        def cast_gu(dst, src, key):
            WK[key] = []
            for kc in range(8):
                k = f"{key}_{kc}"
                WK[key].append(k)
                op("pool", lambda e, kc=kc: e.dma_start(out=dst[:, :, kc, :].rearrange("g p f -> p g f"),
                                                        in_=src[kc * 128:(kc + 1) * 128, :].rearrange("p (g f) -> p g f", f=256)),
                   w=[k], dma=True, persist=True)

        cast_gu(ws_gate, I["w_gate"], "ws_gate")
        cast_gu(ws_up, I["w_up"], "ws_up")
        cast_w(ws_down, I["w_down"], DFF, "ws_down")
        for b in range(NSEQ):
            R_H.reset(); R_A.reset(); R_Y.reset(); R_T.reset()
            hT = R_H.alloc([128, 8, S], BF16)
            xbcT = R_A.alloc([128, 12, S], BF16)
            ybuf = R_Y.alloc([128, NCH, 1024], BF16)
            phase_P1(b, hT)
            dta, E, wt = phase_xbc(hT, xbcT)
            phase_ssd(xbcT, ybuf, dta, E, wt)
            R_A.reset()
            yssdT = R_A.alloc([128, 8, S], BF16)
            wz = R_A.alloc([128, 8, 1024], BF16)
            phase_gate(hT, ybuf, yssdT, wz)
            if debug is not None and debug[0] == "yssd":
                dump_fm(yssdT, 8, "yssdT")
                break
            ys5T = phase_s5(hT)
            if ys5T is None:
                break
            if debug is not None and debug[0] == "ys5":
                dump_fm(ys5T, 8, "ys5T")
                break
            phase_D(b, yssdT, ys5T)

        SCH.barrier()
        SCH.emit(nc)
        print("arena peak cols", A.peak, "ops", len(SCH.ops), {e: SCH.cnt[e] for e in ENGS}, SCH.dcnt)
    return nc


_NC_CACHE = {}


def _prep_inputs(inputs):
    maps = []
    xs = np.ascontiguousarray(inputs["x"], dtype=np.float32)
    shared = {}
    for k, v in inputs.items():
        if k == "x":
            continue
        a = np.ascontiguousarray(np.asarray(v, dtype=np.float32)[0])
        if k in ("ssd_dt_bias", "ssd_a_log"):
            a = a.reshape(32)
        shared[k] = a
    for c in range(NCORES):
        m = dict(shared)
        m["x"] = xs[c * NSEQ:(c + 1) * NSEQ]
        maps.append(m)
    return maps


def kernel(**inputs):
    if "nc" not in _NC_CACHE:
        _NC_CACHE["nc"] = build()
    nc = _NC_CACHE["nc"]
    maps = _prep_inputs(inputs)
    res = run_bass_kernel_spmd(nc, maps, core_ids=list(range(NCORES)))
    return np.concatenate([r["out"] for r in res.results], axis=0).astype(np.float32)
```
